# Optimizing a Trainium2 kernel written in Bass

```python
import jax
import jax.numpy as jnp
from jax import lax
import numpy as np

D_MODEL = 1024
BATCH = 4
SEQ = 8192
DEPTH = 1
DEC_BATCH = 16
DEC_SEQ = 16
PAST_LEN = 4096

CHUNK = 64
D_PLE = 256
R_HEADS = 8
R_HEAD_DIM = 64
D_R = R_HEADS * R_HEAD_DIM
LORA_W = 64
LORA_A = 64
LORA_G = 128
R_PROJ = 3 * D_R + LORA_W + LORA_A + LORA_G
G_HEADS = 4
G_HEAD_DIM = 128
D_G = G_HEADS * G_HEAD_DIM
CONV_W = 4
G_QKV = 3 * D_G
D_MIX = D_R + D_G
OFF_QKV = R_PROJ
OFF_Z = OFF_QKV + G_QKV
OFF_B = OFF_Z + D_G
OFF_A = OFF_B + G_HEADS
D_IN = OFF_A + G_HEADS
D_FF = -(-(8 * D_MODEL) // (3 * 256)) * 256
RMS_EPS = 1e-6
GN_EPS = 64e-5

kernel_name = 'hybrid_rwkv7_gdn_stream_step'


def _rmsnorm(x, g):
    x32 = x.astype(jnp.float32)
    y = x32 * lax.rsqrt(jnp.mean(x32 * x32, axis=-1, keepdims=True) + RMS_EPS)
    return (y * g.astype(jnp.float32)).astype(x.dtype)


def _l2norm(x):
    return x * lax.rsqrt(jnp.sum(x * x, axis=-1, keepdims=True) + 1e-6)


def _rwkv7_scan(r, w, k, v, kk, a, S0):
    def step(S, xs):
        r_t, w_t, k_t, v_t, kk_t, a_t = xs
        sa = jnp.einsum('bhvk,bhk->bhv', S, kk_t)
        S = (S * w_t[:, :, None, :]
             - jnp.einsum('bhv,bhk->bhvk', sa, kk_t * a_t)
             + jnp.einsum('bhv,bhk->bhvk', v_t, k_t))
        y = jnp.einsum('bhvk,bhk->bhv', S, r_t)
        return S, y
    xs = tuple(jnp.moveaxis(t, 1, 0) for t in (r, w, k, v, kk, a))
    S, y = lax.scan(step, S0, xs)
    return jnp.moveaxis(y, 0, 1), S


def _rwkv7_group(f, shift_prev, S0, mu, w0, w_up, a0, a_up, g_up, k_k, k_a, r_k, lnx_w, lnx_b):
    B, L, _ = f.shape
    f32 = jnp.float32
    f_prev = jnp.concatenate([shift_prev[:, None, :].astype(f.dtype), f[:, :-1]], axis=1)
    fm = f + (f_prev - f) * mu
    r, k, v, wl, al, gl = jnp.split(
        fm, [D_R, 2 * D_R, 3 * D_R, 3 * D_R + LORA_W, 3 * D_R + LORA_W + LORA_A], axis=-1)
    w_log = -jax.nn.softplus(-(w0 + jnp.tanh(wl) @ w_up).astype(f32)) - 0.5
    decay = jnp.exp(-jnp.exp(w_log))
    a = jax.nn.sigmoid((a0 + al @ a_up).astype(f32))
    g = (jax.nn.sigmoid(gl) @ g_up).astype(f32)
    hd = lambda t: t.reshape(B, L, R_HEADS, R_HEAD_DIM)
    k32 = k.astype(f32)
    kk = hd(k32 * k_k)
    kk = kk * lax.rsqrt(jnp.sum(kk * kk, axis=-1, keepdims=True) + 1e-12)
    k_eff = hd(k32 * (1.0 + (a - 1.0) * k_a))
    rh = hd(r.astype(f32))
    vh = hd(v.astype(f32))
    y, S = _rwkv7_scan(rh, hd(decay), k_eff, vh, kk, hd(a), S0.astype(f32))
    mean = jnp.mean(y, axis=-1, keepdims=True)
    var = jnp.mean(jnp.square(y - mean), axis=-1, keepdims=True)
    yn = ((y - mean) * lax.rsqrt(var + GN_EPS)).reshape(B, L, D_R) * lnx_w + lnx_b
    bonus = (jnp.sum(rh * k_eff * r_k, axis=-1, keepdims=True) * vh).reshape(B, L, D_R)
    out = ((yn + bonus) * g).astype(f.dtype)
    return out, f[:, -1], S


def _gated_delta_chunked(q, k, v, glog, beta, S0):
    B, L, H, Dk = q.shape
    Dv = v.shape[-1]
    C = min(CHUNK, L)
    N = L // C
    blk = lambda t: jnp.moveaxis(t.reshape((B, N, C) + t.shape[2:]), 3, 1)
    q, k, v, glog, beta = blk(q), blk(k), blk(v), blk(glog), blk(beta)
    G = jnp.cumsum(glog, axis=-1)
    idx = jnp.arange(C)
    incl = idx[:, None] >= idx[None, :]
    strict = idx[:, None] > idx[None, :]
    dG = G[..., :, None] - G[..., None, :]
    dec_incl = jnp.where(incl, jnp.exp(jnp.where(incl, dG, 0.0)), 0.0)
    dec_strict = jnp.where(strict, dec_incl, 0.0)
    kb = k * beta[..., None]
    A = jnp.einsum('bhnid,bhnjd->bhnij', kb, k) * dec_strict
    eye = jnp.eye(C, dtype=jnp.float32)
    T = lax.linalg.triangular_solve(eye + A, jnp.broadcast_to(eye, A.shape),
                                    left_side=True, lower=True)
    U = jnp.einsum('bhnij,bhnje->bhnie', T, v * beta[..., None])
    Wk = jnp.einsum('bhnij,bhnjd->bhnid', T, kb * jnp.exp(G)[..., None])
    qk = jnp.einsum('bhnid,bhnjd->bhnij', q, k) * dec_incl
    qg = q * jnp.exp(G)[..., None]
    G_last = G[..., -1]
    kd = k * jnp.exp(G_last[..., None] - G)[..., None]

    def step(S, xs):
        U_n, W_n, qg_n, qk_n, kd_n, gl_n = xs
        v_new = U_n - jnp.einsum('bhcd,bhde->bhce', W_n, S)
        o = jnp.einsum('bhcd,bhde->bhce', qg_n, S) + jnp.einsum('bhij,bhje->bhie', qk_n, v_new)
        S = S * jnp.exp(gl_n)[..., None, None] + jnp.einsum('bhcd,bhce->bhde', kd_n, v_new)
        return S, o

    xs = tuple(jnp.moveaxis(t, 2, 0) for t in (U, Wk, qg, qk, kd, G_last))
    S, o = lax.scan(step, S0, xs)
    o = jnp.transpose(o, (1, 0, 3, 2, 4)).reshape(B, L, H, Dv)
    return o, S


def _gdn_group(qkv, z, b, a_raw, conv_prev, S0, conv_w, a_log, dt_bias, norm_g):
    B, L, _ = qkv.shape
    f32 = jnp.float32
    full = jnp.concatenate([conv_prev.astype(qkv.dtype), qkv], axis=1)
    conv = full[:, 0:L] * conv_w[0]
    for j in range(1, CONV_W):
        conv = conv + full[:, j:j + L] * conv_w[j]
    conv = jax.nn.silu(conv.astype(f32))
    q, k, v = jnp.split(conv, [D_G, 2 * D_G], axis=-1)
    hd = lambda t: t.reshape(B, L, G_HEADS, G_HEAD_DIM)
    q = _l2norm(hd(q)) * (G_HEAD_DIM ** -0.5)
    k = _l2norm(hd(k))
    v = hd(v)
    beta = jax.nn.sigmoid(b.astype(f32))
    glog = -jnp.exp(a_log.astype(f32)) * jax.nn.softplus(a_raw.astype(f32) + dt_bias)
    o, S = _gated_delta_chunked(q, k, v, glog, beta, S0.astype(f32))
    o = (o * lax.rsqrt(jnp.mean(o * o, axis=-1, keepdims=True) + RMS_EPS) * norm_g
         * jax.nn.silu(hd(z.astype(f32))))
    return o.reshape(B, L, D_G).astype(qkv.dtype), full[:, L:], S


def _layer(x, p, shift0, wkv0, conv0, gdn0, ln_mix_g, w_in, mu_shift, w0, w_lora_up, a0,
           a_lora_up, g_lora_up, k_k, k_a, r_k, ln_x_w, ln_x_b, conv_w, a_log, dt_bias,
           gdn_norm_g, w_out, ln_ffn_g, w_gate, w_up, w_down, ln_ple_g, w_ple_gate, w_ple_proj):
    h = _rmsnorm(x, ln_mix_g)
    proj = h @ w_in
    r_out, shift1, wkv1 = _rwkv7_group(proj[..., :R_PROJ], shift0, wkv0, mu_shift, w0, w_lora_up,
                                       a0, a_lora_up, g_lora_up, k_k, k_a, r_k, ln_x_w, ln_x_b)
    g_out, conv1, gdn1 = _gdn_group(proj[..., OFF_QKV:OFF_Z], proj[..., OFF_Z:OFF_B],
                                    proj[..., OFF_B:OFF_A], proj[..., OFF_A:], conv0, gdn0,
                                    conv_w, a_log, dt_bias, gdn_norm_g)
    x = x + jnp.concatenate([r_out, g_out], axis=-1) @ w_out
    h = _rmsnorm(x, ln_ffn_g)
    x = x + (jax.nn.silu(h @ w_gate) * (h @ w_up)) @ w_down
    h = _rmsnorm(x, ln_ple_g)
    x = x + jax.nn.sigmoid(h @ w_ple_gate) * (p @ w_ple_proj)
    return (x, shift1.astype(shift0.dtype), wkv1.astype(wkv0.dtype),
            conv1.astype(conv0.dtype), gdn1.astype(gdn0.dtype))


def setup_inputs(seed: int = 0) -> dict:
    key = jax.random.key(seed)
    ks = iter(jax.random.split(key, 48))
    f32 = jnp.float32
    nrm = lambda shape, s: jax.random.normal(next(ks), shape, f32) * s
    uni = lambda shape, lo, hi: jax.random.uniform(next(ks), shape, f32, lo, hi)
    dt = jnp.exp(uni((DEPTH, G_HEADS), float(np.log(1e-3)), float(np.log(1e-1))))
    return {
        'x_prompt': nrm((BATCH, SEQ, D_MODEL), 1.0),
        'x_sample': nrm((DEC_BATCH, DEC_SEQ, D_MODEL), 1.0),
        'state_shift': nrm((DEPTH, DEC_BATCH, R_PROJ), 1.0),
        'state_wkv': nrm((DEPTH, DEC_BATCH, R_HEADS, R_HEAD_DIM, R_HEAD_DIM), 0.1),
        'state_conv': nrm((DEPTH, DEC_BATCH, CONV_W - 1, G_QKV), 1.0),
        'state_gdn': nrm((DEPTH, DEC_BATCH, G_HEADS, G_HEAD_DIM, G_HEAD_DIM), 0.1),
        'p_prompt': nrm((DEPTH, BATCH, SEQ, D_PLE), 1.0),
        'p_sample': nrm((DEPTH, DEC_BATCH, DEC_SEQ, D_PLE), 1.0),
        'ln_mix_g': 1.0 + nrm((DEPTH, D_MODEL), 0.02),
        'w_in': nrm((DEPTH, D_MODEL, D_IN), D_MODEL ** -0.5),
        'mu_shift': uni((DEPTH, R_PROJ), 0.0, 1.0),
        'w0': uni((DEPTH, D_R), -5.0, 0.0),
        'w_lora_up': nrm((DEPTH, LORA_W, D_R), 0.1),
        'a0': nrm((DEPTH, D_R), 0.1),
        'a_lora_up': nrm((DEPTH, LORA_A, D_R), 0.1),
        'g_lora_up': nrm((DEPTH, LORA_G, D_R), LORA_G ** -0.5),
        'k_k': 0.85 + nrm((DEPTH, D_R), 0.05),
        'k_a': 1.0 + nrm((DEPTH, D_R), 0.05),
        'r_k': nrm((DEPTH, R_HEADS, R_HEAD_DIM), 0.1),
        'ln_x_w': 1.0 + nrm((DEPTH, D_R), 0.02),
        'ln_x_b': nrm((DEPTH, D_R), 0.02),
        'conv_w': nrm((DEPTH, CONV_W, G_QKV), 0.5),
        'a_log': jnp.log(uni((DEPTH, G_HEADS), 1.0, 16.0)),
        'dt_bias': dt + jnp.log(-jnp.expm1(-dt)),
        'gdn_norm_g': 1.0 + nrm((DEPTH, G_HEAD_DIM), 0.02),
        'w_out': nrm((DEPTH, D_MIX, D_MODEL), D_MIX ** -0.5),
        'ln_ffn_g': 1.0 + nrm((DEPTH, D_MODEL), 0.02),
        'w_gate': nrm((DEPTH, D_MODEL, D_FF), D_MODEL ** -0.5),
        'w_up': nrm((DEPTH, D_MODEL, D_FF), D_MODEL ** -0.5),
        'w_down': nrm((DEPTH, D_FF, D_MODEL), D_FF ** -0.5),
        'ln_ple_g': 1.0 + nrm((DEPTH, D_MODEL), 0.02),
        'w_ple_gate': nrm((DEPTH, D_MODEL, D_MODEL), D_MODEL ** -0.5),
        'w_ple_proj': nrm((DEPTH, D_PLE, D_MODEL), D_PLE ** -0.5),
        'final_norm_g': 1.0 + nrm((D_MODEL,), 0.02),
    }


def reference(x_prompt, x_sample, state_shift, state_wkv, state_conv, state_gdn, p_prompt, p_sample,
              ln_mix_g, w_in, mu_shift, w0, w_lora_up, a0, a_lora_up, g_lora_up, k_k, k_a, r_k,
              ln_x_w, ln_x_b, conv_w, a_log, dt_bias, gdn_norm_g, w_out, ln_ffn_g, w_gate, w_up,
              w_down, ln_ple_g, w_ple_gate, w_ple_proj, final_norm_g):
    Bp = x_prompt.shape[0]
    dt = x_prompt.dtype
    z_shift = jnp.zeros((Bp, R_PROJ), dt)
    z_wkv = jnp.zeros((Bp, R_HEADS, R_HEAD_DIM, R_HEAD_DIM), dt)
    z_conv = jnp.zeros((Bp, CONV_W - 1, G_QKV), dt)
    z_gdn = jnp.zeros((Bp, G_HEADS, G_HEAD_DIM, G_HEAD_DIM), dt)
    yp, ys = x_prompt, x_sample
    sp_shift, sp_wkv, sp_conv, sp_gdn = [], [], [], []
    ss_shift, ss_wkv, ss_conv, ss_gdn = [], [], [], []
    for l in range(DEPTH):
        wts = (ln_mix_g[l], w_in[l], mu_shift[l], w0[l], w_lora_up[l], a0[l], a_lora_up[l],
               g_lora_up[l], k_k[l], k_a[l], r_k[l], ln_x_w[l], ln_x_b[l], conv_w[l], a_log[l],
               dt_bias[l], gdn_norm_g[l], w_out[l], ln_ffn_g[l], w_gate[l], w_up[l], w_down[l],
               ln_ple_g[l], w_ple_gate[l], w_ple_proj[l])
        yp, a1, a2, a3, a4 = _layer(yp, p_prompt[l], z_shift, z_wkv, z_conv, z_gdn, *wts)
        ys, b1, b2, b3, b4 = _layer(ys, p_sample[l], state_shift[l], state_wkv[l],
                                    state_conv[l], state_gdn[l], *wts)
        sp_shift.append(a1); sp_wkv.append(a2); sp_conv.append(a3); sp_gdn.append(a4)
        ss_shift.append(b1); ss_wkv.append(b2); ss_conv.append(b3); ss_gdn.append(b4)
    y_prompt = _rmsnorm(yp, final_norm_g)
    y_sample = _rmsnorm(ys, final_norm_g)
    new_shift_p = jnp.stack(sp_shift)
    new_wkv_p = jnp.stack(sp_wkv)
    new_conv_p = jnp.stack(sp_conv)
    new_gdn_p = jnp.stack(sp_gdn)
    new_shift_s = jnp.stack(ss_shift)
    new_wkv_s = jnp.stack(ss_wkv)
    new_conv_s = jnp.stack(ss_conv)
    new_gdn_s = jnp.stack(ss_gdn)
    return (y_prompt, y_sample, new_shift_p, new_wkv_p, new_conv_p, new_gdn_p,
            new_shift_s, new_wkv_s, new_conv_s, new_gdn_s)
```

```python
import numpy as np
import concourse.bass as bass
import concourse.mybir as mybir
from concourse.bass_utils import run_bass_kernel_spmd

F32 = mybir.dt.float32
BF16 = mybir.dt.bfloat16
AF = mybir.ActivationFunctionType
ALU = mybir.AluOpType
AX = mybir.AxisListType

D = 1024
D_PLE = 256
R_PROJ = 1792
D_IN = 3848
D_FF = 2816
NHC = D_FF // 128
OFF_Z = 3328
OFF_B = 3840
C0 = 0.5 * float(np.exp(-0.5))
GN_EPS = 64e-5


class V:
    def __init__(self, t, ap):
        self.t = t
        self.ap = ap

    def __getitem__(self, k):
        return V(self.t, self.ap[k])

    def re(self, pat, **kw):
        return V(self.t, self.ap.rearrange(pat, **kw))

    def bc(self, shape):
        return V(self.t, self.ap.to_broadcast(list(shape)))

    def bitcast(self, dt):
        return V(self.t, self.ap.bitcast(dt))

    def v(self):
        return self

    @property
    def shape(self):
        return self.ap.shape


class Arena:
    def __init__(self, P, name, nbytes):
        self.P = P
        h = P.nc.alloc_sbuf_tensor("s_" + name, [128, nbytes // 4], F32)
        self.ap = h.ap()
        self.nbytes = nbytes
        self.off = 0
        self.hi = 0

    def take(self, name, shape, dt=F32):
        esz = 2 if dt == BF16 else 4
        n = 1
        for d in shape[1:]:
            n *= d
        off = (self.off + 63) // 64 * 64
        assert off + n * esz <= self.nbytes, (name, off, n * esz, self.nbytes)
        base = self.ap.bitcast(dt) if dt != F32 else self.ap
        view = base[0:shape[0], off // esz:off // esz + n]
        if len(shape) == 3:
            view = view.rearrange("p (a b) -> p a b", b=shape[2])
        elif len(shape) == 4:
            view = view.rearrange("p (a b c) -> p a b c", b=shape[2], c=shape[3])
        self.off = off + n * esz
        self.hi = max(self.hi, self.off)
        t = Tile(view, name)
        self.P.tiles.append(t)
        return t


class Tile:
    def __init__(self, ap, name):
        self.ap = ap
        self.name = name
        self.lw = None
        self.rd = []
        self.dsem = None
        self.dcount = 0
        self.last_dma = None

    def __getitem__(self, k):
        return V(self, self.ap[k])

    def v(self):
        return V(self, self.ap)

    def re(self, pat, **kw):
        return V(self, self.ap.rearrange(pat, **kw))

    @property
    def shape(self):
        return self.ap.shape


class Op:
    __slots__ = ("eng", "fn", "deps", "dma", "sig", "cnt", "idx")


ENGS = ("pe", "act", "dve", "pool", "sp")


class Prog:
    def __init__(self, nc):
        self.nc = nc
        self.ops = []
        self.tiles = []
        self.banks = []
        self.bi = 0

    def sbuf(self, name, shape, dt=F32):
        h = self.nc.alloc_sbuf_tensor("s_" + name, list(shape), dt)
        t = Tile(h.ap(), name)
        self.tiles.append(t)
        return t

    def psum(self, name, shape, dt=F32):
        h = self.nc.alloc_psum_tensor("p_" + name, list(shape), dt)
        t = Tile(h.ap(), name)
        self.tiles.append(t)
        return t

    def dram(self, name, shape, dt=F32, kind="Internal"):
        h = self.nc.dram_tensor(name, list(shape), dt, kind=kind)
        t = Tile(h.ap(), name)
        self.tiles.append(t)
        return t

    def bank(self):
        b = self.banks[self.bi % len(self.banks)]
        self.bi += 1
        return b

    def handoff(self, old, new):
        S = set()
        for t in old:
            if t.lw is not None:
                S.add(t.lw)
            S.update(t.rd)
        for t in new:
            t.rd = list(S | set(t.rd))

    def barrier(self):
        last = {}
        for op in self.ops:
            last[op.eng] = op.idx
        deps = set(last.values())
        for t in self.tiles:
            if t.last_dma is not None:
                deps.add(t.last_dma)
        for e in ENGS:
            op = self.add(e, lambda eng: eng.nop(), [], [])
            op.deps = set(deps)

    def add(self, eng, fn, reads, writes, dma=None):
        op = Op()
        op.eng = eng
        op.fn = fn
        op.idx = len(self.ops)
        op.dma = dma
        op.sig = dma is not None
        op.cnt = 0
        deps = set()
        rt = [x.t if isinstance(x, V) else x for x in reads]
        wt = [x.t if isinstance(x, V) else x for x in writes]
        for t in rt:
            if t.lw is not None:
                deps.add(t.lw)
        for t in wt:
            if t.lw is not None:
                deps.add(t.lw)
            deps.update(t.rd)
        for t in rt:
            t.rd.append(op.idx)
        for t in wt:
            t.lw = op.idx
            t.rd = []
        if eng == "pe":
            deps = {d for d in deps if self.ops[d].eng != "pe"}
        op.deps = deps
        if dma is not None:
            dma.dcount += 1
            op.cnt = dma.dcount * 16
            dma.last_dma = op.idx
        self.ops.append(op)
        return op

    @staticmethod
    def _ap(x):
        if isinstance(x, (V, Tile)):
            return x.ap
        return x

    @staticmethod
    def _tl(*xs):
        return [x for x in xs if isinstance(x, (V, Tile))]

    def mm(self, out, lhsT, rhs, start=True, stop=True):
        o, l, r = self._ap(out), self._ap(lhsT), self._ap(rhs)
        self.add("pe", lambda e: e.matmul(o, l, r, start=start, stop=stop),
                 self._tl(lhsT, rhs), self._tl(out))

    def tr(self, out, in_, ident):
        o, i, d = self._ap(out), self._ap(in_), self._ap(ident)
        self.add("pe", lambda e: e.transpose(o, i, d), self._tl(in_, ident), self._tl(out))

    def act(self, out, in_, func, bias=None, scale=None, accum=None):
        o, i = self._ap(out), self._ap(in_)
        kw = {}
        if bias is not None:
            kw["bias"] = self._ap(bias)
        if scale is not None:
            kw["scale"] = self._ap(scale)
        if accum is not None:
            kw["accum_out"] = self._ap(accum)
        self.add("act", lambda e: e.activation(o, i, func, **kw),
                 self._tl(in_, bias, scale), self._tl(out, accum))

    def tt(self, eng, out, a, b, op):
        o, x, y = self._ap(out), self._ap(a), self._ap(b)
        self.add(eng, lambda e: e.tensor_tensor(o, x, y, op), self._tl(a, b), self._tl(out))

    def ts(self, eng, out, a, s1, op0, s2=None, op1=None):
        o, x = self._ap(out), self._ap(a)
        c1, c2 = self._ap(s1), self._ap(s2)
        kw = {}
        if op1 is not None:
            kw["op1"] = op1
        self.add(eng, lambda e: e.tensor_scalar(o, x, c1, c2, op0, **kw),
                 self._tl(a, s1, s2), self._tl(out))

    def stt(self, out, a, s, b, op0, op1):
        o, x, c, y = self._ap(out), self._ap(a), self._ap(s), self._ap(b)
        self.add("dve", lambda e: e.scalar_tensor_tensor(o, x, c, y, op0, op1),
                 self._tl(a, s, b), self._tl(out))

    def copy(self, eng, out, in_):
        o, i = self._ap(out), self._ap(in_)
        if eng == "act":
            self.add(eng, lambda e: e.copy(o, i), self._tl(in_), self._tl(out))
        else:
            self.add(eng, lambda e: e.tensor_copy(o, i), self._tl(in_), self._tl(out))

    def memset(self, eng, out, val):
        o = self._ap(out)
        self.add(eng, lambda e: e.memset(o, val), [], self._tl(out))

    def recip(self, out, in_):
        o, i = self._ap(out), self._ap(in_)
        self.add("dve", lambda e: e.reciprocal(o, i), self._tl(in_), self._tl(out))

    def reduce(self, out, in_, op=ALU.add):
        o, i = self._ap(out), self._ap(in_)
        self.add("dve", lambda e: e.tensor_reduce(o, i, AX.X, op), self._tl(in_), self._tl(out))

    def scan(self, out, d0, d1):
        o, a, b = self._ap(out), self._ap(d0), self._ap(d1)
        self.add("dve", lambda e: e.tensor_tensor_scan(o, a, b, 0.0, ALU.mult, ALU.add),
                 self._tl(d0, d1), self._tl(out))

    def dma(self, q, out, in_, key=None, **kw):
        o, i = self._ap(out), self._ap(in_)
        if q == "pool":
            kw.setdefault("max_dma_last_dim", 4096)
        kt = key if key is not None else (out.t if isinstance(out, V) else out)
        self.add(q, lambda e: e.dma_start(o, i, **kw), self._tl(in_), self._tl(out), dma=kt)

    def generic(self, eng, fn, reads, writes):
        self.add(eng, fn, reads, writes)

    def emit(self):
        nc = self.nc
        ops = self.ops
        for op in ops:
            for d in op.deps:
                ops[d].sig = True
        esem = {e: nc.alloc_semaphore("es_" + e) for e in ENGS}
        cnt = {e: 0 for e in ENGS}
        for op in ops:
            if op.dma is not None:
                if op.dma.dsem is None:
                    op.dma.dsem = nc.alloc_semaphore("ds_" + op.dma.name)
            elif op.sig:
                cnt[op.eng] += 1
                op.cnt = cnt[op.eng]

        def comp(op):
            if op.dma is not None:
                return op.dma.dsem, op.cnt
            return esem[op.eng], op.cnt

        per = {e: [o for o in ops if o.eng == e] for e in ENGS}
        dma_tiles = [t for t in self.tiles if t.dsem is not None]

        def run(ename, eng):
            waited = {}
            for op in per[ename]:
                need = {}
                for d in op.deps:
                    s, v = comp(ops[d])
                    k = id(s)
                    if waited.get(k, 0) >= v:
                        continue
                    if k not in need or need[k][1] < v:
                        need[k] = (s, v)
                for k, (s, v) in need.items():
                    eng.wait_ge(s, v)
                    waited[k] = v
                ins = op.fn(eng)
                if op.dma is not None:
                    ins.then_inc(op.dma.dsem, 16)
                elif op.sig:
                    ins.then_inc(esem[ename], 1)
            if ename == "sp":
                for t in dma_tiles:
                    eng.wait_ge(t.dsem, t.dcount * 16)

        with nc.Block() as block:
            @block.tensor
            def _(e):
                run("pe", e)

            @block.scalar
            def _(e):
                run("act", e)

            @block.vector
            def _(e):
                run("dve", e)

            @block.gpsimd
            def _(e):
                run("pool", e)

            @block.sync
            def _(e):
                run("sp", e)
        return {e: len(per[e]) for e in ENGS}, cnt


COLS = {}
_c = 0
for _n, _w in (("gmix", 8), ("gffn", 8), ("gple", 8), ("mu", 14), ("w0", 4), ("a0", 4), ("kk", 4),
               ("ka", 4), ("rk", 4), ("cw0", 12), ("cw1", 12), ("cw2", 12), ("cw3", 12)):
    COLS[_n] = (_c, _w)
    _c += _w
NCOLS = _c
BCS = {}
_c = 0
for _n, _w in (("lnxw", 512), ("lnxb", 512), ("gng", 128), ("fng", 1024), ("alog", 4), ("dtb", 4)):
    BCS[_n] = (_c, _w)
    _c += _w
NBC = _c


def _pack_consts(inp):
    cols = np.zeros((128, NCOLS), np.float32)

    def put(name, vec):
        o, w = COLS[name]
        cols[:, o:o + w] = np.asarray(vec, np.float32).reshape(w, 128).T

    put("gmix", inp["ln_mix_g"][0])
    put("gffn", inp["ln_ffn_g"][0])
    put("gple", inp["ln_ple_g"][0])
    put("mu", inp["mu_shift"][0])
    put("w0", inp["w0"][0])
    put("a0", inp["a0"][0])
    put("kk", inp["k_k"][0])
    put("ka", inp["k_a"][0])
    put("rk", inp["r_k"][0].reshape(-1))
    for j in range(4):
        put("cw%d" % j, inp["conv_w"][0][j])
    bc = np.zeros((128, NBC), np.float32)

    def putb(name, vec):
        o, w = BCS[name]
        bc[:, o:o + w] = np.asarray(vec, np.float32).reshape(1, w)

    putb("lnxw", inp["ln_x_w"][0])
    putb("lnxb", inp["ln_x_b"][0])
    putb("gng", inp["gdn_norm_g"][0])
    putb("fng", inp["final_norm_g"])
    putb("alog", inp["a_log"][0])
    putb("dtb", inp["dt_bias"][0])
    return cols, bc


def build(NA, NB, NS, CP=128, CS=16, dbg=None):
    nc = bass.Bass("TRN2", target_bir_lowering=False)
    P = Prog(nc)
    TA, TB, TS = max(NA, 1) * CP, max(NB, 1) * CP, max(NS, 1) * CS
    NS1 = max(NS, 1)

    def din(name, shape):
        return P.dram(name, shape, F32, kind="ExternalInput")

    def dout(name, shape):
        return P.dram(name, shape, F32, kind="ExternalOutput")

    xa = din("xa", [TA, D])
    xb = din("xb", [TB, D])
    pb = din("pb", [TB, D_PLE])
    xs = din("xs", [TS, D])
    ps = din("ps", [TS, D_PLE])
    st_shift = din("st_shift", [NS1, R_PROJ])
    st_wkv = din("st_wkv", [NS1, 8, 64, 64])
    st_conv = din("st_conv", [NS1, 3, 1536])
    st_gdn = din("st_gdn", [NS1, 4, 128, 128])
    w_in = din("w_in", [D, D_IN])
    w_lup = din("w_lup", [64, 512])
    a_lup = din("a_lup", [64, 512])
    g_lup = din("g_lup", [128, 512])
    w_out = din("w_out", [D, D])
    w_gate = din("w_gate", [D, D_FF])
    w_up = din("w_up", [D, D_FF])
    w_down = din("w_down", [D_FF, D])
    w_pg = din("w_pg", [D, D])
    w_pp = din("w_pp", [D_PLE, D])
    cols_d = din("cols", [128, NCOLS])
    bc_d = din("bc", [128, NBC])

    yb = dout("yb", [TB, D])
    ys = dout("ys", [TS, D])
    o_shift_p = dout("o_shift_p", [1, R_PROJ])
    o_wkv_p = dout("o_wkv_p", [1, 8, 64, 64])
    o_conv_p = dout("o_conv_p", [1, 3, 1536])
    o_gdn_p = dout("o_gdn_p", [1, 4, 128, 128])
    o_shift_s = dout("o_shift_s", [NS1, R_PROJ])
    o_wkv_s = dout("o_wkv_s", [NS1, 8, 64, 64])
    o_conv_s = dout("o_conv_s", [NS1, 3, 1536])
    o_gdn_s = dout("o_gdn_s", [NS1, 4, 128, 128])

    sc_gate = P.dram("sc_gate", [NHC, 128, 8, 128], BF16)
    sc_up = P.dram("sc_up", [NHC, 128, 8, 128], BF16)
    NG = NHC // 2
    sc_down = P.dram("sc_down", [NG, 128, 2, D], BF16)
    sc_out = P.dram("sc_out", [4, 128, 2, D], BF16)
    sc_pg = P.dram("sc_pg", [4, 128, 2, D], BF16)
    sc_pp = P.dram("sc_pp", [1, 128, 2, D], BF16)

    P.banks = [P.psum("bank%d" % i, [128, 512], F32) for i in range(8)]

    win = P.sbuf("win", [128, 8, D_IN], BF16)
    lor = P.sbuf("lor", [128, 512], BF16)
    gup = P.sbuf("gup", [128, 512], BF16)
    cols = P.sbuf("cols", [128, NCOLS], F32)
    bcs = P.sbuf("bcs", [128, NBC], F32)
    der = P.sbuf("der", [128, 32], F32)
    identb = P.sbuf("identb", [128, 128], BF16)
    identf = P.sbuf("identf", [128, 128], F32)
    mTs = P.sbuf("mTs", [128, 128], F32)
    mTi = P.sbuf("mTi", [128, 128], F32)
    mLs = P.sbuf("mLs", [128, 128], F32)
    ones = P.sbuf("ones", [128, 128], F32)
    blk = P.sbuf("blk", [128, 2], F32)
    pw = P.sbuf("pw", [128, 16], F32)
    nea = P.sbuf("nea", [128, 4], F32)

    def col(name, i=None, n=None):
        o, w = COLS[name]
        if i is None:
            return cols[:, o:o + w]
        return cols[:, o + i:o + i + (n or 1)]

    def bcv(name):
        o, w = BCS[name]
        return bcs[:, o:o + w]

    for i in range(8):
        P.dma("pool", win[:, i, :], w_in[i * 128:(i + 1) * 128, :])
    P.dma("pool", lor[0:64, :], w_lup.v())
    P.dma("pool", lor[64:128, :], a_lup.v())
    P.dma("pool", gup.v(), g_lup.v())
    P.dma("sp", cols.v(), cols_d.v())
    P.dma("sp", bcs.v(), bc_d.v())
    for g in range(4):
        P.dma("pool", sc_out[g], w_out[g * 256:(g + 1) * 256, :].re("(c p) n -> p c n", p=128), key=sc_out)
    for g in range(NHC):
        P.dma("pool", sc_gate[g], w_gate[:, g * 128:(g + 1) * 128].re("(kc p) n -> p kc n", p=128), key=sc_gate)
        P.dma("pool", sc_up[g], w_up[:, g * 128:(g + 1) * 128].re("(kc p) n -> p kc n", p=128), key=sc_up)
    for g in range(NG):
        P.dma("pool", sc_down[g], w_down[g * 256:(g + 1) * 256, :].re("(c p) n -> p c n", p=128), key=sc_down)
    for g in range(4):
        P.dma("pool", sc_pg[g], w_pg[g * 256:(g + 1) * 256, :].re("(c p) n -> p c n", p=128), key=sc_pg)
    P.dma("pool", sc_pp[0], w_pp.re("(c p) n -> p c n", p=128), key=sc_pp)

    def aff(tile, pattern, cmp, cm):
        a = tile.ap
        P.memset("pool", tile.v(), 1.0)
        P.generic("pool", lambda e: e.affine_select(out=a, in_=a, pattern=pattern, compare_op=cmp, fill=0.0,
                                                     base=0, channel_multiplier=cm), [tile], [tile])

    bd32 = P.sbuf("bd32", [128, 128], BF16)
    off32 = P.sbuf("off32", [128, 128], BF16)
    off64 = P.sbuf("off64", [128, 128], BF16)

    def blockdiag(tile, bs):
        a = tile.ap.rearrange("p (b c) -> p b c", c=bs)
        nb = 128 // bs
        P.memset("pool", tile.v(), 1.0)
        P.generic("pool", lambda e: e.affine_select(out=a, in_=a, pattern=[[-bs, nb], [0, bs]], compare_op=ALU.is_ge,
                                                     fill=0.0, base=0, channel_multiplier=1), [tile], [tile])
        P.generic("pool", lambda e: e.affine_select(out=a, in_=a, pattern=[[bs, nb], [0, bs]], compare_op=ALU.is_ge,
                                                     fill=0.0, base=bs - 1, channel_multiplier=-1), [tile], [tile])

    blockdiag(bd32, 32)
    blockdiag(off64, 64)
    P.tt("pool", off32.v(), off64.v(), bd32.v(), ALU.subtract)
    P.ts("pool", off64.v(), off64.v(), -1.0, ALU.mult, 1.0, ALU.add)
    aff(identb, [[-1, 128]], ALU.is_equal, 1)
    aff(identf, [[-1, 128]], ALU.is_equal, 1)
    aff(mTs, [[1, 128]], ALU.is_gt, -1)
    aff(mTi, [[1, 128]], ALU.is_ge, -1)
    aff(mLs, [[-1, 128]], ALU.is_gt, 1)
    P.memset("dve", ones.v(), 1.0)
    P.memset("dve", blk.v(), 0.0)
    P.memset("dve", blk[0:64, 0:1], 1.0)
    P.memset("dve", blk[64:128, 1:2], 1.0)
    P.memset("dve", pw[:, 0:8], -0.5)
    P.memset("dve", pw[:, 8:16], 0.5)
    P.ts("dve", der[:, 0:14], col("mu"), -1.0, ALU.mult, 1.0, ALU.add)
    P.ts("dve", der[:, 14:18], col("w0"), 0.5, ALU.mult)
    P.ts("dve", der[:, 18:22], col("a0"), 0.5, ALU.mult)
    P.act(nea.v(), bcv("alog"), AF.Exp)
    P.ts("dve", nea.v(), nea.v(), -1.0, ALU.mult)

    A = P.sbuf("A", [128, 4, 128], F32)
    Abf = P.sbuf("Abf", [128, 4, 128], BF16)
    S = P.sbuf("S", [128, 4, 128], F32)
    Sbf = P.sbuf("Sbf", [128, 4, 128], BF16)
    fcar = P.sbuf("fcar", [128, 14, 1], F32)
    qcar = P.sbuf("qcar", [128, 12, 3], F32)

    NSL = 2
    gsl = [P.sbuf("gsl%d" % i, [128, 8, 128], BF16) for i in range(NSL)]
    usl = [P.sbuf("usl%d" % i, [128, 8, 128], BF16) for i in range(NSL)]
    dsl = [P.sbuf("dsl%d" % i, [128, 2, D], BF16) for i in range(NSL)]
    dsi = [0]
    TM = 2 * CP
    xq = [P.sbuf("xq%d" % i, [128, D], F32) for i in range(3)]
    mixTm = P.sbuf("mixTm", [128, 8, TM], BF16)
    h2T = mixTm
    actT = P.sbuf("actT", [128, NHC, TM], BF16)
    sgb = [P.sbuf("sgb%d" % i, [128, TM], F32) for i in range(2)]

    stats = [P.sbuf("st%d" % i, [128, 16], F32) for i in range(6)]
    sti = [0]

    def stat():
        sti[0] += 1
        return stats[sti[0] % len(stats)]

    xn = P.sbuf("xn", [128, D], BF16)
    hT = P.sbuf("hT", [128, 8, CP], BF16)
    sioA = P.sbuf("sioA", [128, 128], F32)
    sioB = P.sbuf("sioB", [128, 128], F32)
    sioW = P.sbuf("sioW", [128, 512], F32)
    sioC = P.sbuf("sioC", [128, 36], F32)

    rem = nc.sbuf_bytes_remaining
    arena = Arena(P, "ctx", (rem - 256) // 64 * 64)

    def norm_T(src, Cc, dstT, toff, gname):
        st = stat()
        P.act(xn[:Cc, :], src, AF.Square, accum=st[:Cc, 0:1])
        P.ts("dve", st[:Cc, 1:2], st[:Cc, 0:1], 1.0 / D, ALU.mult, 1e-6, ALU.add)
        P.tt("pool", st[:Cc, 2:3], st[:Cc, 1:2], pw[:Cc, 0:1], ALU.pow)
        P.ts("dve", xn[:Cc, :], src, st[:Cc, 2:3], ALU.mult)
        bk = P.bank()
        bb = bk.v().bitcast(BF16)
        for kc in range(8):
            P.tr(bb[:, kc * Cc:(kc + 1) * Cc], xn[:Cc, kc * 128:(kc + 1) * 128], identb[:Cc, :Cc])
        P.tt("dve", dstT[:, :, toff:toff + Cc], bb[:, 0:8 * Cc].re("p (k c) -> p k c", c=Cc),
             col(gname).re("p (k o) -> p k o", o=1).bc([128, 8, Cc]), ALU.mult)

    class Ctx:
        pass

    def make_ctx(Cc, tag):
        c = Ctx()
        c.C = Cc
        arena.off = 0
        s = lambda n, sh, dt=F32: arena.take(n + tag, sh, dt)
        c.fb = s("fb", [128, 14, Cc + 1])
        c.qb = s("qb", [128, 12, Cc + 3])
        c.lw = s("lw", [128, Cc], BF16)
        c.sg = s("sg", [128, Cc], BF16)
        c.cum = s("cum", [128, 4, Cc + 1])
        c.sm = s("sm", [128, 16])
        c.art = s("art", [128, 4, 2, Cc], BF16)
        c.br = s("br", [128, 4, Cc], BF16)
        c.kt = s("kt", [128, 4, Cc], BF16)
        c.Bhf = s("Bhf", [128, 4, Cc], BF16)
        c.Khf = s("Khf", [128, 4, Cc], BF16)
        c.vbf = s("vbf", [128, 4, Cc], BF16)
        c.kg = s("kg", [128, 4, Cc], BF16)
        c.qg = s("qg", [128, 4, Cc], BF16)
        c.kq = s("kq", [128, 4, 2, Cc], BF16)
        c.vgb = s("vgb", [128, 4, Cc], BF16)
        c.BK = s("BK", [Cc, 1024], BF16)
        c.Vt = s("Vt", [Cc, 512], BF16)
        c.Vg = s("Vg", [Cc, 512], BF16)
        c.kd = s("kd", [Cc, 512], BF16)
        c.tk = s("tk", [Cc, 64])
        c.eGl = s("eGl", [128, 4])
        c.zs = s("zs", [Cc, 512])
        c.gt = s("gt", [Cc, 512])
        c.pt = s("pt", [Cc, D_PLE])
        c.pbf = s("pbf", [Cc, D_PLE], BF16)
        off0 = arena.off
        c.fm = s("fm", [128, 14, Cc])
        c.t2 = s("t2", [128, 14, Cc])
        c.acc = s("acc", [128, 12, Cc])
        c.tw = s("tw", [128, 4, Cc])
        c.ta = s("ta", [128, 4, Cc])
        c.Wa = s("Wa", [128, 4, Cc])
        c.Wb = s("Wb", [128, 4, Cc])
        offZ = arena.off
        c.rhsG = s("rhsG", [Cc, 4, Cc])
        c.decT = s("decT", [Cc, 4, Cc])
        c.EG = s("EG", [128, 4, Cc])
        c.zset = [c.rhsG, c.decT, c.EG]
        arena.off = offZ
        c.kk = s("kk", [128, 4, Cc])
        c.ka = s("ka", [128, 4, Cc])
        c.keff = s("keff", [128, 4, Cc])
        c.alpha = [c.fm, c.t2, c.acc, c.tw, c.ta, c.Wa, c.Wb, c.kk, c.ka, c.keff]
        c.kset = [c.kk, c.ka, c.keff]
        arena.off = off0
        c.MakT = s("MakT", [Cc, 8, Cc], BF16)
        c.NrbT = s("NrbT", [Cc, 8, Cc], BF16)
        c.NrkT = s("NrkT", [Cc, 8, Cc], BF16)
        c.QKT = s("QKT", [Cc, 4, Cc], BF16)
        c.gh = max(1, min(12, 512 // Cc))
        c.ngr = (12 + c.gh - 1) // c.gh
        c.Pm = [s("Pm%d" % g, [Cc, c.gh, Cc], BF16) for g in range(c.ngr)]
        c.PT = [s("PT%d" % g, [Cc, c.gh, Cc], BF16) for g in range(c.ngr)]
        c.Rm = [s("Rm%d" % g, [Cc, c.gh, Cc], BF16) for g in range(c.ngr)]
        offC = arena.off
        c.Zb = s("Zb", [Cc, 512], BF16)
        c.Rh = s("Rh", [Cc, 512], BF16)
        c.Ub = s("Ub", [Cc, 512], BF16)
        c.Xb = s("Xb", [Cc, 512], BF16)
        c.mix = s("mix", [Cc, D], BF16)
        offE = arena.off
        c.chainset = [c.Zb, c.Rh, c.Ub, c.Xb, c.mix]
        c.moffset = []
        if Cc > 32:
            arena.off = offC
            c.Mo32 = [s("Mo32_%d" % g, [Cc, c.gh, Cc], BF16) for g in range(c.ngr)]
            c.Mo64 = [s("Mo64_%d" % g, [Cc, c.gh, Cc], BF16) for g in range(c.ngr)]
            c.moffset = c.Mo32 + c.Mo64
            assert arena.off <= offE
            arena.off = offE
        c.w5 = [s("w5_%d" % i, [Cc, 512]) for i in range(3)]
        c.beta = [c.MakT, c.NrbT, c.NrkT, c.QKT, c.Zb, c.Rh, c.Ub, c.Xb, c.mix] + c.Pm + c.PT + c.Rm + c.w5 + c.moffset
        return c

    def seq_init_zero(c):
        P.memset("pool", A.v(), 0.0)
        P.memset("pool", Abf.v(), 0.0)
        P.memset("pool", S.v(), 0.0)
        P.memset("pool", Sbf.v(), 0.0)
        P.memset("pool", fcar.v(), 0.0)
        P.memset("pool", qcar.v(), 0.0)
        P.memset("pool", c.cum[:, :, 0:1], 0.0)

    def chunk(c, xt, full, mixT_dst=None):
        Cc = c.C
        fb, qb = c.fb, c.qb
        tk = c.tk
        P.handoff(c.beta + c.zset, c.alpha)

        norm_T(xt[:Cc, :], Cc, hT, 0, "gmix")

        P.copy("pool", fb[:, :, 0:1], fcar.v())
        P.copy("pool", qb[:, :, 0:3], qcar.v())

        def proj(tiles, dst, doff):
            for g0 in range(0, len(tiles), 4):
                grp = tiles[g0:g0 + 4]
                bk = P.bank()
                for i, t in enumerate(grp):
                    for kc in range(8):
                        P.mm(bk[:, i * Cc:(i + 1) * Cc], win[:, kc, t * 128:(t + 1) * 128], hT[:, kc, :Cc],
                             start=(kc == 0), stop=(kc == 7))
                n = len(grp)
                t0 = grp[0] - tiles[0]
                P.copy("act", dst[:, t0:t0 + n, doff:doff + Cc],
                       bk[:, 0:n * Cc].re("p (t c) -> p t c", c=Cc))

        proj(list(range(0, 14)), fb, 1)
        proj(list(range(14, 26)), qb, 3)
        bkb = P.bank()
        for kc in range(8):
            P.mm(bkb[:Cc, 0:8], hT[:, kc, :Cc], win[:, kc, OFF_B:OFF_B + 8], start=(kc == 0), stop=(kc == 7))
        if full:
            bkz = P.bank()
            for kc in range(8):
                P.mm(bkz[:Cc, 0:512], hT[:, kc, :Cc], win[:, kc, OFF_Z:OFF_Z + 512], start=(kc == 0), stop=(kc == 7))

        mu_bc = col("mu").re("p (t o) -> p t o", o=1).bc([128, 14, Cc])
        omu_bc = der[:, 0:14].re("p (t o) -> p t o", o=1).bc([128, 14, Cc])
        P.tt("pool", c.fm.v(), fb[:, :, 0:Cc], mu_bc, ALU.mult)
        P.tt("pool", c.t2.v(), fb[:, :, 1:Cc + 1], omu_bc, ALU.mult)
        P.tt("dve", c.fm.v(), c.fm.v(), c.t2.v(), ALU.add)
        P.copy("pool", fcar.v(), fb[:, :, Cc:Cc + 1])
        fm = c.fm
        r_, k_, v_ = fm[:, 0:4, :], fm[:, 4:8, :], fm[:, 8:12, :]

        def cwbc(j):
            return col("cw%d" % j).re("p (t o) -> p t o", o=1).bc([128, 12, Cc])
        tmp = c.t2[:, 0:12, :]
        P.tt("pool", c.acc.v(), qb[:, :, 0:Cc], cwbc(0), ALU.mult)
        for j in range(1, 4):
            P.tt("pool", tmp, qb[:, :, j:Cc + j], cwbc(j), ALU.mult)
            P.tt("dve", c.acc.v(), c.acc.v(), tmp, ALU.add)
        P.copy("pool", qcar.v(), qb[:, :, Cc:Cc + 3])

        qs = c.acc
        P.act(qs.v(), c.acc.v(), AF.Silu)
        P.act(c.lw[0:64, :], fm[0:64, 12, :], AF.Tanh)
        P.copy("pool", c.lw[64:128, :], fm[64:128, 12, :])
        if full:
            P.act(c.tw[:, 0, :], fm[:, 13, :], AF.Tanh, scale=0.5)
            P.ts("dve", c.sg.v(), c.tw[:, 0, :], 0.5, ALU.mult, 0.5, ALU.add)
        bkw, bka = P.bank(), P.bank()
        for t in range(4):
            P.mm(bkw[:, t * Cc:(t + 1) * Cc], lor[0:64, t * 128:(t + 1) * 128], c.lw[0:64, :])
        for t in range(4):
            P.mm(bka[:, t * Cc:(t + 1) * Cc], lor[64:128, t * 128:(t + 1) * 128], c.lw[64:128, :])
        if full:
            bkg = P.bank()
            P.mm(bkg[:Cc, 0:512], c.sg.v(), gup.v())
            P.copy("act", c.gt.v(), bkg[:Cc, 0:512])
        for t in range(4):
            P.act(c.tw[:, t, :], bkw[:, t * Cc:(t + 1) * Cc], AF.Tanh, bias=der[:, 14 + t:15 + t], scale=0.5)
        for t in range(4):
            P.act(c.ta[:, t, :], bka[:, t * Cc:(t + 1) * Cc], AF.Tanh, bias=der[:, 18 + t:19 + t], scale=0.5)
        P.act(tk[:, 0:4], bkb[:Cc, 0:4], AF.Tanh, scale=0.5)
        P.ts("dve", tk[:, 0:4], tk[:, 0:4], 0.5, ALU.mult, 0.5, ALU.add)
        P.tt("dve", tk[:, 4:8], bkb[:Cc, 4:8], bcv("dtb")[:Cc, :], ALU.add)
        if full:
            P.act(c.zs.v(), bkz[:Cc, 0:512], AF.Silu)

        P.ts("dve", c.tw.v(), c.tw.v(), 1.0, ALU.add)
        for t in range(4):
            P.scan(c.cum[:, t, 1:Cc + 1], ones[:, 0:Cc], c.tw[:, t, :])
        P.ts("dve", c.ta.v(), c.ta.v(), 0.5, ALU.mult, 0.5, ALU.add)
        a_ = c.ta
        P.ts("dve", c.sm[:, 0:4], c.cum[:, :, Cc:Cc + 1].re("p t o -> p (t o)"), -C0, ALU.mult)
        P.act(c.sm[:, 4:8], c.sm[:, 0:4], AF.Exp)
        P.act(tk[:, 8:12], tk[:, 4:8], AF.Exp)
        sp_ = stat()
        u_, L_, t_, m_ = tk[:, 8:12], sp_[:Cc, 0:4], sp_[:Cc, 4:8], sp_[:Cc, 8:12]
        P.act(L_, u_, AF.Ln, bias=1.0)
        P.ts("dve", t_, u_, -0.2, ALU.mult, 0.25, ALU.add)
        for cst in (1.0 / 3.0, 0.5, 1.0):
            P.tt("dve", t_, t_, u_, ALU.mult)
            P.ts("dve", t_, t_, -1.0, ALU.mult, cst, ALU.add)
        P.tt("dve", t_, t_, u_, ALU.mult)
        P.ts("dve", m_, u_, 0.1, ALU.is_lt)
        P.tt("dve", t_, t_, L_, ALU.subtract)
        P.tt("dve", t_, t_, m_, ALU.mult)
        P.tt("dve", tk[:, 8:12], L_, t_, ALU.add)
        P.tt("dve", tk[:, 12:16], tk[:, 8:12], nea[:Cc, :], ALU.mult)
        glog = tk[:, 12:16]

        def cbc(name):
            return col(name).re("p (t o) -> p t o", o=1).bc([128, 4, Cc])
        e1 = c.tw
        P.tt("pool", c.kk.v(), k_, cbc("kk"), ALU.mult)
        P.tt("pool", c.ka.v(), c.kk.v(), a_.v(), ALU.mult)
        P.ts("pool", e1.v(), a_.v(), -1.0, ALU.add)
        P.tt("pool", e1.v(), e1.v(), cbc("ka"), ALU.mult)
        P.stt(c.keff.v(), e1.v(), 1.0, k_, ALU.add, ALU.mult)
        P.act(c.Wa.v(), c.cum[:, :, 0:Cc], AF.Exp, scale=-C0)
        P.tt("dve", c.art[:, :, 0, :], c.kk.v(), c.Wa.v(), ALU.mult)
        P.act(c.Wb.v(), c.cum[:, :, 1:Cc + 1], AF.Exp, scale=C0)
        P.stt(c.br.v(), c.ka.v(), -1.0, c.Wb.v(), ALU.mult, ALU.mult)
        P.tt("dve", c.kt.v(), c.keff.v(), c.Wb.v(), ALU.mult)
        for t in range(4):
            P.act(c.Wa[:, t, :], c.cum[:, t, 1:Cc + 1], AF.Exp, bias=c.sm[:, t:t + 1], scale=C0)
        P.stt(c.Bhf.v(), c.ka.v(), -1.0, c.Wa.v(), ALU.mult, ALU.mult)
        P.tt("pool", c.Khf.v(), c.keff.v(), c.Wa.v(), ALU.mult)
        if full:
            P.act(c.Wb.v(), c.cum[:, :, 1:Cc + 1], AF.Exp, scale=-C0)
            P.tt("dve", c.art[:, :, 1, :], r_, c.Wb.v(), ALU.mult)
        P.copy("pool", c.vbf.v(), v_)
        P.tt("pool", c.kk.v(), c.kk.v(), c.kk.v(), ALU.mult)
        bks = P.bank()
        for t in range(4):
            P.mm(bks[:Cc, 2 * t:2 * t + 2], c.kk[:, t, :], blk.v())
        if full:
            P.tt("pool", e1.v(), r_, cbc("rk"), ALU.mult)
            P.tt("pool", e1.v(), e1.v(), c.keff.v(), ALU.mult)
            for t in range(4):
                P.mm(bks[:Cc, 8 + 2 * t:10 + 2 * t], e1[:, t, :], blk.v())
        sq8 = c.t2[:, 0:8, :]
        P.tt("pool", sq8, qs[:, 0:8, :], qs[:, 0:8, :], ALU.mult)
        for t in range(8):
            P.mm(bks[:Cc, 16 + t:17 + t], c.t2[:, t, :], ones[:, 0:1])
        P.mm(bks[:Cc, 24:28], mLs[:Cc, :Cc], glog)
        P.mm(bks[:, 28:32], ones[:Cc, :], glog)
        P.ts("dve", tk[:, 16:24], bks[:Cc, 0:8], 1e-12, ALU.add)
        P.recip(tk[:, 16:24], tk[:, 16:24])
        s2 = tk[:, 16:24]
        if full:
            P.copy("dve", tk[:, 24:32], bks[:Cc, 8:16])
        P.ts("dve", tk[:, 32:40], bks[:Cc, 16:24], 1e-6, ALU.add)
        P.tt("pool", tk[:, 40:48], tk[:, 32:40], pw[:Cc, 0:8], ALU.pow)
        P.tt("pool", tk[:, 48:52], tk[:, 36:40], pw[:Cc, 8:12], ALU.pow)
        invs = tk[:, 48:52]
        P.recip(tk[:, 52:56], tk[:, 36:40])
        P.tt("dve", tk[:, 52:56], tk[:, 52:56], tk[:, 0:4], ALU.mult)
        c1 = tk[:, 52:56]
        P.ts("dve", tk[:, 56:60], tk[:, 52:56], -1.0, ALU.mult)
        nc1 = tk[:, 56:60]
        P.ts("dve", tk[:, 60:64], tk[:, 40:44], 128.0 ** -0.5, ALU.mult)
        sqn = tk[:, 60:64]
        st = stat()
        P.act(st[:Cc, 0:4], bks[:Cc, 24:28], AF.Exp)
        eGrem = st[:Cc, 0:4]
        P.act(c.eGl.v(), bks[:, 28:32], AF.Exp)

        P.handoff(c.kset, c.zset)
        P.tt("dve", c.rhsG.v(), mTi[:Cc, :Cc].re("p (o c) -> p o c", o=1).bc([Cc, 4, Cc]),
             glog.re("p (h o) -> p h o", o=1).bc([Cc, 4, Cc]), ALU.mult)
        rg = c.rhsG.re("p h c -> p (h c)")
        bkG, bkB = P.bank(), P.bank()
        P.mm(bkG[:Cc, 0:4 * Cc], mLs[:Cc, :Cc], rg)
        P.mm(bkB[:, 0:4 * Cc], ones[:Cc, :], rg)
        P.act(c.decT.re("p h c -> p (h c)"), bkG[:Cc, 0:4 * Cc], AF.Exp)
        P.act(c.EG.re("p h c -> p (h c)"), bkB[:, 0:4 * Cc], AF.Exp)
        dms, dmi = c.rhsG, c.decT
        P.tt("pool", dms.v(), c.decT.v(), mTs[:Cc, :Cc].re("p (o c) -> p o c", o=1).bc([Cc, 4, Cc]), ALU.mult)
        if full:
            P.tt("pool", dmi.v(), c.decT.v(), mTi[:Cc, :Cc].re("p (o c) -> p o c", o=1).bc([Cc, 4, Cc]), ALU.mult)
        P.tt("dve", c.kg.v(), qs[:, 4:8, :], c.EG.v(), ALU.mult)
        if full:
            P.tt("dve", c.qg.v(), qs[:, 0:4, :], c.EG.v(), ALU.mult)
            P.copy("pool", c.kq[:, :, 1, :], qs[:, 0:4, :])
        P.copy("pool", c.kq[:, :, 0, :], qs[:, 4:8, :])
        P.copy("pool", c.vgb.v(), qs[:, 8:12, :])

        bt1 = P.bank()
        b1 = bt1.v().bitcast(BF16)
        for t in range(4):
            P.tr(b1[:Cc, t * 128:(t + 1) * 128], c.Bhf[:, t, :], identb.v())
        for t in range(4):
            P.tr(b1[:Cc, 512 + t * 128:512 + (t + 1) * 128], c.Khf[:, t, :], identb.v())
        P.copy("act", c.BK.v(), b1[:Cc, 0:1024])
        bt2 = P.bank()
        b2 = bt2.v().bitcast(BF16)
        for t in range(4):
            P.tr(b2[:Cc, t * 128:(t + 1) * 128], c.vbf[:, t, :], identb.v())
        for t in range(4):
            P.tr(b2[:Cc, 512 + t * 128:512 + (t + 1) * 128], c.vgb[:, t, :], identb.v())
        P.copy("act", c.Vt.v(), b2[:Cc, 0:512])
        P.copy("act", c.Vg.v(), b2[:Cc, 512:1024])
        bt3 = P.bank()
        b3 = bt3.v().bitcast(BF16)
        for t in range(4):
            P.tr(b3[:Cc, t * 128:(t + 1) * 128], c.kq[:, t, 0, :], identb.v())
        for h in range(4):
            P.ts("dve", c.kd[:, h * 128:(h + 1) * 128], b3[:Cc, h * 128:(h + 1) * 128], eGrem[:, h:h + 1], ALU.mult)

        P.handoff(c.alpha, c.beta)
        NW = 2 if full else 1
        P0 = c.Pm

        def hsl(tiles, h):
            return tiles[h // c.gh][:, h % c.gh, :]

        for h in range(8):
            t, hp = h // 2, (h % 2) * 64
            bk = P.bank()
            rhs = c.art[hp:hp + 64, t, 0:NW, :].re("p w c -> p (w c)")
            P.mm(bk[:Cc, 0:NW * Cc], c.br[hp:hp + 64, t, :], rhs)
            P.mm(bk[:Cc, 256:256 + NW * Cc], c.kt[hp:hp + 64, t, :], rhs)
            P.stt(hsl(P0, h), bk[:Cc, 0:Cc], s2[:, h:h + 1], mTs[:Cc, :Cc], ALU.mult, ALU.mult)
            P.tt("dve", c.MakT[:, h, :], bk[:Cc, 256:256 + Cc], mTs[:Cc, :Cc], ALU.mult)
            if full:
                P.tt("dve", c.NrbT[:, h, :], bk[:Cc, Cc:2 * Cc], mTi[:Cc, :Cc], ALU.mult)
                P.tt("dve", c.NrkT[:, h, :], bk[:Cc, 256 + Cc:256 + 2 * Cc], mTi[:Cc, :Cc], ALU.mult)
        for h in range(4):
            if h % 2 == 0:
                bk = P.bank()
            o = (h % 2) * 256
            P.mm(bk[:Cc, o:o + NW * Cc], c.kq[:, h, 0, :], c.kq[:, h, 0:NW, :].re("p w c -> p (w c)"))
            P.stt(hsl(P0, 8 + h), bk[:Cc, o:o + Cc], nc1[:, h:h + 1], dms[:, h, :], ALU.mult, ALU.mult)
            if full:
                P.tt("dve", c.QKT[:, h, :], bk[:Cc, o + Cc:o + 2 * Cc], dmi[:, h, :], ALU.mult)

        gh, ngr = c.gh, c.ngr
        heads = [list(range(g * gh, min(12, (g + 1) * gh))) for g in range(ngr)]
        hier = Cc > 32

        def hb(m, n):
            return m[:Cc, :Cc].re("p (o c) -> p o c", o=1).bc([Cc, n, Cc])

        def transp(src, dst, g):
            bk = P.bank()
            bb = bk.v().bitcast(BF16)
            n = len(heads[g])
            for i in range(n):
                P.tr(bb[:Cc, i * Cc:(i + 1) * Cc], src[g][:, i, :], identb[:Cc, :Cc])
            P.copy("act", dst[g][:, 0:n, :], bb[:Cc, 0:n * Cc].re("p (h c) -> p h c", c=Cc))

        if hier:
            P.handoff(c.chainset, c.moffset)
        for g in range(ngr):
            n = len(heads[g])
            transp(c.Pm, c.PT, g)
            if hier:
                P.tt("pool", c.Mo32[g][:, 0:n, :], c.PT[g][:, 0:n, :], hb(off32, n), ALU.mult)
                P.tt("pool", c.Mo64[g][:, 0:n, :], c.PT[g][:, 0:n, :], hb(off64, n), ALU.mult)
                P.tt("pool", c.PT[g][:, 0:n, :], c.PT[g][:, 0:n, :], hb(bd32, n), ALU.mult)
                P.tt("dve", c.Pm[g][:, 0:n, :], c.Pm[g][:, 0:n, :], hb(bd32, n), ALU.mult)
            P.tt("pool", c.Rm[g][:, 0:n, :], c.Pm[g][:, 0:n, :], hb(identb, n), ALU.add)
        L = int(np.log2(min(Cc, 32))) - 1
        for lv in range(1, L + 1):
            lastl = lv == L
            for g in range(ngr):
                n = len(heads[g])
                if not lastl:
                    bkP = P.bank()
                    for i in range(n):
                        P.mm(bkP[:Cc, i * Cc:(i + 1) * Cc], c.PT[g][:, i, :], c.Pm[g][:, i, :])
                bkT = P.bank()
                for i in range(n):
                    P.mm(bkT[:Cc, i * Cc:(i + 1) * Cc], c.Pm[g][:, i, :], c.PT[g][:, i, :])
                P.copy("act", c.PT[g][:, 0:n, :], bkT[:Cc, 0:n * Cc].re("p (h c) -> p h c", c=Cc))
                if not lastl:
                    P.copy("act", c.Pm[g][:, 0:n, :], bkP[:Cc, 0:n * Cc].re("p (h c) -> p h c", c=Cc))
                bkR = P.bank()
                for i in range(n):
                    P.mm(bkR[:Cc, i * Cc:(i + 1) * Cc], c.PT[g][:, i, :], c.Rm[g][:, i, :])
                P.tt("dve", c.Rm[g][:, 0:n, :], bkR[:Cc, 0:n * Cc].re("p (h c) -> p h c", c=Cc),
                     c.Rm[g][:, 0:n, :], ALU.add)
        if hier:
            for Mo in (c.Mo32, c.Mo64):
                for g in range(ngr):
                    n = len(heads[g])
                    transp(c.Rm, c.PT, g)
                    bkX = P.bank()
                    for i in range(n):
                        P.mm(bkX[:Cc, i * Cc:(i + 1) * Cc], Mo[g][:, i, :], c.Rm[g][:, i, :])
                    P.copy("act", c.Pm[g][:, 0:n, :], bkX[:Cc, 0:n * Cc].re("p (h c) -> p h c", c=Cc))
                    bkY_ = P.bank()
                    for i in range(n):
                        P.mm(bkY_[:Cc, i * Cc:(i + 1) * Cc], c.PT[g][:, i, :], c.Pm[g][:, i, :])
                    P.tt("dve", c.Rm[g][:, 0:n, :], bkY_[:Cc, 0:n * Cc].re("p (h c) -> p h c", c=Cc),
                         c.Rm[g][:, 0:n, :], ALU.add)
            P.handoff(c.moffset, c.chainset)
        Tt = c.Rm

        bkZ, bkK = P.bank(), P.bank()
        for h in range(8):
            t, hp = h // 2, (h % 2) * 64
            P.mm(bkZ[:Cc, h * 64:(h + 1) * 64], c.art[hp:hp + 64, t, 0, :], Abf[hp:hp + 64, t, hp:hp + 64],
                 start=True, stop=False)
            P.mm(bkZ[:Cc, h * 64:(h + 1) * 64], c.MakT[:, h, :], c.Vt[:, h * 64:(h + 1) * 64],
                 start=False, stop=True)
        for h in range(4):
            P.mm(bkK[:Cc, h * 128:(h + 1) * 128], c.kg[:, h, :], Sbf[:, h, :])
        P.copy("act", c.Zb.v(), bkZ[:Cc, 0:512])
        for h in range(4):
            P.stt(c.Rh[:, h * 128:(h + 1) * 128], c.Vg[:, h * 128:(h + 1) * 128], invs[:, h:h + 1],
                  bkK[:Cc, h * 128:(h + 1) * 128], ALU.mult, ALU.subtract)
        bkU, bkY = P.bank(), P.bank()
        for h in range(8):
            P.mm(bkU[:Cc, h * 64:(h + 1) * 64], hsl(Tt, h), c.Zb[:, h * 64:(h + 1) * 64])
        for h in range(4):
            P.mm(bkY[:Cc, h * 128:(h + 1) * 128], hsl(Tt, 8 + h), c.Rh[:, h * 128:(h + 1) * 128])
        P.tt("dve", c.Ub.re("p (h v) -> p h v", v=64), bkU[:Cc, 0:512].re("p (h v) -> p h v", v=64),
             s2.re("p (h o) -> p h o", o=1).bc([Cc, 8, 64]), ALU.mult)
        P.tt("dve", c.Xb.re("p (h v) -> p h v", v=128), bkY[:Cc, 0:512].re("p (h v) -> p h v", v=128),
             c1.re("p (h o) -> p h o", o=1).bc([Cc, 4, 128]), ALU.mult)
        if full:
            bkO, bkYr = P.bank(), P.bank()
            for h in range(4):
                P.mm(bkO[:Cc, h * 128:(h + 1) * 128], c.qg[:, h, :], Sbf[:, h, :], start=True, stop=False)
                P.mm(bkO[:Cc, h * 128:(h + 1) * 128], c.QKT[:, h, :], c.Xb[:, h * 128:(h + 1) * 128],
                     start=False, stop=True)
            for h in range(8):
                t, hp = h // 2, (h % 2) * 64
                o_ = bkYr[:Cc, h * 64:(h + 1) * 64]
                P.mm(o_, c.art[hp:hp + 64, t, 1, :], Abf[hp:hp + 64, t, hp:hp + 64], start=True, stop=False)
                P.mm(o_, c.NrbT[:, h, :], c.Ub[:, h * 64:(h + 1) * 64], start=False, stop=False)
                P.mm(o_, c.NrkT[:, h, :], c.Vt[:, h * 64:(h + 1) * 64], start=False, stop=True)
        bkA, bkS = P.bank(), P.bank()
        for t in range(4):
            P.mm(bkA[:, t * 128:(t + 1) * 128], c.BK[:, t * 128:(t + 1) * 128], c.Ub[:, t * 128:(t + 1) * 128],
                 start=True, stop=False)
            P.mm(bkA[:, t * 128:(t + 1) * 128], c.BK[:, 512 + t * 128:512 + (t + 1) * 128],
                 c.Vt[:, t * 128:(t + 1) * 128], start=False, stop=True)
        for h in range(4):
            P.mm(bkS[:, h * 128:(h + 1) * 128], c.kd[:, h * 128:(h + 1) * 128], c.Xb[:, h * 128:(h + 1) * 128])
        for t in range(4):
            for hp in (0, 64):
                P.stt(A[hp:hp + 64, t, hp:hp + 64], A[hp:hp + 64, t, hp:hp + 64], c.sm[hp:hp + 64, 4 + t:5 + t],
                      bkA[hp:hp + 64, t * 128 + hp:t * 128 + hp + 64], ALU.mult, ALU.add)
        P.copy("pool", Abf[0:64, :, 0:64], A[0:64, :, 0:64])
        P.copy("pool", Abf[64:128, :, 64:128], A[64:128, :, 64:128])
        for h in range(4):
            P.stt(S[:, h, :], S[:, h, :], c.eGl[:, h:h + 1], bkS[:, h * 128:(h + 1) * 128], ALU.mult, ALU.add)
        P.copy("act", Sbf.v(), S.v())

        if not full:
            return
        y, ysq, w3 = c.w5[0], c.w5[1], c.w5[2]
        y3 = y.re("p (h v) -> p h v", v=64)
        P.copy("act", y.v(), bkYr[:Cc, 0:512])
        st = stat()
        P.reduce(st[:Cc, 0:8], y3)
        P.tt("pool", ysq.v(), y.v(), y.v(), ALU.mult)
        P.reduce(st[:Cc, 8:16], ysq.re("p (h v) -> p h v", v=64))
        st2 = stat()
        P.ts("dve", st2[:Cc, 0:8], st[:Cc, 0:8], 1.0 / 64, ALU.mult)
        P.tt("dve", st2[:Cc, 8:16], st2[:Cc, 0:8], st2[:Cc, 0:8], ALU.mult)
        P.stt(st[:Cc, 8:16], st[:Cc, 8:16], 1.0 / 64, st2[:Cc, 8:16], ALU.mult, ALU.subtract)
        P.ts("dve", st[:Cc, 8:16], st[:Cc, 8:16], GN_EPS, ALU.add)
        P.tt("pool", st[:Cc, 0:8], st[:Cc, 8:16], pw[:Cc, 0:8], ALU.pow)
        P.tt("dve", y3, y3, st2[:Cc, 0:8].re("p (h o) -> p h o", o=1).bc([Cc, 8, 64]), ALU.subtract)
        P.tt("dve", y3, y3, st[:Cc, 0:8].re("p (h o) -> p h o", o=1).bc([Cc, 8, 64]), ALU.mult)
        P.tt("pool", y.v(), y.v(), bcv("lnxw")[:Cc, :], ALU.mult)
        P.tt("pool", y.v(), y.v(), bcv("lnxb")[:Cc, :], ALU.add)
        P.tt("dve", w3.re("p (h v) -> p h v", v=64), c.Vt.re("p (h v) -> p h v", v=64),
             tk[:, 24:32].re("p (h o) -> p h o", o=1).bc([Cc, 8, 64]), ALU.mult)
        P.tt("pool", y.v(), y.v(), w3.v(), ALU.add)
        P.tt("dve", c.mix[:, 0:512], y.v(), c.gt.v(), ALU.mult)
        o, osq = c.w5[1], c.w5[2]
        o3 = o.re("p (h v) -> p h v", v=128)
        P.tt("dve", o3, bkO[:Cc, 0:512].re("p (h v) -> p h v", v=128),
             sqn.re("p (h o) -> p h o", o=1).bc([Cc, 4, 128]), ALU.mult)
        P.tt("pool", osq.v(), o.v(), o.v(), ALU.mult)
        st = stat()
        P.reduce(st[:Cc, 0:4], osq.re("p (h v) -> p h v", v=128))
        P.ts("dve", st[:Cc, 0:4], st[:Cc, 0:4], 1.0 / 128, ALU.mult, 1e-6, ALU.add)
        P.tt("pool", st[:Cc, 4:8], st[:Cc, 0:4], pw[:Cc, 0:4], ALU.pow)
        P.tt("dve", o3, o3, st[:Cc, 4:8].re("p (h o) -> p h o", o=1).bc([Cc, 4, 128]), ALU.mult)
        P.tt("pool", o3, o3, bcv("gng")[:Cc, :].re("p (o v) -> p o v", o=1).bc([Cc, 4, 128]), ALU.mult)
        P.tt("dve", c.mix[:, 512:1024], o.v(), c.zs.v(), ALU.mult)
        bk = P.bank()
        bb = bk.v().bitcast(BF16)
        for kc in range(8):
            P.tr(bb[:, kc * Cc:(kc + 1) * Cc], c.mix[:, kc * 128:(kc + 1) * 128], identb[:Cc, :Cc])
        P.copy("act", mixT_dst, bb[:, 0:8 * Cc].re("p (k c) -> p k c", c=Cc))
        if dbg == "mix":
            P.copy("dve", xt[:Cc, :], c.mix.v())

    def state_out(c, o_shift, o_wkv, o_conv, o_gdn, b):
        bk = P.bank()
        P.tr(bk[0:14, 0:128], fcar.re("p t o -> p (t o)"), identf.v())
        P.copy("act", sioA[0:14, 0:128], bk[0:14, 0:128])
        P.dma("sp", o_shift[b].re("(t p) -> t p", p=128), sioA[0:14, 0:128], key=sioA)
        q36 = sioC
        P.copy("pool", q36[:, 0:36].re("p (r t) -> p t r", r=3), qcar.v())
        bk2 = P.bank()
        P.tr(bk2[0:36, 0:128], q36[:, 0:36], identf.v())
        P.copy("act", sioB[0:36, 0:128], bk2[0:36, 0:128])
        P.dma("sp", o_conv[b].re("r (t p) -> (r t) p", p=128), sioB[0:36, 0:128], key=sioB)
        for t in range(4):
            bk3 = P.bank()
            P.tr(bk3[:, 0:128], A[:, t, :], identf.v())
            P.copy("act", sioW[:, t * 128:(t + 1) * 128], bk3[:, 0:128])
        for t in range(4):
            for hl in range(2):
                P.dma("sp", o_wkv[b, 2 * t + hl],
                      sioW[hl * 64:(hl + 1) * 64, t * 128 + hl * 64:t * 128 + (hl + 1) * 64], key=sioW)
        P.dma("sp", o_gdn[b].re("h k v -> k h v"), S.v(), key=S)

    def state_in(c, b):
        P.dma("sp", sioA[0:14, 0:128], st_shift[b].re("(t p) -> t p", p=128))
        bk = P.bank()
        P.tr(bk[:, 0:14], sioA[0:14, 0:128], identf[0:14, 0:14])
        P.copy("act", fcar.re("p t o -> p (t o)"), bk[:, 0:14])
        P.dma("sp", sioB[0:36, 0:128], st_conv[b].re("r (t p) -> (r t) p", p=128))
        bk2 = P.bank()
        P.tr(bk2[:, 0:36], sioB[0:36, 0:128], identf[0:36, 0:36])
        P.copy("act", qcar.v(), bk2[:, 0:36].re("p (r t) -> p t r", r=3))
        P.memset("pool", c.cum[:, :, 0:1], 0.0)
        P.dma("sp", sioW[0:64, 0:512].re("v (t hl k) -> v t hl k", t=4, hl=2),
              st_wkv[b].re("(t hl) v k -> v t hl k", hl=2))
        P.memset("pool", A.v(), 0.0)
        P.memset("pool", Abf.v(), 0.0)
        for t in range(4):
            bk3 = P.bank()
            P.tr(bk3[:, 0:64], sioW[0:64, t * 128:(t + 1) * 128], identf[0:64, 0:64])
            P.copy("act", A[0:64, t, 0:64], bk3[0:64, 0:64])
            P.copy("act", A[64:128, t, 64:128], bk3[64:128, 0:64])
        P.copy("pool", Abf[0:64, :, 0:64], A[0:64, :, 0:64])
        P.copy("pool", Abf[64:128, :, 64:128], A[64:128, :, 64:128])
        P.dma("sp", S.v(), st_gdn[b].re("h k v -> k h v"))
        P.copy("act", Sbf.v(), S.v())

    def stream_tm(sc, npieces, srcT, nch, Cc, finish):
        bks_ = [[P.bank() for n in range(2)] for j in range(nch)]
        for g in range(npieces):
            ds = dsl[dsi[0] % NSL]
            dsi[0] += 1
            P.dma("sp", ds.v(), sc[g])
            for j in range(nch):
                for n in range(2):
                    for hc in range(2):
                        P.mm(bks_[j][n][:Cc, 0:512], srcT(j, 2 * g + hc), ds[:, hc, n * 512:(n + 1) * 512],
                             start=(g == 0 and hc == 0), stop=(g == npieces - 1 and hc == 1))
        for j in range(nch):
            for n in range(2):
                finish(j, n, bks_[j][n])

    def macro(c, chunks):
        Cc = c.C
        nch = len(chunks)
        T = nch * Cc

        def dump():
            for j, (xr, psrc, ydst) in enumerate(chunks):
                P.dma("sp", ydst, xr[:Cc, :], key=xr)
        if dbg == "mix":
            return dump()
        stream_tm(sc_out, 4, lambda j, k: mixTm[:, k, j * Cc:(j + 1) * Cc], nch, Cc,
                  lambda j, n, bk: P.tt("dve", chunks[j][0][:Cc, n * 512:(n + 1) * 512],
                                        chunks[j][0][:Cc, n * 512:(n + 1) * 512], bk[:Cc, 0:512], ALU.add))
        if dbg == "x1":
            return dump()
        for j, (xr, _, _) in enumerate(chunks):
            norm_T(xr[:Cc, :], Cc, h2T, j * Cc, "gffn")
        for g in range(NHC):
            gs, us = gsl[g % NSL], usl[g % NSL]
            P.dma("sp", gs.v(), sc_gate[g])
            P.dma("sp", us.v(), sc_up[g])
            bG, bU = P.bank(), P.bank()
            for kc in range(8):
                P.mm(bG[:, 0:T], gs[:, kc, :], h2T[:, kc, 0:T], start=(kc == 0), stop=(kc == 7))
            for kc in range(8):
                P.mm(bU[:, 0:T], us[:, kc, :], h2T[:, kc, 0:T], start=(kc == 0), stop=(kc == 7))
            sg_ = sgb[g % 2]
            P.act(sg_[:, 0:T], bG[:, 0:T], AF.Silu)
            P.tt("dve", actT[:, g, 0:T], sg_[:, 0:T], bU[:, 0:T], ALU.mult)
        stream_tm(sc_down, NG, lambda j, k: actT[:, k, j * Cc:(j + 1) * Cc], nch, Cc,
                  lambda j, n, bk: P.tt("dve", chunks[j][0][:Cc, n * 512:(n + 1) * 512],
                                        chunks[j][0][:Cc, n * 512:(n + 1) * 512], bk[:Cc, 0:512], ALU.add))
        if dbg == "x2":
            return dump()
        for j, (xr, psrc, ydst) in enumerate(chunks):
            norm_T(xr[:Cc, :], Cc, h2T, j * Cc, "gple")
        gate = {}

        def fin_gate(j, n, bk):
            w = c.w5[n] if j == 0 else (c.zs if n == 0 else c.w5[2])
            P.act(w.v(), bk[:Cc, 0:512], AF.Tanh, scale=0.5)
            P.ts("dve", w.v(), w.v(), 0.5, ALU.mult, 0.5, ALU.add)
            gate[(j, n)] = w
        stream_tm(sc_pg, 4, lambda j, k: h2T[:, k, j * Cc:(j + 1) * Cc], nch, Cc, fin_gate)
        for j, (xr, psrc, ydst) in enumerate(chunks):
            P.dma("sp", c.pt.v(), psrc)
            P.copy("pool", c.pbf.v(), c.pt.v())
            bk = P.bank()
            bb = bk.v().bitcast(BF16)
            for kc in range(2):
                P.tr(bb[:, kc * Cc:(kc + 1) * Cc], c.pbf[:, kc * 128:(kc + 1) * 128], identb[:Cc, :Cc])
            P.copy("act", actT[:, 0:2, j * Cc:(j + 1) * Cc], bb[:, 0:2 * Cc].re("p (k c) -> p k c", c=Cc))

        def fin_pp(j, n, bk):
            w = gate[(j, n)]
            xr = chunks[j][0]
            P.tt("dve", w.v(), w.v(), bk[:Cc, 0:512], ALU.mult)
            P.tt("pool", xr[:Cc, n * 512:(n + 1) * 512], xr[:Cc, n * 512:(n + 1) * 512], w.v(), ALU.add)
        stream_tm(sc_pp, 1, lambda j, k: actT[:, k, j * Cc:(j + 1) * Cc], nch, Cc, fin_pp)
        if dbg == "x3":
            return dump()
        for j, (xr, psrc, ydst) in enumerate(chunks):
            st = stat()
            P.act(xn[:Cc, :], xr[:Cc, :], AF.Square, accum=st[:Cc, 0:1])
            P.ts("dve", st[:Cc, 1:2], st[:Cc, 0:1], 1.0 / D, ALU.mult, 1e-6, ALU.add)
            P.tt("pool", st[:Cc, 2:3], st[:Cc, 1:2], pw[:Cc, 0:1], ALU.pow)
            P.stt(xr[:Cc, :], xr[:Cc, :], st[:Cc, 2:3], bcv("fng")[:Cc, :], ALU.mult, ALU.mult)
            P.dma("sp", ydst, xr[:Cc, :], key=xr)

    cp = make_ctx(CP, "p")
    seq_init_zero(cp)
    xi = [0]

    def load_x(src):
        xt = xq[xi[0] % 3]
        xi[0] += 1
        P.dma("sp", xt[:src.shape[0], :], src)
        return xt

    srcs = [(xa[i * CP:(i + 1) * CP, :], False) for i in range(NA)] + \
           [(xb[i * CP:(i + 1) * CP, :], True) for i in range(NB)]
    nxt = load_x(srcs[0][0]) if srcs else None
    pend = []
    for i, (src, full) in enumerate(srcs):
        xt = nxt
        if i + 1 < len(srcs):
            nxt = load_x(srcs[i + 1][0])
        j = i - NA
        chunk(cp, xt, full, mixTm[:, :, len(pend) * CP:(len(pend) + 1) * CP] if full else None)
        if full:
            pend.append((xt, pb[j * CP:(j + 1) * CP, :], yb[j * CP:(j + 1) * CP, :]))
            if len(pend) == 2 or i == len(srcs) - 1:
                macro(cp, pend)
                pend = []
    if NA + NB > 0:
        state_out(cp, o_shift_p, o_wkv_p, o_conv_p, o_gdn_p, 0)

    if NS > 0:
        P.barrier()
        cs = make_ctx(CS, "s")
        pend = []
        for b in range(NS):
            state_in(cs, b)
            xt = load_x(xs[b * CS:(b + 1) * CS, :])
            chunk(cs, xt, True, mixTm[:, :, len(pend) * CS:(len(pend) + 1) * CS])
            state_out(cs, o_shift_s, o_wkv_s, o_conv_s, o_gdn_s, b)
            pend.append((xt, ps[b * CS:(b + 1) * CS, :], ys[b * CS:(b + 1) * CS, :]))
            if len(pend) == 2 or b == NS - 1:
                macro(cs, pend)
                pend = []
    info = P.emit()
    return nc, info


_CACHE = {}


def _get_prog(NA, NB, NS, dbg=None):
    key = (NA, NB, NS, dbg)
    if key not in _CACHE:
        _CACHE[key] = build(NA, NB, NS, dbg=dbg)
    return _CACHE[key]


def run_cores(per_core, NA, NB, NS):
    nc, info = _get_prog(NA, NB, NS)
    res = run_bass_kernel_spmd(nc, per_core, core_ids=list(range(len(per_core))))
    return res.results


def kernel(**inp):
    f = lambda a: np.ascontiguousarray(np.asarray(a, np.float32))
    xp = f(inp["x_prompt"])
    B, SEQ, _ = xp.shape
    xsm = f(inp["x_sample"])
    DB, DS, _ = xsm.shape
    pp = f(inp["p_prompt"])[0]
    psm = f(inp["p_sample"])[0]
    n_cores = 8
    halves = n_cores // B
    assert halves == 2
    HT = SEQ // 2
    NA = NB = HT // 128
    NS = DB // n_cores
    cols, bc = _pack_consts(inp)
    shared = {
        "w_in": f(inp["w_in"][0]), "w_lup": f(inp["w_lora_up"][0]), "a_lup": f(inp["a_lora_up"][0]),
        "g_lup": f(inp["g_lora_up"][0]), "w_out": f(inp["w_out"][0]), "w_gate": f(inp["w_gate"][0]),
        "w_up": f(inp["w_up"][0]), "w_down": f(inp["w_down"][0]), "w_pg": f(inp["w_ple_gate"][0]),
        "w_pp": f(inp["w_ple_proj"][0]), "cols": cols, "bc": bc,
    }
    per_core = []
    for cix in range(n_cores):
        b, half = cix // 2, cix % 2
        m = dict(shared)
        m["xa"] = np.zeros((HT, D), np.float32) if half == 0 else xp[b, 0:HT]
        m["xb"] = xp[b, half * HT:(half + 1) * HT]
        m["pb"] = pp[b, half * HT:(half + 1) * HT]
        sl = slice(cix * NS, (cix + 1) * NS)
        m["xs"] = xsm[sl].reshape(NS * DS, D)
        m["ps"] = psm[sl].reshape(NS * DS, D_PLE)
        m["st_shift"] = f(inp["state_shift"][0][sl])
        m["st_wkv"] = f(inp["state_wkv"][0][sl])
        m["st_conv"] = f(inp["state_conv"][0][sl])
        m["st_gdn"] = f(inp["state_gdn"][0][sl])
        per_core.append({k: np.ascontiguousarray(v) for k, v in m.items()})
    res = run_cores(per_core, NA, NB, NS)
    y_prompt = np.zeros((B, SEQ, D), np.float32)
    y_sample = np.zeros((DB, DS, D), np.float32)
    nsp = np.zeros((1, B, R_PROJ), np.float32)
    nwp = np.zeros((1, B, 8, 64, 64), np.float32)
    ncp = np.zeros((1, B, 3, 1536), np.float32)
    ngp = np.zeros((1, B, 4, 128, 128), np.float32)
    nss = np.zeros((1, DB, R_PROJ), np.float32)
    nws = np.zeros((1, DB, 8, 64, 64), np.float32)
    ncs = np.zeros((1, DB, 3, 1536), np.float32)
    ngs = np.zeros((1, DB, 4, 128, 128), np.float32)
    for cix in range(n_cores):
        b, half = cix // 2, cix % 2
        r = res[cix]
        y_prompt[b, half * HT:(half + 1) * HT] = r["yb"]
        sl = slice(cix * NS, (cix + 1) * NS)
        y_sample[sl] = r["ys"].reshape(NS, DS, D)
        if half == 1:
            nsp[0, b] = r["o_shift_p"][0]
            nwp[0, b] = r["o_wkv_p"][0]
            ncp[0, b] = r["o_conv_p"][0]
            ngp[0, b] = r["o_gdn_p"][0]
        nss[0, sl] = r["o_shift_s"]
        nws[0, sl] = r["o_wkv_s"]
        ncs[0, sl] = r["o_conv_s"]
        ngs[0, sl] = r["o_gdn_s"]
    return (y_prompt, y_sample, nsp, nwp, ncp, ngp, nss, nws, ncs, ngs)
```

```python
import numpy as np
import concourse.bass as bass
import concourse.mybir as mybir
from concourse.bass_utils import run_bass_kernel_spmd

F32 = mybir.dt.float32
BF16 = mybir.dt.bfloat16
AF = mybir.ActivationFunctionType
ALU = mybir.AluOpType
AX = mybir.AxisListType

D = 1024
D_PLE = 256
R_PROJ = 1792
D_IN = 3848
D_FF = 2816
NHC = D_FF // 128
OFF_Z = 3328
OFF_B = 3840
C0 = 0.5 * float(np.exp(-0.5))
GN_EPS = 64e-5


class V:
    def __init__(self, t, ap, gen=None):
        self.t = t
        self.ap = ap
        self.gen = gen

    def __getitem__(self, k):
        return V(self.t, self.ap[k], self.gen)

    def re(self, pat, **kw):
        return V(self.t, self.ap.rearrange(pat, **kw), self.gen)

    def bc(self, shape):
        return V(self.t, self.ap.to_broadcast(list(shape)), self.gen)

    def bitcast(self, dt):
        return V(self.t, self.ap.bitcast(dt), self.gen)

    def v(self):
        return self

    @property
    def shape(self):
        return self.ap.shape


class Arena:
    def __init__(self, P, name, nbytes):
        self.P = P
        h = P.nc.alloc_sbuf_tensor("s_" + name, [128, nbytes // 4], F32)
        self.ap = h.ap()
        self.nbytes = nbytes
        self.off = 0
        self.hi = 0

    def take(self, name, shape, dt=F32):
        esz = 2 if dt == BF16 else 4
        n = 1
        for d in shape[1:]:
            n *= d
        off = (self.off + 63) // 64 * 64
        assert off + n * esz <= self.nbytes, (name, off, n * esz, self.nbytes)
        base = self.ap.bitcast(dt) if dt != F32 else self.ap
        view = base[0:shape[0], off // esz:off // esz + n]
        if len(shape) == 3:
            view = view.rearrange("p (a b) -> p a b", b=shape[2])
        elif len(shape) == 4:
            view = view.rearrange("p (a b c) -> p a b c", b=shape[2], c=shape[3])
        self.off = off + n * esz
        self.hi = max(self.hi, self.off)
        t = Tile(view, name)
        self.P.tiles.append(t)
        return t


class Tile:
    def __init__(self, ap, name):
        self.ap = ap
        self.name = name
        self.lw = None
        self.rd = []
        self.dsem = None
        self.dcount = 0
        self.last_dma = None

    def __getitem__(self, k):
        return V(self, self.ap[k])

    def v(self):
        return V(self, self.ap)

    def re(self, pat, **kw):
        return V(self, self.ap.rearrange(pat, **kw))

    @property
    def shape(self):
        return self.ap.shape


class Op:
    __slots__ = ("eng", "fn", "deps", "odeps", "dma", "sig", "cnt", "idx", "n", "kind", "tag", "t0", "t1")


ENGS = ("pe", "act", "dve", "pool", "sp")


class Prog:
    def __init__(self, nc):
        self.nc = nc
        self.ops = []
        self.tiles = []
        self.banks = []
        self.bi = 0

    def sbuf(self, name, shape, dt=F32):
        h = self.nc.alloc_sbuf_tensor("s_" + name, list(shape), dt)
        t = Tile(h.ap(), name)
        self.tiles.append(t)
        return t

    def psum(self, name, shape, dt=F32):
        h = self.nc.alloc_psum_tensor("p_" + name, list(shape), dt)
        t = Tile(h.ap(), name)
        self.tiles.append(t)
        return t

    def dram(self, name, shape, dt=F32, kind="Internal"):
        h = self.nc.dram_tensor(name, list(shape), dt, kind=kind)
        t = Tile(h.ap(), name)
        self.tiles.append(t)
        return t

    def bank(self, pool=None):
        pool = pool or self.cur_pool
        lst = self.pools[pool]
        b = lst[self.pi[pool] % len(lst)]
        self.pi[pool] += 1
        b.gen = getattr(b, "gen", 0) + 1
        return V(b, b.ap, b.gen)

    def handoff(self, old, new):
        S = set()
        for t in old:
            if t.lw is not None:
                S.add(t.lw)
            S.update(t.rd)
        for t in new:
            t.rd = list(S | set(t.rd))

    def barrier(self):
        has_succ = set()
        for op in self.ops:
            has_succ.update(op.deps)
            has_succ.update(op.odeps)
        deps = {op.idx for op in self.ops if op.idx not in has_succ}
        for t in self.tiles:
            if t.last_dma is not None:
                deps.add(t.last_dma)
        if not hasattr(self, "last_bar"):
            self.last_bar = {}
        for e in ENGS:
            op = self.add(e, lambda eng: eng.nop(), [], [], kind="bar")
            op.deps = set(deps)
            self.last_bar[e] = op.idx

    def add(self, eng, fn, reads, writes, dma=None, n=0, kind=""):
        op = Op()
        op.tag = getattr(self, "tag", "")
        op.t0 = op.t1 = 0.0
        op.n = n
        op.kind = kind
        op.odeps = set()
        op.eng = eng
        op.fn = fn
        op.idx = len(self.ops)
        op.dma = dma
        op.sig = dma is not None
        op.cnt = 0
        deps = set()
        for x in list(reads) + list(writes):
            if isinstance(x, V) and x.gen is not None:
                assert x.gen == x.t.gen, "stale PSUM bank use: %s (gen %d, now %d)" % (x.t.name, x.gen, x.t.gen)
        rt = [x.t if isinstance(x, V) else x for x in reads]
        wt = [x.t if isinstance(x, V) else x for x in writes]
        for t in rt:
            if t.lw is not None:
                deps.add(t.lw)
        for t in wt:
            if t.lw is not None:
                deps.add(t.lw)
            deps.update(t.rd)
        for t in rt:
            t.rd.append(op.idx)
        for t in wt:
            t.lw = op.idx
            t.rd = []
        if eng == "pe":
            op.odeps = {d for d in deps if self.ops[d].eng == "pe"}
            deps = deps - op.odeps
        lb = getattr(self, "last_bar", {}).get(eng)
        if lb is not None:
            op.odeps.add(lb)
        op.deps = deps
        if dma is not None:
            dma.dcount += 1
            op.cnt = dma.dcount * 16
            dma.last_dma = op.idx
        self.ops.append(op)
        return op

    @staticmethod
    def _ap(x):
        if isinstance(x, (V, Tile)):
            return x.ap
        return x

    @staticmethod
    def _tl(*xs):
        return [x for x in xs if isinstance(x, (V, Tile))]

    @staticmethod
    def _n(x):
        try:
            return int(Prog._ap(x).free_size())
        except Exception:
            return 128

    def mm(self, out, lhsT, rhs, start=True, stop=True):
        o, l, r = self._ap(out), self._ap(lhsT), self._ap(rhs)
        k = "mm32" if l.dtype == F32 else "mm"
        self.add("pe", lambda e: e.matmul(o, l, r, start=start, stop=stop),
                 self._tl(lhsT, rhs), self._tl(out), n=self._n(rhs), kind=k)

    def tr(self, out, in_, ident):
        o, i, d = self._ap(out), self._ap(in_), self._ap(ident)
        self.add("pe", lambda e: e.transpose(o, i, d), self._tl(in_, ident), self._tl(out), n=128, kind="mm")

    def act(self, out, in_, func, bias=None, scale=None, accum=None):
        o, i = self._ap(out), self._ap(in_)
        kw = {}
        if bias is not None:
            kw["bias"] = self._ap(bias)
        if scale is not None:
            kw["scale"] = self._ap(scale)
        if accum is not None:
            kw["accum_out"] = self._ap(accum)
        self.add("act", lambda e: e.activation(o, i, func, **kw),
                 self._tl(in_, bias, scale), self._tl(out, accum), n=self._n(out), kind="act")

    def tt(self, eng, out, a, b, op):
        o, x, y = self._ap(out), self._ap(a), self._ap(b)
        self.add(eng, lambda e: e.tensor_tensor(o, x, y, op), self._tl(a, b), self._tl(out), n=self._n(out), kind="tt")

    def ts(self, eng, out, a, s1, op0, s2=None, op1=None):
        o, x = self._ap(out), self._ap(a)
        c1, c2 = self._ap(s1), self._ap(s2)
        kw = {}
        if op1 is not None:
            kw["op1"] = op1
        self.add(eng, lambda e: e.tensor_scalar(o, x, c1, c2, op0, **kw),
                 self._tl(a, s1, s2), self._tl(out), n=self._n(out), kind="ts")

    def stt(self, out, a, s, b, op0, op1):
        o, x, c, y = self._ap(out), self._ap(a), self._ap(s), self._ap(b)
        self.add("dve", lambda e: e.scalar_tensor_tensor(o, x, c, y, op0, op1),
                 self._tl(a, s, b), self._tl(out), n=self._n(out), kind="tt")

    def copy(self, eng, out, in_):
        o, i = self._ap(out), self._ap(in_)
        if eng == "act":
            self.add(eng, lambda e: e.copy(o, i), self._tl(in_), self._tl(out), n=self._n(out), kind="act")
        else:
            self.add(eng, lambda e: e.tensor_copy(o, i), self._tl(in_), self._tl(out), n=self._n(out), kind="ts")

    def memset(self, eng, out, val):
        o = self._ap(out)
        self.add(eng, lambda e: e.memset(o, val), [], self._tl(out), n=self._n(out), kind="ts")

    def recip(self, out, in_):
        o, i = self._ap(out), self._ap(in_)
        self.add("dve", lambda e: e.reciprocal(o, i), self._tl(in_), self._tl(out), n=8 * self._n(out), kind="ts")

    def reduce(self, out, in_, op=ALU.add):
        o, i = self._ap(out), self._ap(in_)
        self.add("dve", lambda e: e.tensor_reduce(o, i, AX.X, op), self._tl(in_), self._tl(out), n=self._n(in_), kind="ts")

    def scan(self, out, d0, d1):
        o, a, b = self._ap(out), self._ap(d0), self._ap(d1)
        self.add("dve", lambda e: e.tensor_tensor_scan(o, a, b, 0.0, ALU.mult, ALU.add),
                 self._tl(d0, d1), self._tl(out), n=self._n(out), kind="tt")

    def dma(self, q, out, in_, key=None, **kw):
        o, i = self._ap(out), self._ap(in_)
        if q == "pool":
            kw.setdefault("max_dma_last_dim", 4096)
        kt = key if key is not None else (out.t if isinstance(out, V) else out)
        nb = 1
        for d_ in o.shape:
            nb *= int(d_)
        self.add(q, lambda e: e.dma_start(o, i, **kw), self._tl(in_), self._tl(out), dma=kt, n=nb, kind="dma")

    def generic(self, eng, fn, reads, writes):
        self.add(eng, fn, reads, writes, n=128, kind="ts")

    def _dur(self, op):
        n, k, e = op.n, op.kind, op.eng
        if k == "dma":
            return 1800.0 + n * 0.008
        if k == "bar":
            return 100.0
        if e == "pe":
            if k == "mm32":
                return 4.0 * (60.0 + 0.42 * max(n, 64))
            return 85.0 + 0.42 * max(n - 128, 0)
        if e == "act":
            return 230.0 + 0.8 * n
        if e == "dve":
            return 70.0 + (1.3 if k == "tt" else 1.0) * n
        if e == "pool":
            return 250.0 + 1.6 * n
        return 60.0

    def schedule(self):
        import heapq
        ops = self.ops
        nops = len(ops)
        succ = [[] for _ in range(nops)]
        indeg = [0] * nops
        last_on = {}
        isucc = [[] for _ in range(nops)]
        for op in ops:
            ds = set(op.deps) | set(op.odeps)
            for d in ds:
                succ[d].append(op.idx)
            indeg[op.idx] = len(ds)
            if op.kind in ("dma", "bar"):
                if op.eng in last_on and last_on[op.eng] not in ds:
                    isucc[last_on[op.eng]].append(op.idx)
                    indeg[op.idx] += 1
                last_on[op.eng] = op.idx
        ready = {e: [] for e in ENGS}
        for op in ops:
            if indeg[op.idx] == 0:
                heapq.heappush(ready[op.eng], op.idx)
        free_t = {e: 0.0 for e in ENGS}
        events = []
        order = {e: [] for e in ENGS}
        now = 0.0
        done = 0
        SEM = 60.0
        bw_free = 0.0
        while done < nops:
            for e in ENGS:
                while ready[e] and free_t[e] <= now:
                    i = heapq.heappop(ready[e])
                    op = ops[i]
                    d = self._dur(op)
                    op.t0, op.t1 = now, now + d
                    for s_ in isucc[i]:
                        indeg[s_] -= 1
                        if indeg[s_] == 0:
                            heapq.heappush(ready[ops[s_].eng], s_)
                    if op.kind == "dma":
                        free_t[e] = now + 70.0
                        fin = max(now + 1500.0, bw_free) + op.n * 0.008
                        bw_free = fin
                        op.t1 = fin
                        heapq.heappush(events, (fin, i))
                    else:
                        free_t[e] = now + d
                        heapq.heappush(events, (now + d + SEM, i))
                    order[e].append(i)
            cand = [free_t[e] for e in ENGS if ready[e]]
            if events:
                cand.append(events[0][0])
            assert cand, "scheduler deadlock"
            now = max(now, min(cand))
            while events and events[0][0] <= now:
                t, i = heapq.heappop(events)
                done += 1
                for s_ in succ[i]:
                    indeg[s_] -= 1
                    if indeg[s_] == 0:
                        heapq.heappush(ready[ops[s_].eng], s_)
        assert sum(len(v) for v in order.values()) == nops
        self.sched_busy = {e: sum(self._dur(ops[i]) for i in order[e] if ops[i].kind != "dma") for e in ENGS}
        self.sched_order = order
        self.sched_span = now
        return now


    def emit(self):
        nc = self.nc
        ops = self.ops
        for op in ops:
            for d in op.deps:
                ops[d].sig = True
        esem = {e: nc.alloc_semaphore("es_" + e) for e in ENGS}
        cnt = {e: 0 for e in ENGS}
        if getattr(self, "sched_order", None):
            per = {e: [ops[i] for i in self.sched_order[e]] for e in ENGS}
        else:
            per = {e: [o for o in ops if o.eng == e] for e in ENGS}
        for e in ENGS:
            for op in per[e]:
                if op.dma is not None:
                    if op.dma.dsem is None:
                        op.dma.dsem = nc.alloc_semaphore("ds_" + op.dma.name)
                elif op.sig:
                    cnt[op.eng] += 1
                    op.cnt = cnt[op.eng]

        def comp(op):
            if op.dma is not None:
                return op.dma.dsem, op.cnt
            return esem[op.eng], op.cnt

        dma_tiles = [t for t in self.tiles if t.dsem is not None]

        def run(ename, eng):
            waited = {}
            for op in per[ename]:
                need = {}
                for d in op.deps:
                    s, v = comp(ops[d])
                    k = id(s)
                    if waited.get(k, 0) >= v:
                        continue
                    if k not in need or need[k][1] < v:
                        need[k] = (s, v)
                for k, (s, v) in need.items():
                    eng.wait_ge(s, v)
                    waited[k] = v
                ins = op.fn(eng)
                if op.dma is not None:
                    ins.then_inc(op.dma.dsem, 16)
                elif op.sig:
                    ins.then_inc(esem[ename], 1)
            if ename == "sp":
                for t in dma_tiles:
                    eng.wait_ge(t.dsem, t.dcount * 16)

        with nc.Block() as block:
            @block.tensor
            def _(e):
                run("pe", e)

            @block.scalar
            def _(e):
                run("act", e)

            @block.vector
            def _(e):
                run("dve", e)

            @block.gpsimd
            def _(e):
                run("pool", e)

            @block.sync
            def _(e):
                run("sp", e)
        return {e: len(per[e]) for e in ENGS}, cnt


COLS = {}
_c = 0
for _n, _w in (("gmix", 8), ("gffn", 8), ("gple", 8), ("mu", 14), ("w0", 4), ("a0", 4), ("kk", 4),
               ("ka", 4), ("rk", 4), ("cw0", 12), ("cw1", 12), ("cw2", 12), ("cw3", 12)):
    COLS[_n] = (_c, _w)
    _c += _w
NCOLS = _c
BCS = {}
_c = 0
for _n, _w in (("lnxw", 512), ("lnxb", 512), ("gng", 128), ("fng", 1024), ("alog", 4), ("dtb", 4)):
    BCS[_n] = (_c, _w)
    _c += _w
NBC = _c


def _pack_consts(inp):
    cols = np.zeros((128, NCOLS), np.float32)

    def put(name, vec):
        o, w = COLS[name]
        cols[:, o:o + w] = np.asarray(vec, np.float32).reshape(w, 128).T

    put("gmix", inp["ln_mix_g"][0])
    put("gffn", inp["ln_ffn_g"][0])
    put("gple", inp["ln_ple_g"][0])
    put("mu", inp["mu_shift"][0])
    put("w0", inp["w0"][0])
    put("a0", inp["a0"][0])
    put("kk", inp["k_k"][0])
    put("ka", inp["k_a"][0])
    put("rk", inp["r_k"][0].reshape(-1))
    for j in range(4):
        put("cw%d" % j, inp["conv_w"][0][j])
    bc = np.zeros((128, NBC), np.float32)

    def putb(name, vec):
        o, w = BCS[name]
        bc[:, o:o + w] = np.asarray(vec, np.float32).reshape(1, w)

    putb("lnxw", inp["ln_x_w"][0])
    putb("lnxb", inp["ln_x_b"][0])
    putb("gng", inp["gdn_norm_g"][0])
    putb("fng", inp["final_norm_g"])
    putb("alog", inp["a_log"][0])
    putb("dtb", inp["dt_bias"][0])
    return cols, bc


def build(NA, NB, NS, CP=128, CS=16, dbg=None):
    nc = bass.Bass("TRN2", target_bir_lowering=False)
    P = Prog(nc)
    TA, TB, TS = max(NA, 1) * CP, max(NB, 1) * CP, max(NS, 1) * CS
    NS1 = max(NS, 1)

    def din(name, shape):
        return P.dram(name, shape, F32, kind="ExternalInput")

    def dout(name, shape):
        return P.dram(name, shape, F32, kind="ExternalOutput")

    xa = din("xa", [TA, D])
    xb = din("xb", [TB, D])
    pb = din("pb", [TB, D_PLE])
    xs = din("xs", [TS, D])
    ps = din("ps", [TS, D_PLE])
    st_shift = din("st_shift", [NS1, R_PROJ])
    st_wkv = din("st_wkv", [NS1, 8, 64, 64])
    st_conv = din("st_conv", [NS1, 3, 1536])
    st_gdn = din("st_gdn", [NS1, 4, 128, 128])
    w_in = din("w_in", [D, D_IN])
    w_lup = din("w_lup", [64, 512])
    a_lup = din("a_lup", [64, 512])
    g_lup = din("g_lup", [128, 512])
    w_out = din("w_out", [D, D])
    w_gate = din("w_gate", [D, D_FF])
    w_up = din("w_up", [D, D_FF])
    w_down = din("w_down", [D_FF, D])
    w_pg = din("w_pg", [D, D])
    w_pp = din("w_pp", [D_PLE, D])
    cols_d = din("cols", [128, NCOLS])
    bc_d = din("bc", [128, NBC])

    yb = dout("yb", [TB, D])
    ys = dout("ys", [TS, D])
    o_shift_p = dout("o_shift_p", [1, R_PROJ])
    o_wkv_p = dout("o_wkv_p", [1, 8, 64, 64])
    o_conv_p = dout("o_conv_p", [1, 3, 1536])
    o_gdn_p = dout("o_gdn_p", [1, 4, 128, 128])
    o_shift_s = dout("o_shift_s", [NS1, R_PROJ])
    o_wkv_s = dout("o_wkv_s", [NS1, 8, 64, 64])
    o_conv_s = dout("o_conv_s", [NS1, 3, 1536])
    o_gdn_s = dout("o_gdn_s", [NS1, 4, 128, 128])

    sc_gate = P.dram("sc_gate", [NHC, 128, 8, 128], BF16)
    sc_up = P.dram("sc_up", [NHC, 128, 8, 128], BF16)
    NG = NHC // 2
    sc_down = P.dram("sc_down", [NG, 128, 2, D], BF16)
    sc_out = P.dram("sc_out", [4, 128, 2, D], BF16)
    sc_pg = P.dram("sc_pg", [4, 128, 2, D], BF16)
    sc_pp = P.dram("sc_pp", [1, 128, 2, D], BF16)

    P.banks = [P.psum("bank%d" % i, [128, 512], F32) for i in range(8)]
    import os
    _pl = [int(v) for v in os.environ.get("POOLS", "1,3,4").split(",")]
    P.pools = {"f": P.banks[0:_pl[0]], "c": P.banks[_pl[0]:_pl[0] + _pl[1]], "m": P.banks[_pl[0] + _pl[1]:8]}
    P.pi = {"c": 0, "m": 0, "f": 0}
    SEQ_STREAM = len(P.pools["m"]) < 4
    XQ = os.environ.get("XQ", "act")
    P.cur_pool = "c"

    win = P.sbuf("win", [128, 8, D_IN], BF16)
    lor = P.sbuf("lor", [128, 512], BF16)
    gup = P.sbuf("gup", [128, 512], BF16)
    cols = P.sbuf("cols", [128, NCOLS], F32)
    bcs = P.sbuf("bcs", [128, NBC], F32)
    der = P.sbuf("der", [128, 32], F32)
    identb = P.sbuf("identb", [128, 128], BF16)
    identf = P.sbuf("identf", [128, 128], F32)
    mTs = P.sbuf("mTs", [128, 128], F32)
    mTi = P.sbuf("mTi", [128, 128], F32)
    mLs = P.sbuf("mLs", [128, 128], F32)
    ones = P.sbuf("ones", [128, 128], F32)
    blk = P.sbuf("blk", [128, 2], F32)
    pw = P.sbuf("pw", [128, 16], F32)
    nea = P.sbuf("nea", [128, 4], F32)

    def col(name, i=None, n=None):
        o, w = COLS[name]
        if i is None:
            return cols[:, o:o + w]
        return cols[:, o + i:o + i + (n or 1)]

    def bcv(name):
        o, w = BCS[name]
        return bcs[:, o:o + w]

    for i in range(8):
        P.dma("pool", win[:, i, :], w_in[i * 128:(i + 1) * 128, :])
    P.dma("pool", lor[0:64, :], w_lup.v())
    P.dma("pool", lor[64:128, :], a_lup.v())
    P.dma("pool", gup.v(), g_lup.v())
    P.dma("sp", cols.v(), cols_d.v())
    P.dma("sp", bcs.v(), bc_d.v())
    for g in range(4):
        P.dma("pool", sc_out[g], w_out[g * 256:(g + 1) * 256, :].re("(c p) n -> p c n", p=128), key=sc_out)
    for g in range(NHC):
        P.dma("pool", sc_gate[g], w_gate[:, g * 128:(g + 1) * 128].re("(kc p) n -> p kc n", p=128), key=sc_gate)
        P.dma("pool", sc_up[g], w_up[:, g * 128:(g + 1) * 128].re("(kc p) n -> p kc n", p=128), key=sc_up)
    for g in range(NG):
        P.dma("pool", sc_down[g], w_down[g * 256:(g + 1) * 256, :].re("(c p) n -> p c n", p=128), key=sc_down)
    for g in range(4):
        P.dma("pool", sc_pg[g], w_pg[g * 256:(g + 1) * 256, :].re("(c p) n -> p c n", p=128), key=sc_pg)
    P.dma("pool", sc_pp[0], w_pp.re("(c p) n -> p c n", p=128), key=sc_pp)

    def aff(tile, pattern, cmp, cm):
        a = tile.ap
        P.memset("pool", tile.v(), 1.0)
        P.generic("pool", lambda e: e.affine_select(out=a, in_=a, pattern=pattern, compare_op=cmp, fill=0.0,
                                                     base=0, channel_multiplier=cm), [tile], [tile])

    bd32 = P.sbuf("bd32", [128, 128], BF16)
    off32 = P.sbuf("off32", [128, 128], BF16)
    off64 = P.sbuf("off64", [128, 128], BF16)

    def blockdiag(tile, bs):
        a = tile.ap.rearrange("p (b c) -> p b c", c=bs)
        nb = 128 // bs
        P.memset("pool", tile.v(), 1.0)
        P.generic("pool", lambda e: e.affine_select(out=a, in_=a, pattern=[[-bs, nb], [0, bs]], compare_op=ALU.is_ge,
                                                     fill=0.0, base=0, channel_multiplier=1), [tile], [tile])
        P.generic("pool", lambda e: e.affine_select(out=a, in_=a, pattern=[[bs, nb], [0, bs]], compare_op=ALU.is_ge,
                                                     fill=0.0, base=bs - 1, channel_multiplier=-1), [tile], [tile])

    blockdiag(bd32, 32)
    blockdiag(off64, 64)
    P.tt("pool", off32.v(), off64.v(), bd32.v(), ALU.subtract)
    P.ts("pool", off64.v(), off64.v(), -1.0, ALU.mult, 1.0, ALU.add)
    aff(identb, [[-1, 128]], ALU.is_equal, 1)
    aff(identf, [[-1, 128]], ALU.is_equal, 1)
    aff(mTs, [[1, 128]], ALU.is_gt, -1)
    aff(mTi, [[1, 128]], ALU.is_ge, -1)
    aff(mLs, [[-1, 128]], ALU.is_gt, 1)
    P.memset("dve", ones.v(), 1.0)
    P.memset("dve", blk.v(), 0.0)
    P.memset("dve", blk[0:64, 0:1], 1.0)
    P.memset("dve", blk[64:128, 1:2], 1.0)
    P.memset("dve", pw[:, 0:8], -0.5)
    P.memset("dve", pw[:, 8:16], 0.5)
    P.ts("dve", der[:, 0:14], col("mu"), -1.0, ALU.mult, 1.0, ALU.add)
    P.ts("dve", der[:, 14:18], col("w0"), 0.5, ALU.mult)
    P.ts("dve", der[:, 18:22], col("a0"), 0.5, ALU.mult)
    P.act(nea.v(), bcv("alog"), AF.Exp)
    P.ts("dve", nea.v(), nea.v(), -1.0, ALU.mult)

    A = P.sbuf("A", [128, 4, 128], F32)
    Abf = P.sbuf("Abf", [128, 4, 128], BF16)
    S = P.sbuf("S", [128, 4, 128], F32)
    Sbf = P.sbuf("Sbf", [128, 4, 128], BF16)
    fcar = P.sbuf("fcar", [128, 14, 1], F32)
    qcar = P.sbuf("qcar", [128, 12, 3], F32)

    NSL = 2
    gsl = [P.sbuf("gsl%d" % i, [128, 8, 128], BF16) for i in range(NSL)]
    usl = [P.sbuf("usl%d" % i, [128, 8, 128], BF16) for i in range(NSL)]
    dsl = [P.sbuf("dsl%d" % i, [128, 2, D], BF16) for i in range(NSL)]
    dsi = [0]
    TM = 2 * CP
    NXQ = int(os.environ.get("NXQ", "3"))
    xq = [P.sbuf("xq%d" % i, [128, D], F32) for i in range(NXQ)]
    mixTm = P.sbuf("mixTm", [128, 8, TM], BF16)
    h2T = mixTm
    actT = P.sbuf("actT", [128, NHC, TM], BF16)
    sgb = [P.sbuf("sgb%d" % i, [128, TM], BF16) for i in range(2)]

    stats = {"c": [P.sbuf("st%d" % i, [128, 16], F32) for i in range(6)],
             "m": [P.sbuf("stm%d" % i, [128, 16], F32) for i in range(4)]}
    sti = {"c": 0, "m": 0}

    def stat():
        k = "m" if P.cur_pool == "m" else "c"
        sti[k] += 1
        return stats[k][sti[k] % len(stats[k])]

    xn_c = P.sbuf("xn", [128, D], BF16)
    hT = P.sbuf("hT", [128, 8, CP], BF16)
    sioA = P.sbuf("sioA", [128, 128], F32)
    sioB = P.sbuf("sioB", [128, 128], F32)
    sioW = P.sbuf("sioW", [128, 512], F32)
    sioC = P.sbuf("sioC", [128, 36], F32)
    mpt = P.sbuf("mpt", [128, D_PLE], F32)
    mpbf = P.sbuf("mpbf", [128, D_PLE], BF16)

    rem = nc.sbuf_bytes_remaining
    arena = Arena(P, "ctx", (rem - 256) // 64 * 64)
    print("arena bytes", arena.nbytes)

    xn_m = actT.re("p a b -> p (a b)")[:, 4096 + 2 * TM:4096 + 2 * TM + D]

    def norm_T(src, Cc, dstT, toff, gname, pool=None):
        st = stat()
        xn = xn_c if P.cur_pool != "m" else xn_m
        P.act(xn[:Cc, :], src, AF.Square, accum=st[:Cc, 0:1])
        P.ts("dve", st[:Cc, 1:2], st[:Cc, 0:1], 1.0 / D, ALU.mult, 1e-6, ALU.add)
        P.tt("pool", st[:Cc, 2:3], st[:Cc, 1:2], pw[:Cc, 0:1], ALU.pow)
        P.ts("dve", xn[:Cc, :], src, st[:Cc, 2:3], ALU.mult)
        bk = P.bank(pool)
        bb = bk.v().bitcast(BF16)
        for kc in range(8):
            P.tr(bb[:, kc * Cc:(kc + 1) * Cc], xn[:Cc, kc * 128:(kc + 1) * 128], identb[:Cc, :Cc])
        P.tt("dve", dstT[:, :, toff:toff + Cc], bb[:, 0:8 * Cc].re("p (k c) -> p k c", c=Cc),
             col(gname).re("p (k o) -> p k o", o=1).bc([128, 8, Cc]), ALU.mult)

    class Ctx:
        pass

    def make_ctx(Cc, tag):
        c = Ctx()
        c.C = Cc
        arena.off = 0
        s = lambda n, sh, dt=F32: arena.take(n + tag, sh, dt)
        c.fb = s("fb", [128, 14, Cc + 1])
        c.qb = s("qb", [128, 12, Cc + 3])
        c.lw = s("lw", [128, Cc], BF16)
        c.sg = s("sg", [128, Cc], BF16)
        c.cum = s("cum", [128, 4, Cc + 1])
        c.sm = s("sm", [128, 16])
        c.art = s("art", [128, 4, 2, Cc], BF16)
        c.br = s("br", [128, 4, Cc], BF16)
        c.kt = s("kt", [128, 4, Cc], BF16)
        c.Bhf = s("Bhf", [128, 4, Cc], BF16)
        c.Khf = s("Khf", [128, 4, Cc], BF16)
        c.vbf = s("vbf", [128, 4, Cc], BF16)
        c.kg = s("kg", [128, 4, Cc], BF16)
        c.qg = s("qg", [128, 4, Cc], BF16)
        c.kq = s("kq", [128, 4, 2, Cc], BF16)
        c.vgb = s("vgb", [128, 4, Cc], BF16)
        c.BK = s("BK", [Cc, 1024], BF16)
        c.Vt = s("Vt", [Cc, 512], BF16)
        c.Vg = s("Vg", [Cc, 512], BF16)
        c.kd = s("kd", [Cc, 512], BF16)
        c.tk = s("tk", [Cc, 80])
        c.eGl = s("eGl", [128, 4])
        c.zs = s("zs", [Cc, 512], BF16)
        c.gt = s("gt", [Cc, 512], BF16)
        off0 = arena.off
        c.fm = s("fm", [128, 14, Cc])
        c.t2 = s("t2", [128, 14, Cc])
        c.acc = s("acc", [128, 12, Cc])
        c.tw = s("tw", [128, 4, Cc])
        c.ta = s("ta", [128, 4, Cc])
        c.Wa = s("Wa", [128, 4, Cc])
        c.Wb = s("Wb", [128, 4, Cc])
        offZ = arena.off
        c.rhsG = s("rhsG", [Cc, 4, Cc])
        c.decT = s("decT", [Cc, 4, Cc])
        c.EG = s("EG", [128, 4, Cc])
        c.zset = [c.rhsG, c.decT, c.EG]
        arena.off = offZ
        c.kk = s("kk", [128, 4, Cc])
        c.ka = s("ka", [128, 4, Cc])
        c.keff = s("keff", [128, 4, Cc])
        c.alpha = [c.fm, c.t2, c.acc, c.tw, c.ta, c.Wa, c.Wb, c.kk, c.ka, c.keff]
        c.kset = [c.kk, c.ka, c.keff]
        arena.off = off0
        c.MakT = s("MakT", [Cc, 8, Cc], BF16)
        c.NrbT = s("NrbT", [Cc, 8, Cc], BF16)
        c.NrkT = s("NrkT", [Cc, 8, Cc], BF16)
        c.QKT = s("QKT", [Cc, 4, Cc], BF16)
        c.gh = max(1, min(12, 512 // Cc))
        c.ngr = (12 + c.gh - 1) // c.gh
        c.Pm = [s("Pm%d" % g, [Cc, c.gh, Cc], BF16) for g in range(c.ngr)]
        c.PT = [s("PT%d" % g, [Cc, c.gh, Cc], BF16) for g in range(c.ngr)]
        c.Rm = [s("Rm%d" % g, [Cc, c.gh, Cc], BF16) for g in range(c.ngr)]
        offC = arena.off
        c.Zb = s("Zb", [Cc, 512], BF16)
        c.Rh = s("Rh", [Cc, 512], BF16)
        c.Ub = s("Ub", [Cc, 512], BF16)
        c.Xb = s("Xb", [Cc, 512], BF16)
        c.mix = s("mix", [Cc, D], BF16)
        offE = arena.off
        c.chainset = [c.Zb, c.Rh, c.Ub, c.Xb, c.mix]
        c.moffset = []
        if Cc > 32:
            arena.off = offC
            c.Mo32 = [s("Mo32_%d" % g, [Cc, c.gh, Cc], BF16) for g in range(c.ngr)]
            c.Mo64 = [s("Mo64_%d" % g, [Cc, c.gh, Cc], BF16) for g in range(c.ngr)]
            c.moffset = c.Mo32 + c.Mo64
            assert arena.off <= offE
            arena.off = offE
        c.w5 = [s("w5_%d" % i, [Cc, 512]) for i in range(3)]
        c.beta = [c.MakT, c.NrbT, c.NrkT, c.QKT, c.Zb, c.Rh, c.Ub, c.Xb, c.mix] + c.Pm + c.PT + c.Rm + c.w5 + c.moffset
        return c

    def seq_init_zero(c):
        P.memset("pool", A.v(), 0.0)
        P.memset("pool", Abf.v(), 0.0)
        P.memset("pool", S.v(), 0.0)
        P.memset("pool", Sbf.v(), 0.0)
        P.memset("pool", fcar.v(), 0.0)
        P.memset("pool", qcar.v(), 0.0)
        P.memset("pool", c.cum[:, :, 0:1], 0.0)

    def chunk(c, xt, full, mixT_dst=None):
        Cc = c.C
        fb, qb = c.fb, c.qb
        tk = c.tk
        P.handoff(c.beta + c.zset, c.alpha)

        norm_T(xt[:Cc, :], Cc, hT, 0, "gmix", "f")

        P.copy("pool", fb[:, :, 0:1], fcar.v())
        P.copy("pool", qb[:, :, 0:3], qcar.v())

        def proj(tiles, dst, doff):
            for g0 in range(0, len(tiles), 4):
                grp = tiles[g0:g0 + 4]
                bk = P.bank("f")
                for i, t in enumerate(grp):
                    for kc in range(8):
                        P.mm(bk[:, i * Cc:(i + 1) * Cc], win[:, kc, t * 128:(t + 1) * 128], hT[:, kc, :Cc],
                             start=(kc == 0), stop=(kc == 7))
                n = len(grp)
                t0 = grp[0] - tiles[0]
                P.copy("act", dst[:, t0:t0 + n, doff:doff + Cc],
                       bk[:, 0:n * Cc].re("p (t c) -> p t c", c=Cc))

        proj(list(range(0, 14)), fb, 1)
        proj(list(range(14, 26)), qb, 3)
        bkb = P.bank("f")
        for kc in range(8):
            P.mm(bkb[:Cc, 0:8], hT[:, kc, :Cc], win[:, kc, OFF_B:OFF_B + 8], start=(kc == 0), stop=(kc == 7))
        P.copy("act", tk[:, 64:72], bkb[:Cc, 0:8])
        if full:
            bkz = P.bank("f")
            for kc in range(8):
                P.mm(bkz[:Cc, 0:512], hT[:, kc, :Cc], win[:, kc, OFF_Z:OFF_Z + 512], start=(kc == 0), stop=(kc == 7))
            P.act(c.zs.v(), bkz[:Cc, 0:512], AF.Silu)

        mu_bc = col("mu").re("p (t o) -> p t o", o=1).bc([128, 14, Cc])
        omu_bc = der[:, 0:14].re("p (t o) -> p t o", o=1).bc([128, 14, Cc])
        P.tt("pool", c.fm.v(), fb[:, :, 0:Cc], mu_bc, ALU.mult)
        P.tt("pool", c.t2.v(), fb[:, :, 1:Cc + 1], omu_bc, ALU.mult)
        P.tt("dve", c.fm.v(), c.fm.v(), c.t2.v(), ALU.add)
        P.copy("pool", fcar.v(), fb[:, :, Cc:Cc + 1])
        fm = c.fm
        r_, k_, v_ = fm[:, 0:4, :], fm[:, 4:8, :], fm[:, 8:12, :]

        def cwbc(j):
            return col("cw%d" % j).re("p (t o) -> p t o", o=1).bc([128, 12, Cc])
        tmp = c.t2[:, 0:12, :]
        P.tt("pool", c.acc.v(), qb[:, :, 0:Cc], cwbc(0), ALU.mult)
        for j in range(1, 4):
            P.tt("pool", tmp, qb[:, :, j:Cc + j], cwbc(j), ALU.mult)
            P.tt("dve", c.acc.v(), c.acc.v(), tmp, ALU.add)
        P.copy("pool", qcar.v(), qb[:, :, Cc:Cc + 3])

        qs = c.acc
        P.act(qs.v(), c.acc.v(), AF.Silu)
        P.act(c.lw[0:64, :], fm[0:64, 12, :], AF.Tanh)
        P.copy("pool", c.lw[64:128, :], fm[64:128, 12, :])
        if full:
            P.act(c.tw[:, 0, :], fm[:, 13, :], AF.Tanh, scale=0.5)
            P.ts("dve", c.sg.v(), c.tw[:, 0, :], 0.5, ALU.mult, 0.5, ALU.add)
        bkw, bka = P.bank(), P.bank()
        for t in range(4):
            P.mm(bkw[:, t * Cc:(t + 1) * Cc], lor[0:64, t * 128:(t + 1) * 128], c.lw[0:64, :])
        for t in range(4):
            P.mm(bka[:, t * Cc:(t + 1) * Cc], lor[64:128, t * 128:(t + 1) * 128], c.lw[64:128, :])
        if full:
            bkg = P.bank()
            P.mm(bkg[:Cc, 0:512], c.sg.v(), gup.v())
            P.copy("act", c.gt.v(), bkg[:Cc, 0:512])
        for t in range(4):
            P.act(c.tw[:, t, :], bkw[:, t * Cc:(t + 1) * Cc], AF.Tanh, bias=der[:, 14 + t:15 + t], scale=0.5)
        for t in range(4):
            P.act(c.ta[:, t, :], bka[:, t * Cc:(t + 1) * Cc], AF.Tanh, bias=der[:, 18 + t:19 + t], scale=0.5)
        P.act(tk[:, 0:4], tk[:, 64:68], AF.Tanh, scale=0.5)
        P.ts("dve", tk[:, 0:4], tk[:, 0:4], 0.5, ALU.mult, 0.5, ALU.add)
        P.tt("dve", tk[:, 4:8], tk[:, 68:72], bcv("dtb")[:Cc, :], ALU.add)

        P.ts("dve", c.tw.v(), c.tw.v(), 1.0, ALU.add)
        for t in range(4):
            P.scan(c.cum[:, t, 1:Cc + 1], ones[:, 0:Cc], c.tw[:, t, :])
        P.ts("dve", c.ta.v(), c.ta.v(), 0.5, ALU.mult, 0.5, ALU.add)
        a_ = c.ta
        P.ts("dve", c.sm[:, 0:4], c.cum[:, :, Cc:Cc + 1].re("p t o -> p (t o)"), -C0, ALU.mult)
        P.act(c.sm[:, 4:8], c.sm[:, 0:4], AF.Exp)
        P.act(tk[:, 8:12], tk[:, 4:8], AF.Exp)
        sp_ = stat()
        u_, L_, t_, m_ = tk[:, 8:12], sp_[:Cc, 0:4], sp_[:Cc, 4:8], sp_[:Cc, 8:12]
        P.act(L_, u_, AF.Ln, bias=1.0)
        P.ts("dve", t_, u_, -0.2, ALU.mult, 0.25, ALU.add)
        for cst in (1.0 / 3.0, 0.5, 1.0):
            P.tt("dve", t_, t_, u_, ALU.mult)
            P.ts("dve", t_, t_, -1.0, ALU.mult, cst, ALU.add)
        P.tt("dve", t_, t_, u_, ALU.mult)
        P.ts("dve", m_, u_, 0.1, ALU.is_lt)
        P.tt("dve", t_, t_, L_, ALU.subtract)
        P.tt("dve", t_, t_, m_, ALU.mult)
        P.tt("dve", tk[:, 8:12], L_, t_, ALU.add)
        P.tt("dve", tk[:, 12:16], tk[:, 8:12], nea[:Cc, :], ALU.mult)
        glog = tk[:, 12:16]

        def cbc(name):
            return col(name).re("p (t o) -> p t o", o=1).bc([128, 4, Cc])
        e1 = c.tw
        P.tt("pool", c.kk.v(), k_, cbc("kk"), ALU.mult)
        P.tt("pool", c.ka.v(), c.kk.v(), a_.v(), ALU.mult)
        for t in range(4):
            P.ts("pool", e1[:, t, :], a_[:, t, :], -1.0, ALU.add, col("ka", t), ALU.mult)
        P.stt(c.keff.v(), e1.v(), 1.0, k_, ALU.add, ALU.mult)
        P.act(c.Wa.v(), c.cum[:, :, 0:Cc], AF.Exp, scale=-C0)
        P.tt("dve", c.art[:, :, 0, :], c.kk.v(), c.Wa.v(), ALU.mult)
        P.act(c.Wb.v(), c.cum[:, :, 1:Cc + 1], AF.Exp, scale=C0)
        P.stt(c.br.v(), c.ka.v(), -1.0, c.Wb.v(), ALU.mult, ALU.mult)
        P.tt("dve", c.kt.v(), c.keff.v(), c.Wb.v(), ALU.mult)
        for t in range(4):
            P.act(c.Wa[:, t, :], c.cum[:, t, 1:Cc + 1], AF.Exp, bias=c.sm[:, t:t + 1], scale=C0)
        P.stt(c.Bhf.v(), c.ka.v(), -1.0, c.Wa.v(), ALU.mult, ALU.mult)
        P.tt("pool", c.Khf.v(), c.keff.v(), c.Wa.v(), ALU.mult)
        if full:
            P.act(c.Wb.v(), c.cum[:, :, 1:Cc + 1], AF.Exp, scale=-C0)
            P.tt("dve", c.art[:, :, 1, :], r_, c.Wb.v(), ALU.mult)
        P.copy("pool", c.vbf.v(), v_)
        P.tt("pool", c.kk.v(), c.kk.v(), c.kk.v(), ALU.mult)
        bks = P.bank()
        for t in range(4):
            P.mm(bks[:Cc, 2 * t:2 * t + 2], c.kk[:, t, :], blk.v())
        if full:
            P.tt("pool", e1.v(), r_, cbc("rk"), ALU.mult)
            P.tt("pool", e1.v(), e1.v(), c.keff.v(), ALU.mult)
            for t in range(4):
                P.mm(bks[:Cc, 8 + 2 * t:10 + 2 * t], e1[:, t, :], blk.v())
        sq8 = c.t2[:, 0:8, :]
        P.tt("pool", sq8, qs[:, 0:8, :], qs[:, 0:8, :], ALU.mult)
        for t in range(8):
            P.mm(bks[:Cc, 16 + t:17 + t], c.t2[:, t, :], ones[:, 0:1])
        P.mm(bks[:Cc, 24:28], mLs[:Cc, :Cc], glog)
        P.mm(bks[:, 28:32], ones[:Cc, :], glog)
        P.ts("dve", tk[:, 16:24], bks[:Cc, 0:8], 1e-12, ALU.add)
        P.recip(tk[:, 16:24], tk[:, 16:24])
        s2 = tk[:, 16:24]
        if full:
            P.copy("dve", tk[:, 24:32], bks[:Cc, 8:16])
        P.ts("dve", tk[:, 32:40], bks[:Cc, 16:24], 1e-6, ALU.add)
        P.tt("pool", tk[:, 40:48], tk[:, 32:40], pw[:Cc, 0:8], ALU.pow)
        P.tt("pool", tk[:, 48:52], tk[:, 36:40], pw[:Cc, 8:12], ALU.pow)
        invs = tk[:, 48:52]
        P.recip(tk[:, 52:56], tk[:, 36:40])
        P.tt("dve", tk[:, 52:56], tk[:, 52:56], tk[:, 0:4], ALU.mult)
        c1 = tk[:, 52:56]
        P.ts("dve", tk[:, 56:60], tk[:, 52:56], -1.0, ALU.mult)
        nc1 = tk[:, 56:60]
        P.ts("dve", tk[:, 60:64], tk[:, 40:44], 128.0 ** -0.5, ALU.mult)
        sqn = tk[:, 60:64]
        st = stat()
        P.act(st[:Cc, 0:4], bks[:Cc, 24:28], AF.Exp)
        eGrem = st[:Cc, 0:4]
        P.act(c.eGl.v(), bks[:, 28:32], AF.Exp)

        P.handoff(c.kset, c.zset)
        P.tt("dve", c.rhsG.v(), mTi[:Cc, :Cc].re("p (o c) -> p o c", o=1).bc([Cc, 4, Cc]),
             glog.re("p (h o) -> p h o", o=1).bc([Cc, 4, Cc]), ALU.mult)
        rg = c.rhsG.re("p h c -> p (h c)")
        bkG, bkB = P.bank(), P.bank()
        P.mm(bkG[:Cc, 0:4 * Cc], mLs[:Cc, :Cc], rg)
        P.mm(bkB[:, 0:4 * Cc], ones[:Cc, :], rg)
        P.act(c.decT.re("p h c -> p (h c)"), bkG[:Cc, 0:4 * Cc], AF.Exp)
        P.act(c.EG.re("p h c -> p (h c)"), bkB[:, 0:4 * Cc], AF.Exp)
        dms, dmi = c.rhsG, c.decT
        P.tt("pool", dms.v(), c.decT.v(), mTs[:Cc, :Cc].re("p (o c) -> p o c", o=1).bc([Cc, 4, Cc]), ALU.mult)
        if full:
            P.tt("pool", dmi.v(), c.decT.v(), mTi[:Cc, :Cc].re("p (o c) -> p o c", o=1).bc([Cc, 4, Cc]), ALU.mult)
        P.tt("dve", c.kg.v(), qs[:, 4:8, :], c.EG.v(), ALU.mult)
        if full:
            P.tt("dve", c.qg.v(), qs[:, 0:4, :], c.EG.v(), ALU.mult)
            P.copy("pool", c.kq[:, :, 1, :], qs[:, 0:4, :])
        P.copy("pool", c.kq[:, :, 0, :], qs[:, 4:8, :])
        P.copy("pool", c.vgb.v(), qs[:, 8:12, :])

        bt1 = P.bank()
        b1 = bt1.v().bitcast(BF16)
        for t in range(4):
            P.tr(b1[:Cc, t * 128:(t + 1) * 128], c.Bhf[:, t, :], identb.v())
        for t in range(4):
            P.tr(b1[:Cc, 512 + t * 128:512 + (t + 1) * 128], c.Khf[:, t, :], identb.v())
        P.copy("act", c.BK.v(), b1[:Cc, 0:1024])
        bt2 = P.bank()
        b2 = bt2.v().bitcast(BF16)
        for t in range(4):
            P.tr(b2[:Cc, t * 128:(t + 1) * 128], c.vbf[:, t, :], identb.v())
        for t in range(4):
            P.tr(b2[:Cc, 512 + t * 128:512 + (t + 1) * 128], c.vgb[:, t, :], identb.v())
        P.copy("act", c.Vt.v(), b2[:Cc, 0:512])
        P.copy("act", c.Vg.v(), b2[:Cc, 512:1024])
        bt3 = P.bank()
        b3 = bt3.v().bitcast(BF16)
        for t in range(4):
            P.tr(b3[:Cc, t * 128:(t + 1) * 128], c.kq[:, t, 0, :], identb.v())
        for h in range(4):
            P.ts("dve", c.kd[:, h * 128:(h + 1) * 128], b3[:Cc, h * 128:(h + 1) * 128], eGrem[:, h:h + 1], ALU.mult)

        P.handoff(c.alpha, c.beta)
        NW = 2 if full else 1
        P0 = c.Pm

        def hsl(tiles, h):
            return tiles[h // c.gh][:, h % c.gh, :]

        for h in range(8):
            t, hp = h // 2, (h % 2) * 64
            bk = P.bank()
            rhs = c.art[hp:hp + 64, t, 0:NW, :].re("p w c -> p (w c)")
            P.mm(bk[:Cc, 0:NW * Cc], c.br[hp:hp + 64, t, :], rhs)
            P.mm(bk[:Cc, 256:256 + NW * Cc], c.kt[hp:hp + 64, t, :], rhs)
            P.stt(hsl(P0, h), bk[:Cc, 0:Cc], s2[:, h:h + 1], mTs[:Cc, :Cc], ALU.mult, ALU.mult)
            P.tt("dve", c.MakT[:, h, :], bk[:Cc, 256:256 + Cc], mTs[:Cc, :Cc], ALU.mult)
            if full:
                P.tt("dve", c.NrbT[:, h, :], bk[:Cc, Cc:2 * Cc], mTi[:Cc, :Cc], ALU.mult)
                P.tt("dve", c.NrkT[:, h, :], bk[:Cc, 256 + Cc:256 + 2 * Cc], mTi[:Cc, :Cc], ALU.mult)
        for h in range(4):
            if h % 2 == 0:
                bk = P.bank()
            o = (h % 2) * 256
            P.mm(bk[:Cc, o:o + NW * Cc], c.kq[:, h, 0, :], c.kq[:, h, 0:NW, :].re("p w c -> p (w c)"))
            P.stt(hsl(P0, 8 + h), bk[:Cc, o:o + Cc], nc1[:, h:h + 1], dms[:, h, :], ALU.mult, ALU.mult)
            if full:
                P.tt("dve", c.QKT[:, h, :], bk[:Cc, o + Cc:o + 2 * Cc], dmi[:, h, :], ALU.mult)

        gh, ngr = c.gh, c.ngr
        heads = [list(range(g * gh, min(12, (g + 1) * gh))) for g in range(ngr)]
        hier = Cc > 32

        def hb(m, n):
            return m[:Cc, :Cc].re("p (o c) -> p o c", o=1).bc([Cc, n, Cc])

        def transp(src, dst, g):
            bk = P.bank()
            bb = bk.v().bitcast(BF16)
            n = len(heads[g])
            for i in range(n):
                P.tr(bb[:Cc, i * Cc:(i + 1) * Cc], src[g][:, i, :], identb[:Cc, :Cc])
            P.copy("act", dst[g][:, 0:n, :], bb[:Cc, 0:n * Cc].re("p (h c) -> p h c", c=Cc))

        if hier:
            P.handoff(c.chainset, c.moffset)
        for g in range(ngr):
            n = len(heads[g])
            transp(c.Pm, c.PT, g)
            if hier:
                P.tt("pool", c.Mo32[g][:, 0:n, :], c.PT[g][:, 0:n, :], hb(off32, n), ALU.mult)
                P.tt("pool", c.Mo64[g][:, 0:n, :], c.PT[g][:, 0:n, :], hb(off64, n), ALU.mult)
                P.tt("pool", c.PT[g][:, 0:n, :], c.PT[g][:, 0:n, :], hb(bd32, n), ALU.mult)
                P.tt("dve", c.Pm[g][:, 0:n, :], c.Pm[g][:, 0:n, :], hb(bd32, n), ALU.mult)
            P.tt("pool", c.Rm[g][:, 0:n, :], c.Pm[g][:, 0:n, :], hb(identb, n), ALU.add)
        L = int(np.log2(min(Cc, 32))) - 1
        for lv in range(1, L + 1):
            lastl = lv == L
            for g in range(ngr):
                n = len(heads[g])
                if not lastl:
                    bkP = P.bank()
                    for i in range(n):
                        P.mm(bkP[:Cc, i * Cc:(i + 1) * Cc], c.PT[g][:, i, :], c.Pm[g][:, i, :])
                bkT = P.bank()
                for i in range(n):
                    P.mm(bkT[:Cc, i * Cc:(i + 1) * Cc], c.Pm[g][:, i, :], c.PT[g][:, i, :])
                P.copy("act", c.PT[g][:, 0:n, :], bkT[:Cc, 0:n * Cc].re("p (h c) -> p h c", c=Cc))
                if not lastl:
                    P.copy("act", c.Pm[g][:, 0:n, :], bkP[:Cc, 0:n * Cc].re("p (h c) -> p h c", c=Cc))
                bkR = P.bank()
                for i in range(n):
                    P.mm(bkR[:Cc, i * Cc:(i + 1) * Cc], c.PT[g][:, i, :], c.Rm[g][:, i, :])
                P.tt("dve", c.Rm[g][:, 0:n, :], bkR[:Cc, 0:n * Cc].re("p (h c) -> p h c", c=Cc),
                     c.Rm[g][:, 0:n, :], ALU.add)
        if hier:
            for Mo in (c.Mo32, c.Mo64):
                for g in range(ngr):
                    n = len(heads[g])
                    transp(c.Rm, c.PT, g)
                    bkX = P.bank()
                    for i in range(n):
                        P.mm(bkX[:Cc, i * Cc:(i + 1) * Cc], Mo[g][:, i, :], c.Rm[g][:, i, :])
                    P.copy("act", c.Pm[g][:, 0:n, :], bkX[:Cc, 0:n * Cc].re("p (h c) -> p h c", c=Cc))
                    bkY_ = P.bank()
                    for i in range(n):
                        P.mm(bkY_[:Cc, i * Cc:(i + 1) * Cc], c.PT[g][:, i, :], c.Pm[g][:, i, :])
                    P.tt("dve", c.Rm[g][:, 0:n, :], bkY_[:Cc, 0:n * Cc].re("p (h c) -> p h c", c=Cc),
                         c.Rm[g][:, 0:n, :], ALU.add)
            P.handoff(c.moffset, c.chainset)
        Tt = c.Rm

        bkZ, bkK = P.bank(), P.bank()
        for h in range(8):
            t, hp = h // 2, (h % 2) * 64
            P.mm(bkZ[:Cc, h * 64:(h + 1) * 64], c.art[hp:hp + 64, t, 0, :], Abf[hp:hp + 64, t, hp:hp + 64],
                 start=True, stop=False)
            P.mm(bkZ[:Cc, h * 64:(h + 1) * 64], c.MakT[:, h, :], c.Vt[:, h * 64:(h + 1) * 64],
                 start=False, stop=True)
        for h in range(4):
            P.mm(bkK[:Cc, h * 128:(h + 1) * 128], c.kg[:, h, :], Sbf[:, h, :])
        P.copy("act", c.Zb.v(), bkZ[:Cc, 0:512])
        for h in range(4):
            P.stt(c.Rh[:, h * 128:(h + 1) * 128], c.Vg[:, h * 128:(h + 1) * 128], invs[:, h:h + 1],
                  bkK[:Cc, h * 128:(h + 1) * 128], ALU.mult, ALU.subtract)
        bkU, bkY = P.bank(), P.bank()
        for h in range(8):
            P.mm(bkU[:Cc, h * 64:(h + 1) * 64], hsl(Tt, h), c.Zb[:, h * 64:(h + 1) * 64])
        for h in range(4):
            P.mm(bkY[:Cc, h * 128:(h + 1) * 128], hsl(Tt, 8 + h), c.Rh[:, h * 128:(h + 1) * 128])
        P.tt("dve", c.Ub.re("p (h v) -> p h v", v=64), bkU[:Cc, 0:512].re("p (h v) -> p h v", v=64),
             s2.re("p (h o) -> p h o", o=1).bc([Cc, 8, 64]), ALU.mult)
        P.tt("dve", c.Xb.re("p (h v) -> p h v", v=128), bkY[:Cc, 0:512].re("p (h v) -> p h v", v=128),
             c1.re("p (h o) -> p h o", o=1).bc([Cc, 4, 128]), ALU.mult)
        if full:
            bkO, bkYr = P.bank(), P.bank()
            for h in range(4):
                P.mm(bkO[:Cc, h * 128:(h + 1) * 128], c.qg[:, h, :], Sbf[:, h, :], start=True, stop=False)
                P.mm(bkO[:Cc, h * 128:(h + 1) * 128], c.QKT[:, h, :], c.Xb[:, h * 128:(h + 1) * 128],
                     start=False, stop=True)
            for h in range(8):
                t, hp = h // 2, (h % 2) * 64
                o_ = bkYr[:Cc, h * 64:(h + 1) * 64]
                P.mm(o_, c.art[hp:hp + 64, t, 1, :], Abf[hp:hp + 64, t, hp:hp + 64], start=True, stop=False)
                P.mm(o_, c.NrbT[:, h, :], c.Ub[:, h * 64:(h + 1) * 64], start=False, stop=False)
                P.mm(o_, c.NrkT[:, h, :], c.Vt[:, h * 64:(h + 1) * 64], start=False, stop=True)
            P.copy("act", c.w5[0].v(), bkYr[:Cc, 0:512])
            P.tt("dve", c.w5[1].re("p (h v) -> p h v", v=128), bkO[:Cc, 0:512].re("p (h v) -> p h v", v=128),
                 sqn.re("p (h o) -> p h o", o=1).bc([Cc, 4, 128]), ALU.mult)
        bkA, bkS = P.bank(), P.bank()
        for t in range(4):
            P.mm(bkA[:, t * 128:(t + 1) * 128], c.BK[:, t * 128:(t + 1) * 128], c.Ub[:, t * 128:(t + 1) * 128],
                 start=True, stop=False)
            P.mm(bkA[:, t * 128:(t + 1) * 128], c.BK[:, 512 + t * 128:512 + (t + 1) * 128],
                 c.Vt[:, t * 128:(t + 1) * 128], start=False, stop=True)
        for h in range(4):
            P.mm(bkS[:, h * 128:(h + 1) * 128], c.kd[:, h * 128:(h + 1) * 128], c.Xb[:, h * 128:(h + 1) * 128])
        for t in range(4):
            for hp in (0, 64):
                P.stt(A[hp:hp + 64, t, hp:hp + 64], A[hp:hp + 64, t, hp:hp + 64], c.sm[hp:hp + 64, 4 + t:5 + t],
                      bkA[hp:hp + 64, t * 128 + hp:t * 128 + hp + 64], ALU.mult, ALU.add)
        P.copy("pool", Abf[0:64, :, 0:64], A[0:64, :, 0:64])
        P.copy("pool", Abf[64:128, :, 64:128], A[64:128, :, 64:128])
        for h in range(4):
            P.stt(S[:, h, :], S[:, h, :], c.eGl[:, h:h + 1], bkS[:, h * 128:(h + 1) * 128], ALU.mult, ALU.add)
        P.copy("act", Sbf.v(), S.v())

        if not full:
            return
        y, ysq, w3 = c.w5[0], c.w5[2], c.w5[2]
        y3 = y.re("p (h v) -> p h v", v=64)
        st = stat()
        P.reduce(st[:Cc, 0:8], y3)
        P.tt("pool", ysq.v(), y.v(), y.v(), ALU.mult)
        P.reduce(st[:Cc, 8:16], ysq.re("p (h v) -> p h v", v=64))
        st2 = stat()
        P.ts("dve", st2[:Cc, 0:8], st[:Cc, 0:8], 1.0 / 64, ALU.mult)
        P.tt("dve", st2[:Cc, 8:16], st2[:Cc, 0:8], st2[:Cc, 0:8], ALU.mult)
        P.stt(st[:Cc, 8:16], st[:Cc, 8:16], 1.0 / 64, st2[:Cc, 8:16], ALU.mult, ALU.subtract)
        P.ts("dve", st[:Cc, 8:16], st[:Cc, 8:16], GN_EPS, ALU.add)
        P.tt("pool", st[:Cc, 0:8], st[:Cc, 8:16], pw[:Cc, 0:8], ALU.pow)
        P.tt("dve", y3, y3, st2[:Cc, 0:8].re("p (h o) -> p h o", o=1).bc([Cc, 8, 64]), ALU.subtract)
        P.tt("dve", y3, y3, st[:Cc, 0:8].re("p (h o) -> p h o", o=1).bc([Cc, 8, 64]), ALU.mult)
        P.tt("pool", y.v(), y.v(), bcv("lnxw")[:Cc, :], ALU.mult)
        P.tt("pool", y.v(), y.v(), bcv("lnxb")[:Cc, :], ALU.add)
        P.tt("dve", w3.re("p (h v) -> p h v", v=64), c.Vt.re("p (h v) -> p h v", v=64),
             tk[:, 24:32].re("p (h o) -> p h o", o=1).bc([Cc, 8, 64]), ALU.mult)
        P.tt("pool", y.v(), y.v(), w3.v(), ALU.add)
        P.tt("dve", c.mix[:, 0:512], y.v(), c.gt.v(), ALU.mult)
        o, osq = c.w5[1], c.w5[2]
        o3 = o.re("p (h v) -> p h v", v=128)
        P.tt("pool", osq.v(), o.v(), o.v(), ALU.mult)
        st = stat()
        P.reduce(st[:Cc, 0:4], osq.re("p (h v) -> p h v", v=128))
        P.ts("dve", st[:Cc, 0:4], st[:Cc, 0:4], 1.0 / 128, ALU.mult, 1e-6, ALU.add)
        P.tt("pool", st[:Cc, 4:8], st[:Cc, 0:4], pw[:Cc, 0:4], ALU.pow)
        P.tt("dve", o3, o3, st[:Cc, 4:8].re("p (h o) -> p h o", o=1).bc([Cc, 4, 128]), ALU.mult)
        P.tt("pool", o3, o3, bcv("gng")[:Cc, :].re("p (o v) -> p o v", o=1).bc([Cc, 4, 128]), ALU.mult)
        P.tt("dve", c.mix[:, 512:1024], o.v(), c.zs.v(), ALU.mult)
        bk = P.bank()
        bb = bk.v().bitcast(BF16)
        for kc in range(8):
            P.tr(bb[:, kc * Cc:(kc + 1) * Cc], c.mix[:, kc * 128:(kc + 1) * 128], identb[:Cc, :Cc])
        P.copy("act", mixT_dst, bb[:, 0:8 * Cc].re("p (k c) -> p k c", c=Cc))
        if dbg == "mix":
            P.copy("dve", xt[:Cc, :], c.mix.v())

    def state_out(c, o_shift, o_wkv, o_conv, o_gdn, b):
        bk = P.bank()
        P.tr(bk[0:14, 0:128], fcar.re("p t o -> p (t o)"), identf.v())
        P.copy("act", sioA[0:14, 0:128], bk[0:14, 0:128])
        P.dma("sp", o_shift[b].re("(t p) -> t p", p=128), sioA[0:14, 0:128], key=sioA)
        q36 = sioC
        P.copy("pool", q36[:, 0:36].re("p (r t) -> p t r", r=3), qcar.v())
        bk2 = P.bank()
        P.tr(bk2[0:36, 0:128], q36[:, 0:36], identf.v())
        P.copy("act", sioB[0:36, 0:128], bk2[0:36, 0:128])
        P.dma("sp", o_conv[b].re("r (t p) -> (r t) p", p=128), sioB[0:36, 0:128], key=sioB)
        for t in range(4):
            bk3 = P.bank()
            P.tr(bk3[:, 0:128], A[:, t, :], identf.v())
            P.copy("act", sioW[:, t * 128:(t + 1) * 128], bk3[:, 0:128])
        for t in range(4):
            for hl in range(2):
                P.dma("sp", o_wkv[b, 2 * t + hl],
                      sioW[hl * 64:(hl + 1) * 64, t * 128 + hl * 64:t * 128 + (hl + 1) * 64], key=sioW)
        P.dma("sp", o_gdn[b].re("h k v -> k h v"), S.v(), key=S)

    def state_in(c, b):
        P.dma("sp", sioA[0:14, 0:128], st_shift[b].re("(t p) -> t p", p=128))
        bk = P.bank()
        P.tr(bk[:, 0:14], sioA[0:14, 0:128], identf[0:14, 0:14])
        P.copy("act", fcar.re("p t o -> p (t o)"), bk[:, 0:14])
        P.dma("sp", sioB[0:36, 0:128], st_conv[b].re("r (t p) -> (r t) p", p=128))
        bk2 = P.bank()
        P.tr(bk2[:, 0:36], sioB[0:36, 0:128], identf[0:36, 0:36])
        P.copy("act", qcar.v(), bk2[:, 0:36].re("p (r t) -> p t r", r=3))
        P.memset("pool", c.cum[:, :, 0:1], 0.0)
        P.dma("sp", sioW[0:64, 0:512].re("v (t hl k) -> v t hl k", t=4, hl=2),
              st_wkv[b].re("(t hl) v k -> v t hl k", hl=2))
        P.memset("pool", A.v(), 0.0)
        P.memset("pool", Abf.v(), 0.0)
        for t in range(4):
            bk3 = P.bank()
            P.tr(bk3[:, 0:64], sioW[0:64, t * 128:(t + 1) * 128], identf[0:64, 0:64])
            P.copy("act", A[0:64, t, 0:64], bk3[0:64, 0:64])
            P.copy("act", A[64:128, t, 64:128], bk3[64:128, 0:64])
        P.copy("pool", Abf[0:64, :, 0:64], A[0:64, :, 0:64])
        P.copy("pool", Abf[64:128, :, 64:128], A[64:128, :, 64:128])
        P.dma("sp", S.v(), st_gdn[b].re("h k v -> k h v"))
        P.copy("act", Sbf.v(), S.v())

    def stream_tm(sc, npieces, srcT, nch, Cc, finish):
        groups = [[j] for j in range(nch)] if SEQ_STREAM else [list(range(nch))]
        for js in groups:
            bks_ = {j: [P.bank() for n in range(2)] for j in js}
            for g in range(npieces):
                ds = dsl[dsi[0] % NSL]
                dsi[0] += 1
                P.dma("sp", ds.v(), sc[g])
                for j in js:
                    for n in range(2):
                        for hc in range(2):
                            P.mm(bks_[j][n][:Cc, 0:512], srcT(j, 2 * g + hc), ds[:, hc, n * 512:(n + 1) * 512],
                                 start=(g == 0 and hc == 0), stop=(g == npieces - 1 and hc == 1))
            for j in js:
                for n in range(2):
                    finish(j, n, bks_[j][n])

    def macro(c, chunks):
        Cc = c.C
        nch = len(chunks)
        T = nch * Cc
        P.cur_pool = "m"

        def dump():
            for j, (xr, psrc, ydst) in enumerate(chunks):
                P.dma(XQ, ydst, xr[:Cc, :], key=xr)
        if dbg == "mix":
            return dump()
        stream_tm(sc_out, 4, lambda j, k: mixTm[:, k, j * Cc:(j + 1) * Cc], nch, Cc,
                  lambda j, n, bk: P.tt("dve", chunks[j][0][:Cc, n * 512:(n + 1) * 512],
                                        chunks[j][0][:Cc, n * 512:(n + 1) * 512], bk[:Cc, 0:512], ALU.add))
        if dbg == "x1":
            return dump()
        for j, (xr, _, _) in enumerate(chunks):
            norm_T(xr[:Cc, :], Cc, h2T, j * Cc, "gffn")
        for g in range(NHC):
            gs, us = gsl[g % NSL], usl[g % NSL]
            P.dma("sp", gs.v(), sc_gate[g])
            P.dma("sp", us.v(), sc_up[g])
            bG, bU = P.bank(), P.bank()
            for kc in range(8):
                P.mm(bG[:, 0:T], gs[:, kc, :], h2T[:, kc, 0:T], start=(kc == 0), stop=(kc == 7))
            for kc in range(8):
                P.mm(bU[:, 0:T], us[:, kc, :], h2T[:, kc, 0:T], start=(kc == 0), stop=(kc == 7))
            sg_ = sgb[g % 2]
            P.act(sg_[:, 0:T], bG[:, 0:T], AF.Silu)
            P.tt("dve", actT[:, g, 0:T], sg_[:, 0:T], bU[:, 0:T], ALU.mult)
        stream_tm(sc_down, NG, lambda j, k: actT[:, k, j * Cc:(j + 1) * Cc], nch, Cc,
                  lambda j, n, bk: P.tt("dve", chunks[j][0][:Cc, n * 512:(n + 1) * 512],
                                        chunks[j][0][:Cc, n * 512:(n + 1) * 512], bk[:Cc, 0:512], ALU.add))
        if dbg == "x2":
            return dump()
        for j, (xr, psrc, ydst) in enumerate(chunks):
            norm_T(xr[:Cc, :], Cc, h2T, j * Cc, "gple")
        gate = {}
        actF = actT.re("p a b -> p (a b)").bitcast(F32)
        pTv = actT.re("p a b -> p (a b)")[:, 4096:4096 + 2 * TM].re("p (k c) -> p k c", k=2)

        def fin_gate(j, n, bk):
            w = actF[:Cc, (2 * j + n) * 512:(2 * j + n + 1) * 512]
            P.act(w.v(), bk[:Cc, 0:512], AF.Tanh, scale=0.5)
            P.ts("dve", w.v(), w.v(), 0.5, ALU.mult, 0.5, ALU.add)
            gate[(j, n)] = w
        stream_tm(sc_pg, 4, lambda j, k: h2T[:, k, j * Cc:(j + 1) * Cc], nch, Cc, fin_gate)
        for j, (xr, psrc, ydst) in enumerate(chunks):
            P.dma(XQ, mpt[:Cc, :], psrc)
            P.copy("pool", mpbf[:Cc, :], mpt[:Cc, :])
            bk = P.bank()
            bb = bk.v().bitcast(BF16)
            for kc in range(2):
                P.tr(bb[:, kc * Cc:(kc + 1) * Cc], mpbf[:Cc, kc * 128:(kc + 1) * 128], identb[:Cc, :Cc])
            P.copy("act", pTv[:, :, j * Cc:(j + 1) * Cc], bb[:, 0:2 * Cc].re("p (k c) -> p k c", c=Cc))

        def fin_pp(j, n, bk):
            w = gate[(j, n)]
            xr = chunks[j][0]
            P.tt("dve", w.v(), w.v(), bk[:Cc, 0:512], ALU.mult)
            P.tt("pool", xr[:Cc, n * 512:(n + 1) * 512], xr[:Cc, n * 512:(n + 1) * 512], w.v(), ALU.add)
        stream_tm(sc_pp, 1, lambda j, k: pTv[:, k, j * Cc:(j + 1) * Cc], nch, Cc, fin_pp)
        if dbg == "x3":
            return dump()
        for j, (xr, psrc, ydst) in enumerate(chunks):
            st = stat()
            P.act(xn_m[:Cc, :], xr[:Cc, :], AF.Square, accum=st[:Cc, 0:1])
            P.ts("dve", st[:Cc, 1:2], st[:Cc, 0:1], 1.0 / D, ALU.mult, 1e-6, ALU.add)
            P.tt("pool", st[:Cc, 2:3], st[:Cc, 1:2], pw[:Cc, 0:1], ALU.pow)
            P.stt(xr[:Cc, :], xr[:Cc, :], st[:Cc, 2:3], bcv("fng")[:Cc, :], ALU.mult, ALU.mult)
            P.dma(XQ, ydst, xr[:Cc, :], key=xr)

    cp = make_ctx(CP, "p")
    seq_init_zero(cp)
    xi = [0]

    def load_x(src):
        xt = xq[xi[0] % NXQ]
        xi[0] += 1
        P.dma(XQ, xt[:src.shape[0], :], src)
        return xt

    srcs = [(xa[i * CP:(i + 1) * CP, :], False) for i in range(NA)] + \
           [(xb[i * CP:(i + 1) * CP, :], True) for i in range(NB)]
    nxt = load_x(srcs[0][0]) if srcs else None
    pend = []
    for i, (src, full) in enumerate(srcs):
        xt = nxt
        if i + 1 < len(srcs):
            nxt = load_x(srcs[i + 1][0])
        j = i - NA
        P.tag = "chunk%d" % i
        if os.environ.get("BAR", "macro") in ("chunk", "both"):
            P.barrier()
        chunk(cp, xt, full, mixTm[:, :, len(pend) * CP:(len(pend) + 1) * CP] if full else None)
        if full:
            pend.append((xt, pb[j * CP:(j + 1) * CP, :], yb[j * CP:(j + 1) * CP, :]))
            if len(pend) == 2 or i == len(srcs) - 1:
                P.tag = "macro%d" % i
                if os.environ.get("BAR", "macro") in ("macro", "both"):
                    P.barrier()
                macro(cp, pend)
                if os.environ.get("BAR", "macro") in ("macro", "both"):
                    P.barrier()
                P.cur_pool = "c"
                pend = []
    if NA + NB > 0:
        state_out(cp, o_shift_p, o_wkv_p, o_conv_p, o_gdn_p, 0)

    if NS > 0:
        P.barrier()
        cs = make_ctx(CS, "s")
        pend = []
        for b in range(NS):
            state_in(cs, b)
            xt = load_x(xs[b * CS:(b + 1) * CS, :])
            chunk(cs, xt, True, mixTm[:, :, len(pend) * CS:(len(pend) + 1) * CS])
            state_out(cs, o_shift_s, o_wkv_s, o_conv_s, o_gdn_s, b)
            pend.append((xt, ps[b * CS:(b + 1) * CS, :], ys[b * CS:(b + 1) * CS, :]))
            if len(pend) == 2 or b == NS - 1:
                macro(cs, pend)
                P.cur_pool = "c"
                pend = []
    span = P.schedule()
    if os.environ.get("NOSCHED"):
        P.sched_order = None
    print("arena used", arena.hi)
    info = P.emit()
    return nc, (info, span)


_CACHE = {}


def _get_prog(NA, NB, NS, dbg=None):
    key = (NA, NB, NS, dbg)
    if key not in _CACHE:
        _CACHE[key] = build(NA, NB, NS, dbg=dbg)
    return _CACHE[key]


def run_cores(per_core, NA, NB, NS):
    nc, info = _get_prog(NA, NB, NS)
    res = run_bass_kernel_spmd(nc, per_core, core_ids=list(range(len(per_core))))
    return res.results


def kernel(**inp):
    f = lambda a: np.ascontiguousarray(np.asarray(a, np.float32))
    xp = f(inp["x_prompt"])
    B, SEQ, _ = xp.shape
    xsm = f(inp["x_sample"])
    DB, DS, _ = xsm.shape
    pp = f(inp["p_prompt"])[0]
    psm = f(inp["p_sample"])[0]
    n_cores = 8
    halves = n_cores // B
    assert halves == 2
    HT = SEQ // 2
    NA = NB = HT // 128
    NS = DB // n_cores
    cols, bc = _pack_consts(inp)
    shared = {
        "w_in": f(inp["w_in"][0]), "w_lup": f(inp["w_lora_up"][0]), "a_lup": f(inp["a_lora_up"][0]),
        "g_lup": f(inp["g_lora_up"][0]), "w_out": f(inp["w_out"][0]), "w_gate": f(inp["w_gate"][0]),
        "w_up": f(inp["w_up"][0]), "w_down": f(inp["w_down"][0]), "w_pg": f(inp["w_ple_gate"][0]),
        "w_pp": f(inp["w_ple_proj"][0]), "cols": cols, "bc": bc,
    }
    per_core = []
    for cix in range(n_cores):
        b, half = cix // 2, cix % 2
        m = dict(shared)
        m["xa"] = np.zeros((HT, D), np.float32) if half == 0 else xp[b, 0:HT]
        m["xb"] = xp[b, half * HT:(half + 1) * HT]
        m["pb"] = pp[b, half * HT:(half + 1) * HT]
        sl = slice(cix * NS, (cix + 1) * NS)
        m["xs"] = xsm[sl].reshape(NS * DS, D)
        m["ps"] = psm[sl].reshape(NS * DS, D_PLE)
        m["st_shift"] = f(inp["state_shift"][0][sl])
        m["st_wkv"] = f(inp["state_wkv"][0][sl])
        m["st_conv"] = f(inp["state_conv"][0][sl])
        m["st_gdn"] = f(inp["state_gdn"][0][sl])
        per_core.append({k: np.ascontiguousarray(v) for k, v in m.items()})
    res = run_cores(per_core, NA, NB, NS)
    y_prompt = np.zeros((B, SEQ, D), np.float32)
    y_sample = np.zeros((DB, DS, D), np.float32)
    nsp = np.zeros((1, B, R_PROJ), np.float32)
    nwp = np.zeros((1, B, 8, 64, 64), np.float32)
    ncp = np.zeros((1, B, 3, 1536), np.float32)
    ngp = np.zeros((1, B, 4, 128, 128), np.float32)
    nss = np.zeros((1, DB, R_PROJ), np.float32)
    nws = np.zeros((1, DB, 8, 64, 64), np.float32)
    ncs = np.zeros((1, DB, 3, 1536), np.float32)
    ngs = np.zeros((1, DB, 4, 128, 128), np.float32)
    for cix in range(n_cores):
        b, half = cix // 2, cix % 2
        r = res[cix]
        y_prompt[b, half * HT:(half + 1) * HT] = r["yb"]
        sl = slice(cix * NS, (cix + 1) * NS)
        y_sample[sl] = r["ys"].reshape(NS, DS, D)
        if half == 1:
            nsp[0, b] = r["o_shift_p"][0]
            nwp[0, b] = r["o_wkv_p"][0]
            ncp[0, b] = r["o_conv_p"][0]
            ngp[0, b] = r["o_gdn_p"][0]
        nss[0, sl] = r["o_shift_s"]
        nws[0, sl] = r["o_wkv_s"]
        ncs[0, sl] = r["o_conv_s"]
        ngs[0, sl] = r["o_gdn_s"]
    return (y_prompt, y_sample, nsp, nwp, ncp, ngp, nss, nws, ncs, ngs)
```

```python
import numpy as np
import concourse.bass as bass
import concourse.mybir as mybir
from concourse.bass_utils import run_bass_kernel_spmd

F32 = mybir.dt.float32
BF16 = mybir.dt.bfloat16
AF = mybir.ActivationFunctionType
ALU = mybir.AluOpType
AX = mybir.AxisListType

D = 1024
D_PLE = 256
R_PROJ = 1792
D_IN = 3848
D_FF = 2816
NHC = D_FF // 128
OFF_Z = 3328
OFF_B = 3840
C0 = 0.5 * float(np.exp(-0.5))
GN_EPS = 64e-5


class V:
    def __init__(self, t, ap, gen=None):
        self.t = t
        self.ap = ap
        self.gen = gen

    def __getitem__(self, k):
        return V(self.t, self.ap[k], self.gen)

    def re(self, pat, **kw):
        return V(self.t, self.ap.rearrange(pat, **kw), self.gen)

    def bc(self, shape):
        return V(self.t, self.ap.to_broadcast(list(shape)), self.gen)

    def bitcast(self, dt):
        return V(self.t, self.ap.bitcast(dt), self.gen)

    def v(self):
        return self

    @property
    def shape(self):
        return self.ap.shape


class Arena:
    def __init__(self, P, name, nbytes):
        self.P = P
        h = P.nc.alloc_sbuf_tensor("s_" + name, [128, nbytes // 4], F32)
        self.ap = h.ap()
        self.nbytes = nbytes
        self.off = 0
        self.hi = 0

    def take(self, name, shape, dt=F32):
        esz = 2 if dt == BF16 else 4
        n = 1
        for d in shape[1:]:
            n *= d
        off = (self.off + 63) // 64 * 64
        assert off + n * esz <= self.nbytes, (name, off, n * esz, self.nbytes)
        base = self.ap.bitcast(dt) if dt != F32 else self.ap
        view = base[0:shape[0], off // esz:off // esz + n]
        if len(shape) == 3:
            view = view.rearrange("p (a b) -> p a b", b=shape[2])
        elif len(shape) == 4:
            view = view.rearrange("p (a b c) -> p a b c", b=shape[2], c=shape[3])
        self.off = off + n * esz
        self.hi = max(self.hi, self.off)
        t = Tile(view, name)
        self.P.tiles.append(t)
        return t


class Tile:
    def __init__(self, ap, name):
        self.ap = ap
        self.name = name
        self.lw = None
        self.rd = []
        self.dsem = None
        self.dcount = 0
        self.last_dma = None

    def __getitem__(self, k):
        return V(self, self.ap[k])

    def v(self):
        return V(self, self.ap)

    def re(self, pat, **kw):
        return V(self, self.ap.rearrange(pat, **kw))

    @property
    def shape(self):
        return self.ap.shape


class Op:
    __slots__ = ("eng", "fn", "deps", "odeps", "dma", "sig", "cnt", "idx", "n", "kind", "tag", "t0", "t1")


ENGS = ("pe", "act", "dve", "pool", "sp")


class Prog:
    def __init__(self, nc):
        self.nc = nc
        self.ops = []
        self.tiles = []
        self.banks = []
        self.bi = 0

    def sbuf(self, name, shape, dt=F32):
        h = self.nc.alloc_sbuf_tensor("s_" + name, list(shape), dt)
        t = Tile(h.ap(), name)
        self.tiles.append(t)
        return t

    def psum(self, name, shape, dt=F32):
        h = self.nc.alloc_psum_tensor("p_" + name, list(shape), dt)
        t = Tile(h.ap(), name)
        self.tiles.append(t)
        return t

    def dram(self, name, shape, dt=F32, kind="Internal"):
        h = self.nc.dram_tensor(name, list(shape), dt, kind=kind)
        t = Tile(h.ap(), name)
        self.tiles.append(t)
        return t

    def bank(self, pool=None):
        pool = pool or self.cur_pool
        lst = self.pools[pool]
        b = lst[self.pi[pool] % len(lst)]
        self.pi[pool] += 1
        b.gen = getattr(b, "gen", 0) + 1
        return V(b, b.ap, b.gen)

    def handoff(self, old, new):
        S = set()
        for t in old:
            if t.lw is not None:
                S.add(t.lw)
            S.update(t.rd)
        for t in new:
            t.rd = list(S | set(t.rd))

    def barrier(self):
        has_succ = set()
        for op in self.ops:
            has_succ.update(op.deps)
            has_succ.update(op.odeps)
        deps = {op.idx for op in self.ops if op.idx not in has_succ}
        for t in self.tiles:
            if t.last_dma is not None:
                deps.add(t.last_dma)
        if not hasattr(self, "last_bar"):
            self.last_bar = {}
        for e in ENGS:
            op = self.add(e, lambda eng: eng.nop(), [], [], kind="bar")
            op.deps = set(deps)
            self.last_bar[e] = op.idx

    def add(self, eng, fn, reads, writes, dma=None, n=0, kind=""):
        op = Op()
        op.tag = getattr(self, "tag", "")
        op.t0 = op.t1 = 0.0
        op.n = n
        op.kind = kind
        op.odeps = set()
        op.eng = eng
        op.fn = fn
        op.idx = len(self.ops)
        op.dma = dma
        op.sig = dma is not None
        op.cnt = 0
        deps = set()
        for x in list(reads) + list(writes):
            if isinstance(x, V) and x.gen is not None:
                assert x.gen == x.t.gen, "stale PSUM bank use: %s (gen %d, now %d)" % (x.t.name, x.gen, x.t.gen)
        rt = [x.t if isinstance(x, V) else x for x in reads]
        wt = [x.t if isinstance(x, V) else x for x in writes]
        for t in rt:
            if t.lw is not None:
                deps.add(t.lw)
        for t in wt:
            if t.lw is not None:
                deps.add(t.lw)
            deps.update(t.rd)
        for t in rt:
            t.rd.append(op.idx)
        for t in wt:
            t.lw = op.idx
            t.rd = []
        if eng == "pe":
            op.odeps = {d for d in deps if self.ops[d].eng == "pe"}
            deps = deps - op.odeps
        lb = getattr(self, "last_bar", {}).get(eng)
        if lb is not None:
            op.odeps.add(lb)
        op.deps = deps
        if dma is not None:
            dma.dcount += 1
            op.cnt = dma.dcount * 16
            dma.last_dma = op.idx
            import os as _os3
            kser = int(_os3.environ.get("SERDMA", "2"))
            if eng == "pool":
                kser = int(_os3.environ.get("SERPOOL", "2"))
            if kser > 0:
                pq = getattr(self, "prev_dma_q", {})
                lst = pq.setdefault(eng, [])
                if len(lst) >= kser:
                    op.deps.add(lst[-kser])
                lst.append(op.idx)
                self.prev_dma_q = pq
        self.ops.append(op)
        return op

    @staticmethod
    def _ap(x):
        if isinstance(x, (V, Tile)):
            return x.ap
        return x

    @staticmethod
    def _tl(*xs):
        return [x for x in xs if isinstance(x, (V, Tile))]

    @staticmethod
    def _n(x):
        try:
            return int(Prog._ap(x).free_size())
        except Exception:
            return 128

    def mm(self, out, lhsT, rhs, start=True, stop=True):
        o, l, r = self._ap(out), self._ap(lhsT), self._ap(rhs)
        k = "mm32" if l.dtype == F32 else "mm"
        self.add("pe", lambda e: e.matmul(o, l, r, start=start, stop=stop),
                 self._tl(lhsT, rhs), self._tl(out), n=self._n(rhs), kind=k)

    def tr(self, out, in_, ident):
        o, i, d = self._ap(out), self._ap(in_), self._ap(ident)
        self.add("pe", lambda e: e.transpose(o, i, d), self._tl(in_, ident), self._tl(out), n=128, kind="mm")

    def act(self, out, in_, func, bias=None, scale=None, accum=None):
        o, i = self._ap(out), self._ap(in_)
        kw = {}
        if bias is not None:
            kw["bias"] = self._ap(bias)
        if scale is not None:
            kw["scale"] = self._ap(scale)
        if accum is not None:
            kw["accum_out"] = self._ap(accum)
        self.add("act", lambda e: e.activation(o, i, func, **kw),
                 self._tl(in_, bias, scale), self._tl(out, accum), n=self._n(out), kind="act")

    def tt(self, eng, out, a, b, op):
        o, x, y = self._ap(out), self._ap(a), self._ap(b)
        self.add(eng, lambda e: e.tensor_tensor(o, x, y, op), self._tl(a, b), self._tl(out), n=self._n(out), kind="tt")

    def ts(self, eng, out, a, s1, op0, s2=None, op1=None):
        o, x = self._ap(out), self._ap(a)
        c1, c2 = self._ap(s1), self._ap(s2)
        kw = {}
        if op1 is not None:
            kw["op1"] = op1
        self.add(eng, lambda e: e.tensor_scalar(o, x, c1, c2, op0, **kw),
                 self._tl(a, s1, s2), self._tl(out), n=self._n(out), kind="ts")

    def stt(self, out, a, s, b, op0, op1):
        o, x, c, y = self._ap(out), self._ap(a), self._ap(s), self._ap(b)
        self.add("dve", lambda e: e.scalar_tensor_tensor(o, x, c, y, op0, op1),
                 self._tl(a, s, b), self._tl(out), n=self._n(out), kind="tt")

    def copy(self, eng, out, in_):
        o, i = self._ap(out), self._ap(in_)
        if eng == "act":
            self.add(eng, lambda e: e.copy(o, i), self._tl(in_), self._tl(out), n=self._n(out), kind="act")
        else:
            self.add(eng, lambda e: e.tensor_copy(o, i), self._tl(in_), self._tl(out), n=self._n(out), kind="ts")

    def memset(self, eng, out, val):
        o = self._ap(out)
        self.add(eng, lambda e: e.memset(o, val), [], self._tl(out), n=self._n(out), kind="ts")

    def recip(self, out, in_):
        o, i = self._ap(out), self._ap(in_)
        self.add("dve", lambda e: e.reciprocal(o, i), self._tl(in_), self._tl(out), n=8 * self._n(out), kind="ts")

    def reduce(self, out, in_, op=ALU.add):
        o, i = self._ap(out), self._ap(in_)
        self.add("dve", lambda e: e.tensor_reduce(o, i, AX.X, op), self._tl(in_), self._tl(out), n=self._n(in_), kind="ts")

    def scan(self, out, d0, d1):
        o, a, b = self._ap(out), self._ap(d0), self._ap(d1)
        self.add("dve", lambda e: e.tensor_tensor_scan(o, a, b, 0.0, ALU.mult, ALU.add),
                 self._tl(d0, d1), self._tl(out), n=self._n(out), kind="tt")

    def dma(self, q, out, in_, key=None, **kw):
        o, i = self._ap(out), self._ap(in_)
        if q == "pool":
            kw.setdefault("max_dma_last_dim", 4096)
        kt = key if key is not None else (out.t if isinstance(out, V) else out)
        nb = 1
        for d_ in o.shape:
            nb *= int(d_)
        self.add(q, lambda e: e.dma_start(o, i, **kw), self._tl(in_), self._tl(out), dma=kt, n=nb, kind="dma")

    def generic(self, eng, fn, reads, writes):
        self.add(eng, fn, reads, writes, n=128, kind="ts")

    def _dur(self, op):
        n, k, e = op.n, op.kind, op.eng
        if k == "dma":
            return 1800.0 + n * 0.008
        if k == "bar":
            return 100.0
        if e == "pe":
            if k == "mm32":
                return 4.0 * (60.0 + 0.42 * max(n, 64))
            return 85.0 + 0.42 * max(n - 128, 0)
        if e == "act":
            return 230.0 + 0.8 * n
        if e == "dve":
            return 70.0 + (1.3 if k == "tt" else 1.0) * n
        if e == "pool":
            return 250.0 + 1.6 * n
        return 60.0

    def schedule(self):
        import heapq
        ops = self.ops
        nops = len(ops)
        succ = [[] for _ in range(nops)]
        indeg = [0] * nops
        last_on = {}
        isucc = [[] for _ in range(nops)]
        for op in ops:
            ds = set(op.deps) | set(op.odeps)
            for d in ds:
                succ[d].append(op.idx)
            indeg[op.idx] = len(ds)
            import os as _os
            inord = _os.environ.get("INORDER", "")
            if op.kind in ("dma", "bar") or op.eng in inord.split(","):
                if op.eng in last_on and last_on[op.eng] not in ds:
                    isucc[last_on[op.eng]].append(op.idx)
                    indeg[op.idx] += 1
                last_on[op.eng] = op.idx
        ready = {e: [] for e in ENGS}
        for op in ops:
            if indeg[op.idx] == 0:
                heapq.heappush(ready[op.eng], op.idx)
        free_t = {e: 0.0 for e in ENGS}
        events = []
        order = {e: [] for e in ENGS}
        now = 0.0
        done = 0
        SEM = 60.0
        bw_free = 0.0
        while done < nops:
            for e in ENGS:
                while ready[e] and free_t[e] <= now:
                    i = heapq.heappop(ready[e])
                    op = ops[i]
                    d = self._dur(op)
                    op.t0, op.t1 = now, now + d
                    for s_ in isucc[i]:
                        indeg[s_] -= 1
                        if indeg[s_] == 0:
                            heapq.heappush(ready[ops[s_].eng], s_)
                    if op.kind == "dma":
                        free_t[e] = now + 70.0
                        fin = max(now + 1500.0, bw_free) + op.n * 0.008
                        bw_free = fin
                        op.t1 = fin
                        heapq.heappush(events, (fin, i))
                    else:
                        free_t[e] = now + d
                        heapq.heappush(events, (now + d + SEM, i))
                    order[e].append(i)
            cand = [free_t[e] for e in ENGS if ready[e]]
            if events:
                cand.append(events[0][0])
            assert cand, "scheduler deadlock"
            now = max(now, min(cand))
            while events and events[0][0] <= now:
                t, i = heapq.heappop(events)
                done += 1
                for s_ in succ[i]:
                    indeg[s_] -= 1
                    if indeg[s_] == 0:
                        heapq.heappush(ready[ops[s_].eng], s_)
        assert sum(len(v) for v in order.values()) == nops
        self.sched_busy = {e: sum(self._dur(ops[i]) for i in order[e] if ops[i].kind != "dma") for e in ENGS}
        self.sched_order = order
        self.sched_span = now
        return now


    def emit(self):
        nc = self.nc
        ops = self.ops
        for op in ops:
            for d in op.deps:
                ops[d].sig = True
        esem = {e: nc.alloc_semaphore("es_" + e) for e in ENGS}
        cnt = {e: 0 for e in ENGS}
        if getattr(self, "sched_order", None):
            per = {e: [ops[i] for i in self.sched_order[e]] for e in ENGS}
        else:
            per = {e: [o for o in ops if o.eng == e] for e in ENGS}
        for e in ENGS:
            for op in per[e]:
                if op.dma is not None:
                    if op.dma.dsem is None:
                        op.dma.dsem = nc.alloc_semaphore("ds_" + op.dma.name)
                elif op.sig:
                    cnt[op.eng] += 1
                    op.cnt = cnt[op.eng]

        def comp(op):
            if op.dma is not None:
                return op.dma.dsem, op.cnt
            return esem[op.eng], op.cnt

        dma_tiles = [t for t in self.tiles if t.dsem is not None]

        def run(ename, eng):
            waited = {}
            for op in per[ename]:
                need = {}
                for d in op.deps:
                    s, v = comp(ops[d])
                    k = id(s)
                    if waited.get(k, 0) >= v:
                        continue
                    if k not in need or need[k][1] < v:
                        need[k] = (s, v)
                import os as _os2
                for k, (s, v) in need.items():
                    if _os2.environ.get("BREAK") == ename and len(waited) % 7 == 3:
                        waited[k] = v
                        continue
                    eng.wait_ge(s, v)
                    waited[k] = v
                ins = op.fn(eng)
                if op.dma is not None:
                    ins.then_inc(op.dma.dsem, 16)
                elif op.sig:
                    ins.then_inc(esem[ename], 1)
            if ename == "sp":
                for t in dma_tiles:
                    eng.wait_ge(t.dsem, t.dcount * 16)

        with nc.Block() as block:
            @block.tensor
            def _(e):
                run("pe", e)

            @block.scalar
            def _(e):
                run("act", e)

            @block.vector
            def _(e):
                run("dve", e)

            @block.gpsimd
            def _(e):
                run("pool", e)

            @block.sync
            def _(e):
                run("sp", e)
        return {e: len(per[e]) for e in ENGS}, cnt


COLS = {}
_c = 0
for _n, _w in (("gmix", 8), ("gffn", 8), ("gple", 8), ("mu", 14), ("w0", 4), ("a0", 4), ("kk", 4),
               ("ka", 4), ("rk", 4), ("cw0", 12), ("cw1", 12), ("cw2", 12), ("cw3", 12)):
    COLS[_n] = (_c, _w)
    _c += _w
NCOLS = _c
BCS = {}
_c = 0
for _n, _w in (("lnxw", 512), ("lnxb", 512), ("gng", 128), ("fng", 1024), ("alog", 4), ("dtb", 4)):
    BCS[_n] = (_c, _w)
    _c += _w
NBC = _c


def _pack_consts(inp):
    cols = np.zeros((128, NCOLS), np.float32)

    def put(name, vec):
        o, w = COLS[name]
        cols[:, o:o + w] = np.asarray(vec, np.float32).reshape(w, 128).T

    put("gmix", inp["ln_mix_g"][0])
    put("gffn", inp["ln_ffn_g"][0])
    put("gple", inp["ln_ple_g"][0])
    put("mu", inp["mu_shift"][0])
    put("w0", inp["w0"][0])
    put("a0", inp["a0"][0])
    put("kk", inp["k_k"][0])
    put("ka", inp["k_a"][0])
    put("rk", inp["r_k"][0].reshape(-1))
    for j in range(4):
        put("cw%d" % j, inp["conv_w"][0][j])
    bc = np.zeros((128, NBC), np.float32)

    def putb(name, vec):
        o, w = BCS[name]
        bc[:, o:o + w] = np.asarray(vec, np.float32).reshape(1, w)

    putb("lnxw", inp["ln_x_w"][0])
    putb("lnxb", inp["ln_x_b"][0])
    putb("gng", inp["gdn_norm_g"][0])
    putb("fng", inp["final_norm_g"])
    putb("alog", inp["a_log"][0])
    putb("dtb", inp["dt_bias"][0])
    return cols, bc


def build(NA, NB, NS, CP=128, CS=16, dbg=None):
    nc = bass.Bass("TRN2", target_bir_lowering=False)
    P = Prog(nc)
    TA, TB, TS = max(NA, 1) * CP, max(NB, 1) * CP, max(NS, 1) * CS
    NS1 = max(NS, 1)

    def din(name, shape):
        return P.dram(name, shape, F32, kind="ExternalInput")

    def dout(name, shape):
        return P.dram(name, shape, F32, kind="ExternalOutput")

    xa = din("xa", [TA, D])
    xb = din("xb", [TB, D])
    pb = din("pb", [TB, D_PLE])
    xs = din("xs", [TS, D])
    ps = din("ps", [TS, D_PLE])
    st_shift = din("st_shift", [NS1, R_PROJ])
    st_wkv = din("st_wkv", [NS1, 8, 64, 64])
    st_conv = din("st_conv", [NS1, 3, 1536])
    st_gdn = din("st_gdn", [NS1, 4, 128, 128])
    w_in = din("w_in", [D, D_IN])
    w_lup = din("w_lup", [64, 512])
    a_lup = din("a_lup", [64, 512])
    g_lup = din("g_lup", [128, 512])
    w_out = din("w_out", [D, D])
    w_gate = din("w_gate", [D, D_FF])
    w_up = din("w_up", [D, D_FF])
    w_down = din("w_down", [D_FF, D])
    w_pg = din("w_pg", [D, D])
    w_pp = din("w_pp", [D_PLE, D])
    cols_d = din("cols", [128, NCOLS])
    bc_d = din("bc", [128, NBC])

    yb = dout("yb", [TB, D])
    ys = dout("ys", [TS, D])
    o_shift_p = dout("o_shift_p", [1, R_PROJ])
    o_wkv_p = dout("o_wkv_p", [1, 8, 64, 64])
    o_conv_p = dout("o_conv_p", [1, 3, 1536])
    o_gdn_p = dout("o_gdn_p", [1, 4, 128, 128])
    o_shift_s = dout("o_shift_s", [NS1, R_PROJ])
    o_wkv_s = dout("o_wkv_s", [NS1, 8, 64, 64])
    o_conv_s = dout("o_conv_s", [NS1, 3, 1536])
    o_gdn_s = dout("o_gdn_s", [NS1, 4, 128, 128])

    sc_gate = P.dram("sc_gate", [NHC, 128, 8, 128], BF16)
    sc_up = P.dram("sc_up", [NHC, 128, 8, 128], BF16)
    NG = NHC // 2
    sc_down = P.dram("sc_down", [NG, 128, 2, D], BF16)
    sc_out = P.dram("sc_out", [4, 128, 2, D], BF16)
    sc_pg = P.dram("sc_pg", [4, 128, 2, D], BF16)
    sc_pp = P.dram("sc_pp", [1, 128, 2, D], BF16)

    P.banks = [P.psum("bank%d" % i, [128, 512], F32) for i in range(8)]
    import os
    _pl = [int(v) for v in os.environ.get("POOLS", "1,3,4").split(",")]
    P.pools = {"f": P.banks[0:_pl[0]], "c": P.banks[_pl[0]:_pl[0] + _pl[1]], "m": P.banks[_pl[0] + _pl[1]:8]}
    P.pi = {"c": 0, "m": 0, "f": 0}
    SEQ_STREAM = len(P.pools["m"]) < 4
    XQ = os.environ.get("XQ", "act")
    P.cur_pool = "c"

    win = P.sbuf("win", [128, 8, D_IN], BF16)
    lor = P.sbuf("lor", [128, 512], BF16)
    gup = P.sbuf("gup", [128, 512], BF16)
    cols = P.sbuf("cols", [128, NCOLS], F32)
    bcs = P.sbuf("bcs", [128, NBC], F32)
    der = P.sbuf("der", [128, 32], F32)
    identb = P.sbuf("identb", [128, 128], BF16)
    identf = P.sbuf("identf", [128, 128], F32)
    mTs = P.sbuf("mTs", [128, 128], F32)
    mTi = P.sbuf("mTi", [128, 128], F32)
    mLs = P.sbuf("mLs", [128, 128], F32)
    ones = P.sbuf("ones", [128, 128], F32)
    blk = P.sbuf("blk", [128, 2], F32)
    pw = P.sbuf("pw", [128, 16], F32)
    nea = P.sbuf("nea", [128, 4], F32)

    def col(name, i=None, n=None):
        o, w = COLS[name]
        if i is None:
            return cols[:, o:o + w]
        return cols[:, o + i:o + i + (n or 1)]

    def bcv(name):
        o, w = BCS[name]
        return bcs[:, o:o + w]

    for i in range(8):
        P.dma("pool", win[:, i, :], w_in[i * 128:(i + 1) * 128, :])
    P.dma("pool", lor[0:64, :], w_lup.v())
    P.dma("pool", lor[64:128, :], a_lup.v())
    P.dma("pool", gup.v(), g_lup.v())
    P.dma("sp", cols.v(), cols_d.v())
    P.dma("sp", bcs.v(), bc_d.v())
    for g in range(4):
        P.dma("pool", sc_out[g], w_out[g * 256:(g + 1) * 256, :].re("(c p) n -> p c n", p=128), key=sc_out)
    for g in range(NHC):
        P.dma("pool", sc_gate[g], w_gate[:, g * 128:(g + 1) * 128].re("(kc p) n -> p kc n", p=128), key=sc_gate)
        P.dma("pool", sc_up[g], w_up[:, g * 128:(g + 1) * 128].re("(kc p) n -> p kc n", p=128), key=sc_up)
    for g in range(NG):
        P.dma("pool", sc_down[g], w_down[g * 256:(g + 1) * 256, :].re("(c p) n -> p c n", p=128), key=sc_down)
    for g in range(4):
        P.dma("pool", sc_pg[g], w_pg[g * 256:(g + 1) * 256, :].re("(c p) n -> p c n", p=128), key=sc_pg)
    P.dma("pool", sc_pp[0], w_pp.re("(c p) n -> p c n", p=128), key=sc_pp)

    def aff(tile, pattern, cmp, cm):
        a = tile.ap
        P.memset("pool", tile.v(), 1.0)
        P.generic("pool", lambda e: e.affine_select(out=a, in_=a, pattern=pattern, compare_op=cmp, fill=0.0,
                                                     base=0, channel_multiplier=cm), [tile], [tile])

    bd32 = P.sbuf("bd32", [128, 128], BF16)
    off32 = P.sbuf("off32", [128, 128], BF16)
    off64 = P.sbuf("off64", [128, 128], BF16)

    def blockdiag(tile, bs):
        a = tile.ap.rearrange("p (b c) -> p b c", c=bs)
        nb = 128 // bs
        P.memset("pool", tile.v(), 1.0)
        P.generic("pool", lambda e: e.affine_select(out=a, in_=a, pattern=[[-bs, nb], [0, bs]], compare_op=ALU.is_ge,
                                                     fill=0.0, base=0, channel_multiplier=1), [tile], [tile])
        P.generic("pool", lambda e: e.affine_select(out=a, in_=a, pattern=[[bs, nb], [0, bs]], compare_op=ALU.is_ge,
                                                     fill=0.0, base=bs - 1, channel_multiplier=-1), [tile], [tile])

    blockdiag(bd32, 32)
    blockdiag(off64, 64)
    P.tt("pool", off32.v(), off64.v(), bd32.v(), ALU.subtract)
    P.ts("pool", off64.v(), off64.v(), -1.0, ALU.mult, 1.0, ALU.add)
    aff(identb, [[-1, 128]], ALU.is_equal, 1)
    aff(identf, [[-1, 128]], ALU.is_equal, 1)
    aff(mTs, [[1, 128]], ALU.is_gt, -1)
    aff(mTi, [[1, 128]], ALU.is_ge, -1)
    aff(mLs, [[-1, 128]], ALU.is_gt, 1)
    P.memset("dve", ones.v(), 1.0)
    P.memset("dve", blk.v(), 0.0)
    P.memset("dve", blk[0:64, 0:1], 1.0)
    P.memset("dve", blk[64:128, 1:2], 1.0)
    P.memset("dve", pw[:, 0:8], -0.5)
    P.memset("dve", pw[:, 8:16], 0.5)
    P.ts("dve", der[:, 0:14], col("mu"), -1.0, ALU.mult, 1.0, ALU.add)
    P.ts("dve", der[:, 14:18], col("w0"), 0.5, ALU.mult)
    P.ts("dve", der[:, 18:22], col("a0"), 0.5, ALU.mult)
    P.act(nea.v(), bcv("alog"), AF.Exp)
    P.ts("dve", nea.v(), nea.v(), -1.0, ALU.mult)

    A = P.sbuf("A", [128, 4, 128], F32)
    Abf = P.sbuf("Abf", [128, 4, 128], BF16)
    S = P.sbuf("S", [128, 4, 128], F32)
    Sbf = P.sbuf("Sbf", [128, 4, 128], BF16)
    fcar = P.sbuf("fcar", [128, 14, 1], F32)
    qcar = P.sbuf("qcar", [128, 12, 3], F32)

    NSL = 2
    gsl = [P.sbuf("gsl%d" % i, [128, 8, 128], BF16) for i in range(NSL)]
    usl = [P.sbuf("usl%d" % i, [128, 8, 128], BF16) for i in range(NSL)]
    dsl = [P.sbuf("dsl%d" % i, [128, 2, D], BF16) for i in range(NSL)]
    dsi = [0]
    TM = 2 * CP
    NXQ = int(os.environ.get("NXQ", "3"))
    xq = [P.sbuf("xq%d" % i, [128, D], F32) for i in range(NXQ)]
    mixTm = P.sbuf("mixTm", [128, 8, TM], BF16)
    h2T = mixTm
    actT = P.sbuf("actT", [128, NHC, TM], BF16)
    sgb = [P.sbuf("sgb%d" % i, [128, TM], BF16) for i in range(2)]

    stats = {"c": [P.sbuf("st%d" % i, [128, 16], F32) for i in range(6)],
             "m": [P.sbuf("stm%d" % i, [128, 16], F32) for i in range(4)]}
    sti = {"c": 0, "m": 0}

    def stat():
        k = "m" if P.cur_pool == "m" else "c"
        sti[k] += 1
        return stats[k][sti[k] % len(stats[k])]

    xn_c = P.sbuf("xn", [128, D], BF16)
    hT = P.sbuf("hT", [128, 8, CP], BF16)
    sioA = P.sbuf("sioA", [128, 128], F32)
    sioB = P.sbuf("sioB", [128, 128], F32)
    sioW = P.sbuf("sioW", [128, 512], F32)
    sioC = P.sbuf("sioC", [128, 36], F32)
    mpt = P.sbuf("mpt", [128, D_PLE], F32)
    mpbf = P.sbuf("mpbf", [128, D_PLE], BF16)

    rem = nc.sbuf_bytes_remaining
    arena = Arena(P, "ctx", (rem - 256) // 64 * 64)
    print("arena bytes", arena.nbytes)

    xn_m = actT.re("p a b -> p (a b)")[:, 4096 + 2 * TM:4096 + 2 * TM + D]

    def norm_T(src, Cc, dstT, toff, gname, pool=None):
        st = stat()
        xn = xn_c if P.cur_pool != "m" else xn_m
        P.act(xn[:Cc, :], src, AF.Square, accum=st[:Cc, 0:1])
        P.ts("dve", st[:Cc, 1:2], st[:Cc, 0:1], 1.0 / D, ALU.mult, 1e-6, ALU.add)
        P.tt("pool", st[:Cc, 2:3], st[:Cc, 1:2], pw[:Cc, 0:1], ALU.pow)
        P.ts("dve", xn[:Cc, :], src, st[:Cc, 2:3], ALU.mult)
        bk = P.bank(pool)
        bb = bk.v().bitcast(BF16)
        for kc in range(8):
            P.tr(bb[:, kc * Cc:(kc + 1) * Cc], xn[:Cc, kc * 128:(kc + 1) * 128], identb[:Cc, :Cc])
        P.tt("dve", dstT[:, :, toff:toff + Cc], bb[:, 0:8 * Cc].re("p (k c) -> p k c", c=Cc),
             col(gname).re("p (k o) -> p k o", o=1).bc([128, 8, Cc]), ALU.mult)

    class Ctx:
        pass

    def make_ctx(Cc, tag):
        c = Ctx()
        c.C = Cc
        arena.off = 0
        s = lambda n, sh, dt=F32: arena.take(n + tag, sh, dt)
        c.fb = s("fb", [128, 14, Cc + 1])
        c.qb = s("qb", [128, 12, Cc + 3])
        c.lw = s("lw", [128, Cc], BF16)
        c.sg = s("sg", [128, Cc], BF16)
        c.cum = s("cum", [128, 4, Cc + 1])
        c.sm = s("sm", [128, 16])
        c.art = s("art", [128, 4, 2, Cc], BF16)
        c.br = s("br", [128, 4, Cc], BF16)
        c.kt = s("kt", [128, 4, Cc], BF16)
        c.Bhf = s("Bhf", [128, 4, Cc], BF16)
        c.Khf = s("Khf", [128, 4, Cc], BF16)
        c.vbf = s("vbf", [128, 4, Cc], BF16)
        c.kg = s("kg", [128, 4, Cc], BF16)
        c.qg = s("qg", [128, 4, Cc], BF16)
        c.kq = s("kq", [128, 4, 2, Cc], BF16)
        c.vgb = s("vgb", [128, 4, Cc], BF16)
        c.BK = s("BK", [Cc, 1024], BF16)
        c.Vt = s("Vt", [Cc, 512], BF16)
        c.Vg = s("Vg", [Cc, 512], BF16)
        c.kd = s("kd", [Cc, 512], BF16)
        c.tk = s("tk", [Cc, 80])
        c.eGl = s("eGl", [128, 4])
        c.zs = s("zs", [Cc, 512], BF16)
        c.gt = s("gt", [Cc, 512], BF16)
        off0 = arena.off
        c.fm = s("fm", [128, 14, Cc])
        c.t2 = s("t2", [128, 14, Cc])
        c.acc = s("acc", [128, 12, Cc])
        c.tw = s("tw", [128, 4, Cc])
        c.ta = s("ta", [128, 4, Cc])
        c.Wa = s("Wa", [128, 4, Cc])
        c.Wb = s("Wb", [128, 4, Cc])
        offZ = arena.off
        c.rhsG = s("rhsG", [Cc, 4, Cc])
        c.decT = s("decT", [Cc, 4, Cc])
        c.EG = s("EG", [128, 4, Cc])
        c.zset = [c.rhsG, c.decT, c.EG]
        arena.off = offZ
        c.kk = s("kk", [128, 4, Cc])
        c.ka = s("ka", [128, 4, Cc])
        c.keff = s("keff", [128, 4, Cc])
        c.alpha = [c.fm, c.t2, c.acc, c.tw, c.ta, c.Wa, c.Wb, c.kk, c.ka, c.keff]
        c.kset = [c.kk, c.ka, c.keff]
        arena.off = off0
        c.MakT = s("MakT", [Cc, 8, Cc], BF16)
        c.NrbT = s("NrbT", [Cc, 8, Cc], BF16)
        c.NrkT = s("NrkT", [Cc, 8, Cc], BF16)
        c.QKT = s("QKT", [Cc, 4, Cc], BF16)
        c.gh = max(1, min(12, 512 // Cc))
        c.ngr = (12 + c.gh - 1) // c.gh
        c.Pm = [s("Pm%d" % g, [Cc, c.gh, Cc], BF16) for g in range(c.ngr)]
        c.PT = [s("PT%d" % g, [Cc, c.gh, Cc], BF16) for g in range(c.ngr)]
        c.Rm = [s("Rm%d" % g, [Cc, c.gh, Cc], BF16) for g in range(c.ngr)]
        offC = arena.off
        c.Zb = s("Zb", [Cc, 512], BF16)
        c.Rh = s("Rh", [Cc, 512], BF16)
        c.Ub = s("Ub", [Cc, 512], BF16)
        c.Xb = s("Xb", [Cc, 512], BF16)
        c.mix = s("mix", [Cc, D], BF16)
        offE = arena.off
        c.chainset = [c.Zb, c.Rh, c.Ub, c.Xb, c.mix]
        c.moffset = []
        if Cc > 32:
            arena.off = offC
            c.Mo32 = [s("Mo32_%d" % g, [Cc, c.gh, Cc], BF16) for g in range(c.ngr)]
            c.Mo64 = [s("Mo64_%d" % g, [Cc, c.gh, Cc], BF16) for g in range(c.ngr)]
            c.moffset = c.Mo32 + c.Mo64
            assert arena.off <= offE
            arena.off = offE
        c.w5 = [s("w5_%d" % i, [Cc, 512]) for i in range(3)]
        c.beta = [c.MakT, c.NrbT, c.NrkT, c.QKT, c.Zb, c.Rh, c.Ub, c.Xb, c.mix] + c.Pm + c.PT + c.Rm + c.w5 + c.moffset
        return c

    def seq_init_zero(c):
        P.memset("pool", A.v(), 0.0)
        P.memset("pool", Abf.v(), 0.0)
        P.memset("pool", S.v(), 0.0)
        P.memset("pool", Sbf.v(), 0.0)
        P.memset("pool", fcar.v(), 0.0)
        P.memset("pool", qcar.v(), 0.0)
        P.memset("pool", c.cum[:, :, 0:1], 0.0)

    def chunk(c, xt, full, mixT_dst=None):
        Cc = c.C
        fb, qb = c.fb, c.qb
        tk = c.tk
        P.handoff(c.beta + c.zset, c.alpha)

        norm_T(xt[:Cc, :], Cc, hT, 0, "gmix", "f")

        P.copy("pool", fb[:, :, 0:1], fcar.v())
        P.copy("pool", qb[:, :, 0:3], qcar.v())

        def proj(tiles, dst, doff):
            for g0 in range(0, len(tiles), 4):
                grp = tiles[g0:g0 + 4]
                bk = P.bank("f")
                for i, t in enumerate(grp):
                    for kc in range(8):
                        P.mm(bk[:, i * Cc:(i + 1) * Cc], win[:, kc, t * 128:(t + 1) * 128], hT[:, kc, :Cc],
                             start=(kc == 0), stop=(kc == 7))
                n = len(grp)
                t0 = grp[0] - tiles[0]
                P.copy("act", dst[:, t0:t0 + n, doff:doff + Cc],
                       bk[:, 0:n * Cc].re("p (t c) -> p t c", c=Cc))

        proj(list(range(0, 14)), fb, 1)
        proj(list(range(14, 26)), qb, 3)
        bkb = P.bank("f")
        for kc in range(8):
            P.mm(bkb[:Cc, 0:8], hT[:, kc, :Cc], win[:, kc, OFF_B:OFF_B + 8], start=(kc == 0), stop=(kc == 7))
        P.copy("act", tk[:, 64:72], bkb[:Cc, 0:8])
        if full:
            bkz = P.bank("f")
            for kc in range(8):
                P.mm(bkz[:Cc, 0:512], hT[:, kc, :Cc], win[:, kc, OFF_Z:OFF_Z + 512], start=(kc == 0), stop=(kc == 7))
            P.act(c.zs.v(), bkz[:Cc, 0:512], AF.Silu)

        mu_bc = col("mu").re("p (t o) -> p t o", o=1).bc([128, 14, Cc])
        omu_bc = der[:, 0:14].re("p (t o) -> p t o", o=1).bc([128, 14, Cc])
        P.tt("pool", c.fm.v(), fb[:, :, 0:Cc], mu_bc, ALU.mult)
        P.tt("pool", c.t2.v(), fb[:, :, 1:Cc + 1], omu_bc, ALU.mult)
        P.tt("dve", c.fm.v(), c.fm.v(), c.t2.v(), ALU.add)
        P.copy("pool", fcar.v(), fb[:, :, Cc:Cc + 1])
        fm = c.fm
        r_, k_, v_ = fm[:, 0:4, :], fm[:, 4:8, :], fm[:, 8:12, :]

        def cwbc(j):
            return col("cw%d" % j).re("p (t o) -> p t o", o=1).bc([128, 12, Cc])
        tmp = c.t2[:, 0:12, :]
        P.tt("pool", c.acc.v(), qb[:, :, 0:Cc], cwbc(0), ALU.mult)
        for j in range(1, 4):
            P.tt("pool", tmp, qb[:, :, j:Cc + j], cwbc(j), ALU.mult)
            P.tt("dve", c.acc.v(), c.acc.v(), tmp, ALU.add)
        P.copy("pool", qcar.v(), qb[:, :, Cc:Cc + 3])

        qs = c.acc
        P.act(qs.v(), c.acc.v(), AF.Silu)
        P.act(c.lw[0:64, :], fm[0:64, 12, :], AF.Tanh)
        P.copy("pool", c.lw[64:128, :], fm[64:128, 12, :])
        if full:
            P.act(c.tw[:, 0, :], fm[:, 13, :], AF.Tanh, scale=0.5)
            P.ts("dve", c.sg.v(), c.tw[:, 0, :], 0.5, ALU.mult, 0.5, ALU.add)
        bkw, bka = P.bank(), P.bank()
        for t in range(4):
            P.mm(bkw[:, t * Cc:(t + 1) * Cc], lor[0:64, t * 128:(t + 1) * 128], c.lw[0:64, :])
        for t in range(4):
            P.mm(bka[:, t * Cc:(t + 1) * Cc], lor[64:128, t * 128:(t + 1) * 128], c.lw[64:128, :])
        if full:
            bkg = P.bank()
            P.mm(bkg[:Cc, 0:512], c.sg.v(), gup.v())
            P.copy("act", c.gt.v(), bkg[:Cc, 0:512])
        for t in range(4):
            P.act(c.tw[:, t, :], bkw[:, t * Cc:(t + 1) * Cc], AF.Tanh, bias=der[:, 14 + t:15 + t], scale=0.5)
        for t in range(4):
            P.act(c.ta[:, t, :], bka[:, t * Cc:(t + 1) * Cc], AF.Tanh, bias=der[:, 18 + t:19 + t], scale=0.5)
        P.act(tk[:, 0:4], tk[:, 64:68], AF.Tanh, scale=0.5)
        P.ts("dve", tk[:, 0:4], tk[:, 0:4], 0.5, ALU.mult, 0.5, ALU.add)
        P.tt("dve", tk[:, 4:8], tk[:, 68:72], bcv("dtb")[:Cc, :], ALU.add)

        P.ts("dve", c.tw.v(), c.tw.v(), 1.0, ALU.add)
        for t in range(4):
            P.scan(c.cum[:, t, 1:Cc + 1], ones[:, 0:Cc], c.tw[:, t, :])
        P.ts("dve", c.ta.v(), c.ta.v(), 0.5, ALU.mult, 0.5, ALU.add)
        a_ = c.ta
        P.ts("dve", c.sm[:, 0:4], c.cum[:, :, Cc:Cc + 1].re("p t o -> p (t o)"), -C0, ALU.mult)
        P.act(c.sm[:, 4:8], c.sm[:, 0:4], AF.Exp)
        P.act(tk[:, 8:12], tk[:, 4:8], AF.Exp)
        sp_ = stat()
        u_, L_, t_, m_ = tk[:, 8:12], sp_[:Cc, 0:4], sp_[:Cc, 4:8], sp_[:Cc, 8:12]
        P.act(L_, u_, AF.Ln, bias=1.0)
        P.ts("dve", t_, u_, -0.2, ALU.mult, 0.25, ALU.add)
        for cst in (1.0 / 3.0, 0.5, 1.0):
            P.tt("dve", t_, t_, u_, ALU.mult)
            P.ts("dve", t_, t_, -1.0, ALU.mult, cst, ALU.add)
        P.tt("dve", t_, t_, u_, ALU.mult)
        P.ts("dve", m_, u_, 0.1, ALU.is_lt)
        P.tt("dve", t_, t_, L_, ALU.subtract)
        P.tt("dve", t_, t_, m_, ALU.mult)
        P.tt("dve", tk[:, 8:12], L_, t_, ALU.add)
        P.tt("dve", tk[:, 12:16], tk[:, 8:12], nea[:Cc, :], ALU.mult)
        glog = tk[:, 12:16]

        def cbc(name):
            return col(name).re("p (t o) -> p t o", o=1).bc([128, 4, Cc])
        e1 = c.tw
        P.tt("pool", c.kk.v(), k_, cbc("kk"), ALU.mult)
        P.tt("pool", c.ka.v(), c.kk.v(), a_.v(), ALU.mult)
        for t in range(4):
            P.ts("pool", e1[:, t, :], a_[:, t, :], -1.0, ALU.add, col("ka", t), ALU.mult)
        P.stt(c.keff.v(), e1.v(), 1.0, k_, ALU.add, ALU.mult)
        P.act(c.Wa.v(), c.cum[:, :, 0:Cc], AF.Exp, scale=-C0)
        P.tt("dve", c.art[:, :, 0, :], c.kk.v(), c.Wa.v(), ALU.mult)
        P.act(c.Wb.v(), c.cum[:, :, 1:Cc + 1], AF.Exp, scale=C0)
        P.stt(c.br.v(), c.ka.v(), -1.0, c.Wb.v(), ALU.mult, ALU.mult)
        P.tt("dve", c.kt.v(), c.keff.v(), c.Wb.v(), ALU.mult)
        for t in range(4):
            P.act(c.Wa[:, t, :], c.cum[:, t, 1:Cc + 1], AF.Exp, bias=c.sm[:, t:t + 1], scale=C0)
        P.stt(c.Bhf.v(), c.ka.v(), -1.0, c.Wa.v(), ALU.mult, ALU.mult)
        P.tt("pool", c.Khf.v(), c.keff.v(), c.Wa.v(), ALU.mult)
        if full:
            P.act(c.Wb.v(), c.cum[:, :, 1:Cc + 1], AF.Exp, scale=-C0)
            P.tt("dve", c.art[:, :, 1, :], r_, c.Wb.v(), ALU.mult)
        P.copy("pool", c.vbf.v(), v_)
        P.tt("pool", c.kk.v(), c.kk.v(), c.kk.v(), ALU.mult)
        bks = P.bank()
        for t in range(4):
            P.mm(bks[:Cc, 2 * t:2 * t + 2], c.kk[:, t, :], blk.v())
        if full:
            P.tt("pool", e1.v(), r_, cbc("rk"), ALU.mult)
            P.tt("pool", e1.v(), e1.v(), c.keff.v(), ALU.mult)
            for t in range(4):
                P.mm(bks[:Cc, 8 + 2 * t:10 + 2 * t], e1[:, t, :], blk.v())
        sq8 = c.t2[:, 0:8, :]
        P.tt("pool", sq8, qs[:, 0:8, :], qs[:, 0:8, :], ALU.mult)
        for t in range(8):
            P.mm(bks[:Cc, 16 + t:17 + t], c.t2[:, t, :], ones[:, 0:1])
        P.mm(bks[:Cc, 24:28], mLs[:Cc, :Cc], glog)
        P.mm(bks[:, 28:32], ones[:Cc, :], glog)
        P.ts("dve", tk[:, 16:24], bks[:Cc, 0:8], 1e-12, ALU.add)
        P.recip(tk[:, 16:24], tk[:, 16:24])
        s2 = tk[:, 16:24]
        if full:
            P.copy("dve", tk[:, 24:32], bks[:Cc, 8:16])
        P.ts("dve", tk[:, 32:40], bks[:Cc, 16:24], 1e-6, ALU.add)
        P.tt("pool", tk[:, 40:48], tk[:, 32:40], pw[:Cc, 0:8], ALU.pow)
        P.tt("pool", tk[:, 48:52], tk[:, 36:40], pw[:Cc, 8:12], ALU.pow)
        invs = tk[:, 48:52]
        P.recip(tk[:, 52:56], tk[:, 36:40])
        P.tt("dve", tk[:, 52:56], tk[:, 52:56], tk[:, 0:4], ALU.mult)
        c1 = tk[:, 52:56]
        P.ts("dve", tk[:, 56:60], tk[:, 52:56], -1.0, ALU.mult)
        nc1 = tk[:, 56:60]
        P.ts("dve", tk[:, 60:64], tk[:, 40:44], 128.0 ** -0.5, ALU.mult)
        sqn = tk[:, 60:64]
        st = stat()
        P.act(st[:Cc, 0:4], bks[:Cc, 24:28], AF.Exp)
        eGrem = st[:Cc, 0:4]
        P.act(c.eGl.v(), bks[:, 28:32], AF.Exp)

        P.handoff(c.kset, c.zset)
        P.tt("dve", c.rhsG.v(), mTi[:Cc, :Cc].re("p (o c) -> p o c", o=1).bc([Cc, 4, Cc]),
             glog.re("p (h o) -> p h o", o=1).bc([Cc, 4, Cc]), ALU.mult)
        rg = c.rhsG.re("p h c -> p (h c)")
        bkG, bkB = P.bank(), P.bank()
        P.mm(bkG[:Cc, 0:4 * Cc], mLs[:Cc, :Cc], rg)
        P.mm(bkB[:, 0:4 * Cc], ones[:Cc, :], rg)
        P.act(c.decT.re("p h c -> p (h c)"), bkG[:Cc, 0:4 * Cc], AF.Exp)
        P.act(c.EG.re("p h c -> p (h c)"), bkB[:, 0:4 * Cc], AF.Exp)
        dms, dmi = c.rhsG, c.decT
        P.tt("pool", dms.v(), c.decT.v(), mTs[:Cc, :Cc].re("p (o c) -> p o c", o=1).bc([Cc, 4, Cc]), ALU.mult)
        if full:
            P.tt("pool", dmi.v(), c.decT.v(), mTi[:Cc, :Cc].re("p (o c) -> p o c", o=1).bc([Cc, 4, Cc]), ALU.mult)
        P.tt("dve", c.kg.v(), qs[:, 4:8, :], c.EG.v(), ALU.mult)
        if full:
            P.tt("dve", c.qg.v(), qs[:, 0:4, :], c.EG.v(), ALU.mult)
            P.copy("pool", c.kq[:, :, 1, :], qs[:, 0:4, :])
        P.copy("pool", c.kq[:, :, 0, :], qs[:, 4:8, :])
        P.copy("pool", c.vgb.v(), qs[:, 8:12, :])

        bt1 = P.bank()
        b1 = bt1.v().bitcast(BF16)
        for t in range(4):
            P.tr(b1[:Cc, t * 128:(t + 1) * 128], c.Bhf[:, t, :], identb.v())
        for t in range(4):
            P.tr(b1[:Cc, 512 + t * 128:512 + (t + 1) * 128], c.Khf[:, t, :], identb.v())
        P.copy("act", c.BK.v(), b1[:Cc, 0:1024])
        bt2 = P.bank()
        b2 = bt2.v().bitcast(BF16)
        for t in range(4):
            P.tr(b2[:Cc, t * 128:(t + 1) * 128], c.vbf[:, t, :], identb.v())
        for t in range(4):
            P.tr(b2[:Cc, 512 + t * 128:512 + (t + 1) * 128], c.vgb[:, t, :], identb.v())
        P.copy("act", c.Vt.v(), b2[:Cc, 0:512])
        P.copy("act", c.Vg.v(), b2[:Cc, 512:1024])
        bt3 = P.bank()
        b3 = bt3.v().bitcast(BF16)
        for t in range(4):
            P.tr(b3[:Cc, t * 128:(t + 1) * 128], c.kq[:, t, 0, :], identb.v())
        for h in range(4):
            P.ts("dve", c.kd[:, h * 128:(h + 1) * 128], b3[:Cc, h * 128:(h + 1) * 128], eGrem[:, h:h + 1], ALU.mult)

        P.handoff(c.alpha, c.beta)
        NW = 2 if full else 1
        P0 = c.Pm

        def hsl(tiles, h):
            return tiles[h // c.gh][:, h % c.gh, :]

        for h in range(8):
            t, hp = h // 2, (h % 2) * 64
            bk = P.bank()
            rhs = c.art[hp:hp + 64, t, 0:NW, :].re("p w c -> p (w c)")
            P.mm(bk[:Cc, 0:NW * Cc], c.br[hp:hp + 64, t, :], rhs)
            P.mm(bk[:Cc, 256:256 + NW * Cc], c.kt[hp:hp + 64, t, :], rhs)
            P.stt(hsl(P0, h), bk[:Cc, 0:Cc], s2[:, h:h + 1], mTs[:Cc, :Cc], ALU.mult, ALU.mult)
            P.tt("dve", c.MakT[:, h, :], bk[:Cc, 256:256 + Cc], mTs[:Cc, :Cc], ALU.mult)
            if full:
                P.tt("dve", c.NrbT[:, h, :], bk[:Cc, Cc:2 * Cc], mTi[:Cc, :Cc], ALU.mult)
                P.tt("dve", c.NrkT[:, h, :], bk[:Cc, 256 + Cc:256 + 2 * Cc], mTi[:Cc, :Cc], ALU.mult)
        for h in range(4):
            if h % 2 == 0:
                bk = P.bank()
            o = (h % 2) * 256
            P.mm(bk[:Cc, o:o + NW * Cc], c.kq[:, h, 0, :], c.kq[:, h, 0:NW, :].re("p w c -> p (w c)"))
            P.stt(hsl(P0, 8 + h), bk[:Cc, o:o + Cc], nc1[:, h:h + 1], dms[:, h, :], ALU.mult, ALU.mult)
            if full:
                P.tt("dve", c.QKT[:, h, :], bk[:Cc, o + Cc:o + 2 * Cc], dmi[:, h, :], ALU.mult)

        gh, ngr = c.gh, c.ngr
        heads = [list(range(g * gh, min(12, (g + 1) * gh))) for g in range(ngr)]
        hier = Cc > 32

        def hb(m, n):
            return m[:Cc, :Cc].re("p (o c) -> p o c", o=1).bc([Cc, n, Cc])

        def transp(src, dst, g):
            bk = P.bank()
            bb = bk.v().bitcast(BF16)
            n = len(heads[g])
            for i in range(n):
                P.tr(bb[:Cc, i * Cc:(i + 1) * Cc], src[g][:, i, :], identb[:Cc, :Cc])
            P.copy("act", dst[g][:, 0:n, :], bb[:Cc, 0:n * Cc].re("p (h c) -> p h c", c=Cc))

        if hier:
            P.handoff(c.chainset, c.moffset)
        for g in range(ngr):
            n = len(heads[g])
            transp(c.Pm, c.PT, g)
            if hier:
                P.tt("pool", c.Mo32[g][:, 0:n, :], c.PT[g][:, 0:n, :], hb(off32, n), ALU.mult)
                P.tt("pool", c.Mo64[g][:, 0:n, :], c.PT[g][:, 0:n, :], hb(off64, n), ALU.mult)
                P.tt("pool", c.PT[g][:, 0:n, :], c.PT[g][:, 0:n, :], hb(bd32, n), ALU.mult)
                P.tt("dve", c.Pm[g][:, 0:n, :], c.Pm[g][:, 0:n, :], hb(bd32, n), ALU.mult)
            P.tt("pool", c.Rm[g][:, 0:n, :], c.Pm[g][:, 0:n, :], hb(identb, n), ALU.add)
        L = int(np.log2(min(Cc, 32))) - 1
        for lv in range(1, L + 1):
            lastl = lv == L
            for g in range(ngr):
                n = len(heads[g])
                if not lastl:
                    bkP = P.bank()
                    for i in range(n):
                        P.mm(bkP[:Cc, i * Cc:(i + 1) * Cc], c.PT[g][:, i, :], c.Pm[g][:, i, :])
                bkT = P.bank()
                for i in range(n):
                    P.mm(bkT[:Cc, i * Cc:(i + 1) * Cc], c.Pm[g][:, i, :], c.PT[g][:, i, :])
                P.copy("act", c.PT[g][:, 0:n, :], bkT[:Cc, 0:n * Cc].re("p (h c) -> p h c", c=Cc))
                if not lastl:
                    P.copy("act", c.Pm[g][:, 0:n, :], bkP[:Cc, 0:n * Cc].re("p (h c) -> p h c", c=Cc))
                bkR = P.bank()
                for i in range(n):
                    P.mm(bkR[:Cc, i * Cc:(i + 1) * Cc], c.PT[g][:, i, :], c.Rm[g][:, i, :])
                P.tt("dve", c.Rm[g][:, 0:n, :], bkR[:Cc, 0:n * Cc].re("p (h c) -> p h c", c=Cc),
                     c.Rm[g][:, 0:n, :], ALU.add)
        if hier:
            for Mo in (c.Mo32, c.Mo64):
                for g in range(ngr):
                    n = len(heads[g])
                    transp(c.Rm, c.PT, g)
                    bkX = P.bank()
                    for i in range(n):
                        P.mm(bkX[:Cc, i * Cc:(i + 1) * Cc], Mo[g][:, i, :], c.Rm[g][:, i, :])
                    P.copy("act", c.Pm[g][:, 0:n, :], bkX[:Cc, 0:n * Cc].re("p (h c) -> p h c", c=Cc))
                    bkY_ = P.bank()
                    for i in range(n):
                        P.mm(bkY_[:Cc, i * Cc:(i + 1) * Cc], c.PT[g][:, i, :], c.Pm[g][:, i, :])
                    P.tt("dve", c.Rm[g][:, 0:n, :], bkY_[:Cc, 0:n * Cc].re("p (h c) -> p h c", c=Cc),
                         c.Rm[g][:, 0:n, :], ALU.add)
            P.handoff(c.moffset, c.chainset)
        Tt = c.Rm

        bkZ, bkK = P.bank(), P.bank()
        for h in range(8):
            t, hp = h // 2, (h % 2) * 64
            P.mm(bkZ[:Cc, h * 64:(h + 1) * 64], c.art[hp:hp + 64, t, 0, :], Abf[hp:hp + 64, t, hp:hp + 64],
                 start=True, stop=False)
            P.mm(bkZ[:Cc, h * 64:(h + 1) * 64], c.MakT[:, h, :], c.Vt[:, h * 64:(h + 1) * 64],
                 start=False, stop=True)
        for h in range(4):
            P.mm(bkK[:Cc, h * 128:(h + 1) * 128], c.kg[:, h, :], Sbf[:, h, :])
        P.copy("act", c.Zb.v(), bkZ[:Cc, 0:512])
        for h in range(4):
            P.stt(c.Rh[:, h * 128:(h + 1) * 128], c.Vg[:, h * 128:(h + 1) * 128], invs[:, h:h + 1],
                  bkK[:Cc, h * 128:(h + 1) * 128], ALU.mult, ALU.subtract)
        bkU, bkY = P.bank(), P.bank()
        for h in range(8):
            P.mm(bkU[:Cc, h * 64:(h + 1) * 64], hsl(Tt, h), c.Zb[:, h * 64:(h + 1) * 64])
        for h in range(4):
            P.mm(bkY[:Cc, h * 128:(h + 1) * 128], hsl(Tt, 8 + h), c.Rh[:, h * 128:(h + 1) * 128])
        P.tt("dve", c.Ub.re("p (h v) -> p h v", v=64), bkU[:Cc, 0:512].re("p (h v) -> p h v", v=64),
             s2.re("p (h o) -> p h o", o=1).bc([Cc, 8, 64]), ALU.mult)
        P.tt("dve", c.Xb.re("p (h v) -> p h v", v=128), bkY[:Cc, 0:512].re("p (h v) -> p h v", v=128),
             c1.re("p (h o) -> p h o", o=1).bc([Cc, 4, 128]), ALU.mult)
        if full:
            bkO, bkYr = P.bank(), P.bank()
            for h in range(4):
                P.mm(bkO[:Cc, h * 128:(h + 1) * 128], c.qg[:, h, :], Sbf[:, h, :], start=True, stop=False)
                P.mm(bkO[:Cc, h * 128:(h + 1) * 128], c.QKT[:, h, :], c.Xb[:, h * 128:(h + 1) * 128],
                     start=False, stop=True)
            for h in range(8):
                t, hp = h // 2, (h % 2) * 64
                o_ = bkYr[:Cc, h * 64:(h + 1) * 64]
                P.mm(o_, c.art[hp:hp + 64, t, 1, :], Abf[hp:hp + 64, t, hp:hp + 64], start=True, stop=False)
                P.mm(o_, c.NrbT[:, h, :], c.Ub[:, h * 64:(h + 1) * 64], start=False, stop=False)
                P.mm(o_, c.NrkT[:, h, :], c.Vt[:, h * 64:(h + 1) * 64], start=False, stop=True)
            P.copy("act", c.w5[0].v(), bkYr[:Cc, 0:512])
            P.tt("dve", c.w5[1].re("p (h v) -> p h v", v=128), bkO[:Cc, 0:512].re("p (h v) -> p h v", v=128),
                 sqn.re("p (h o) -> p h o", o=1).bc([Cc, 4, 128]), ALU.mult)
        bkA, bkS = P.bank(), P.bank()
        for t in range(4):
            P.mm(bkA[:, t * 128:(t + 1) * 128], c.BK[:, t * 128:(t + 1) * 128], c.Ub[:, t * 128:(t + 1) * 128],
                 start=True, stop=False)
            P.mm(bkA[:, t * 128:(t + 1) * 128], c.BK[:, 512 + t * 128:512 + (t + 1) * 128],
                 c.Vt[:, t * 128:(t + 1) * 128], start=False, stop=True)
        for h in range(4):
            P.mm(bkS[:, h * 128:(h + 1) * 128], c.kd[:, h * 128:(h + 1) * 128], c.Xb[:, h * 128:(h + 1) * 128])
        for t in range(4):
            for hp in (0, 64):
                P.stt(A[hp:hp + 64, t, hp:hp + 64], A[hp:hp + 64, t, hp:hp + 64], c.sm[hp:hp + 64, 4 + t:5 + t],
                      bkA[hp:hp + 64, t * 128 + hp:t * 128 + hp + 64], ALU.mult, ALU.add)
        P.copy("pool", Abf[0:64, :, 0:64], A[0:64, :, 0:64])
        P.copy("pool", Abf[64:128, :, 64:128], A[64:128, :, 64:128])
        for h in range(4):
            P.stt(S[:, h, :], S[:, h, :], c.eGl[:, h:h + 1], bkS[:, h * 128:(h + 1) * 128], ALU.mult, ALU.add)
        P.copy("act", Sbf.v(), S.v())

        if not full:
            return
        y, ysq, w3 = c.w5[0], c.w5[2], c.w5[2]
        y3 = y.re("p (h v) -> p h v", v=64)
        st = stat()
        P.reduce(st[:Cc, 0:8], y3)
        P.tt("pool", ysq.v(), y.v(), y.v(), ALU.mult)
        P.reduce(st[:Cc, 8:16], ysq.re("p (h v) -> p h v", v=64))
        st2 = stat()
        P.ts("dve", st2[:Cc, 0:8], st[:Cc, 0:8], 1.0 / 64, ALU.mult)
        P.tt("dve", st2[:Cc, 8:16], st2[:Cc, 0:8], st2[:Cc, 0:8], ALU.mult)
        P.stt(st[:Cc, 8:16], st[:Cc, 8:16], 1.0 / 64, st2[:Cc, 8:16], ALU.mult, ALU.subtract)
        P.ts("dve", st[:Cc, 8:16], st[:Cc, 8:16], GN_EPS, ALU.add)
        P.tt("pool", st[:Cc, 0:8], st[:Cc, 8:16], pw[:Cc, 0:8], ALU.pow)
        P.tt("dve", y3, y3, st2[:Cc, 0:8].re("p (h o) -> p h o", o=1).bc([Cc, 8, 64]), ALU.subtract)
        P.tt("dve", y3, y3, st[:Cc, 0:8].re("p (h o) -> p h o", o=1).bc([Cc, 8, 64]), ALU.mult)
        P.tt("pool", y.v(), y.v(), bcv("lnxw")[:Cc, :], ALU.mult)
        P.tt("pool", y.v(), y.v(), bcv("lnxb")[:Cc, :], ALU.add)
        P.tt("dve", w3.re("p (h v) -> p h v", v=64), c.Vt.re("p (h v) -> p h v", v=64),
             tk[:, 24:32].re("p (h o) -> p h o", o=1).bc([Cc, 8, 64]), ALU.mult)
        P.tt("pool", y.v(), y.v(), w3.v(), ALU.add)
        P.tt("dve", c.mix[:, 0:512], y.v(), c.gt.v(), ALU.mult)
        o, osq = c.w5[1], c.w5[2]
        o3 = o.re("p (h v) -> p h v", v=128)
        P.tt("pool", osq.v(), o.v(), o.v(), ALU.mult)
        st = stat()
        P.reduce(st[:Cc, 0:4], osq.re("p (h v) -> p h v", v=128))
        P.ts("dve", st[:Cc, 0:4], st[:Cc, 0:4], 1.0 / 128, ALU.mult, 1e-6, ALU.add)
        P.tt("pool", st[:Cc, 4:8], st[:Cc, 0:4], pw[:Cc, 0:4], ALU.pow)
        P.tt("dve", o3, o3, st[:Cc, 4:8].re("p (h o) -> p h o", o=1).bc([Cc, 4, 128]), ALU.mult)
        P.tt("pool", o3, o3, bcv("gng")[:Cc, :].re("p (o v) -> p o v", o=1).bc([Cc, 4, 128]), ALU.mult)
        P.tt("dve", c.mix[:, 512:1024], o.v(), c.zs.v(), ALU.mult)
        bk = P.bank()
        bb = bk.v().bitcast(BF16)
        for kc in range(8):
            P.tr(bb[:, kc * Cc:(kc + 1) * Cc], c.mix[:, kc * 128:(kc + 1) * 128], identb[:Cc, :Cc])
        P.copy("act", mixT_dst, bb[:, 0:8 * Cc].re("p (k c) -> p k c", c=Cc))
        if dbg == "mix":
            P.copy("dve", xt[:Cc, :], c.mix.v())

    def state_out(c, o_shift, o_wkv, o_conv, o_gdn, b):
        bk = P.bank()
        P.tr(bk[0:14, 0:128], fcar.re("p t o -> p (t o)"), identf.v())
        P.copy("act", sioA[0:14, 0:128], bk[0:14, 0:128])
        P.dma("sp", o_shift[b].re("(t p) -> t p", p=128), sioA[0:14, 0:128], key=sioA)
        q36 = sioC
        P.copy("pool", q36[:, 0:36].re("p (r t) -> p t r", r=3), qcar.v())
        bk2 = P.bank()
        P.tr(bk2[0:36, 0:128], q36[:, 0:36], identf.v())
        P.copy("act", sioB[0:36, 0:128], bk2[0:36, 0:128])
        P.dma("sp", o_conv[b].re("r (t p) -> (r t) p", p=128), sioB[0:36, 0:128], key=sioB)
        for t in range(4):
            bk3 = P.bank()
            P.tr(bk3[:, 0:128], A[:, t, :], identf.v())
            P.copy("act", sioW[:, t * 128:(t + 1) * 128], bk3[:, 0:128])
        for t in range(4):
            for hl in range(2):
                P.dma("sp", o_wkv[b, 2 * t + hl],
                      sioW[hl * 64:(hl + 1) * 64, t * 128 + hl * 64:t * 128 + (hl + 1) * 64], key=sioW)
        P.dma("sp", o_gdn[b].re("h k v -> k h v"), S.v(), key=S)

    def state_in(c, b):
        P.dma("sp", sioA[0:14, 0:128], st_shift[b].re("(t p) -> t p", p=128))
        bk = P.bank()
        P.tr(bk[:, 0:14], sioA[0:14, 0:128], identf[0:14, 0:14])
        P.copy("act", fcar.re("p t o -> p (t o)"), bk[:, 0:14])
        P.dma("sp", sioB[0:36, 0:128], st_conv[b].re("r (t p) -> (r t) p", p=128))
        bk2 = P.bank()
        P.tr(bk2[:, 0:36], sioB[0:36, 0:128], identf[0:36, 0:36])
        P.copy("act", qcar.v(), bk2[:, 0:36].re("p (r t) -> p t r", r=3))
        P.memset("pool", c.cum[:, :, 0:1], 0.0)
        P.dma("sp", sioW[0:64, 0:512].re("v (t hl k) -> v t hl k", t=4, hl=2),
              st_wkv[b].re("(t hl) v k -> v t hl k", hl=2))
        P.memset("pool", A.v(), 0.0)
        P.memset("pool", Abf.v(), 0.0)
        for t in range(4):
            bk3 = P.bank()
            P.tr(bk3[:, 0:64], sioW[0:64, t * 128:(t + 1) * 128], identf[0:64, 0:64])
            P.copy("act", A[0:64, t, 0:64], bk3[0:64, 0:64])
            P.copy("act", A[64:128, t, 64:128], bk3[64:128, 0:64])
        P.copy("pool", Abf[0:64, :, 0:64], A[0:64, :, 0:64])
        P.copy("pool", Abf[64:128, :, 64:128], A[64:128, :, 64:128])
        P.dma("sp", S.v(), st_gdn[b].re("h k v -> k h v"))
        P.copy("act", Sbf.v(), S.v())

    def stream_tm(sc, npieces, srcT, nch, Cc, finish):
        groups = [[j] for j in range(nch)] if SEQ_STREAM else [list(range(nch))]
        for js in groups:
            bks_ = {j: [P.bank() for n in range(2)] for j in js}
            for g in range(npieces):
                ds = dsl[dsi[0] % NSL]
                dsi[0] += 1
                P.dma("sp", ds.v(), sc[g])
                for j in js:
                    for n in range(2):
                        for hc in range(2):
                            P.mm(bks_[j][n][:Cc, 0:512], srcT(j, 2 * g + hc), ds[:, hc, n * 512:(n + 1) * 512],
                                 start=(g == 0 and hc == 0), stop=(g == npieces - 1 and hc == 1))
            for j in js:
                for n in range(2):
                    finish(j, n, bks_[j][n])

    def macro(c, chunks):
        Cc = c.C
        nch = len(chunks)
        T = nch * Cc
        P.cur_pool = "m"

        def dump():
            for j, (xr, psrc, ydst) in enumerate(chunks):
                P.dma(XQ, ydst, xr[:Cc, :], key=xr)
        if dbg == "mix":
            return dump()
        stream_tm(sc_out, 4, lambda j, k: mixTm[:, k, j * Cc:(j + 1) * Cc], nch, Cc,
                  lambda j, n, bk: P.tt("dve", chunks[j][0][:Cc, n * 512:(n + 1) * 512],
                                        chunks[j][0][:Cc, n * 512:(n + 1) * 512], bk[:Cc, 0:512], ALU.add))
        if dbg == "x1":
            return dump()
        for j, (xr, _, _) in enumerate(chunks):
            norm_T(xr[:Cc, :], Cc, h2T, j * Cc, "gffn")
        for g in range(NHC):
            gs, us = gsl[g % NSL], usl[g % NSL]
            P.dma("sp", gs.v(), sc_gate[g])
            P.dma("sp", us.v(), sc_up[g])
            bG, bU = P.bank(), P.bank()
            for kc in range(8):
                P.mm(bG[:, 0:T], gs[:, kc, :], h2T[:, kc, 0:T], start=(kc == 0), stop=(kc == 7))
            for kc in range(8):
                P.mm(bU[:, 0:T], us[:, kc, :], h2T[:, kc, 0:T], start=(kc == 0), stop=(kc == 7))
            sg_ = sgb[g % 2]
            P.act(sg_[:, 0:T], bG[:, 0:T], AF.Silu)
            P.tt("dve", actT[:, g, 0:T], sg_[:, 0:T], bU[:, 0:T], ALU.mult)
        stream_tm(sc_down, NG, lambda j, k: actT[:, k, j * Cc:(j + 1) * Cc], nch, Cc,
                  lambda j, n, bk: P.tt("dve", chunks[j][0][:Cc, n * 512:(n + 1) * 512],
                                        chunks[j][0][:Cc, n * 512:(n + 1) * 512], bk[:Cc, 0:512], ALU.add))
        if dbg == "x2":
            return dump()
        for j, (xr, psrc, ydst) in enumerate(chunks):
            norm_T(xr[:Cc, :], Cc, h2T, j * Cc, "gple")
        gate = {}
        actF = actT.re("p a b -> p (a b)").bitcast(F32)
        pTv = actT.re("p a b -> p (a b)")[:, 4096:4096 + 2 * TM].re("p (k c) -> p k c", k=2)

        def fin_gate(j, n, bk):
            w = actF[:Cc, (2 * j + n) * 512:(2 * j + n + 1) * 512]
            P.act(w.v(), bk[:Cc, 0:512], AF.Tanh, scale=0.5)
            P.ts("dve", w.v(), w.v(), 0.5, ALU.mult, 0.5, ALU.add)
            gate[(j, n)] = w
        stream_tm(sc_pg, 4, lambda j, k: h2T[:, k, j * Cc:(j + 1) * Cc], nch, Cc, fin_gate)
        for j, (xr, psrc, ydst) in enumerate(chunks):
            P.dma(XQ, mpt[:Cc, :], psrc)
            P.copy("pool", mpbf[:Cc, :], mpt[:Cc, :])
            bk = P.bank()
            bb = bk.v().bitcast(BF16)
            for kc in range(2):
                P.tr(bb[:, kc * Cc:(kc + 1) * Cc], mpbf[:Cc, kc * 128:(kc + 1) * 128], identb[:Cc, :Cc])
            P.copy("act", pTv[:, :, j * Cc:(j + 1) * Cc], bb[:, 0:2 * Cc].re("p (k c) -> p k c", c=Cc))

        def fin_pp(j, n, bk):
            w = gate[(j, n)]
            xr = chunks[j][0]
            P.tt("dve", w.v(), w.v(), bk[:Cc, 0:512], ALU.mult)
            P.tt("pool", xr[:Cc, n * 512:(n + 1) * 512], xr[:Cc, n * 512:(n + 1) * 512], w.v(), ALU.add)
        stream_tm(sc_pp, 1, lambda j, k: pTv[:, k, j * Cc:(j + 1) * Cc], nch, Cc, fin_pp)
        if dbg == "x3":
            return dump()
        for j, (xr, psrc, ydst) in enumerate(chunks):
            st = stat()
            P.act(xn_m[:Cc, :], xr[:Cc, :], AF.Square, accum=st[:Cc, 0:1])
            P.ts("dve", st[:Cc, 1:2], st[:Cc, 0:1], 1.0 / D, ALU.mult, 1e-6, ALU.add)
            P.tt("pool", st[:Cc, 2:3], st[:Cc, 1:2], pw[:Cc, 0:1], ALU.pow)
            P.stt(xr[:Cc, :], xr[:Cc, :], st[:Cc, 2:3], bcv("fng")[:Cc, :], ALU.mult, ALU.mult)
            P.dma(XQ, ydst, xr[:Cc, :], key=xr)

    cp = make_ctx(CP, "p")
    seq_init_zero(cp)
    xi = [0]

    def load_x(src):
        xt = xq[xi[0] % NXQ]
        xi[0] += 1
        P.dma(XQ, xt[:src.shape[0], :], src)
        return xt

    srcs = [(xa[i * CP:(i + 1) * CP, :], False) for i in range(NA)] + \
           [(xb[i * CP:(i + 1) * CP, :], True) for i in range(NB)]
    nxt = load_x(srcs[0][0]) if srcs else None
    pend = []
    for i, (src, full) in enumerate(srcs):
        xt = nxt
        if i + 1 < len(srcs):
            nxt = load_x(srcs[i + 1][0])
        j = i - NA
        P.tag = "chunk%d" % i
        if os.environ.get("BAR", "none") in ("chunk", "both"):
            P.barrier()
        chunk(cp, xt, full, mixTm[:, :, len(pend) * CP:(len(pend) + 1) * CP] if full else None)
        if full:
            pend.append((xt, pb[j * CP:(j + 1) * CP, :], yb[j * CP:(j + 1) * CP, :]))
            if len(pend) == 2 or i == len(srcs) - 1:
                P.tag = "macro%d" % i
                if os.environ.get("BAR", "none") in ("macro", "both"):
                    P.barrier()
                macro(cp, pend)
                if os.environ.get("BAR", "none") in ("macro", "both"):
                    P.barrier()
                P.cur_pool = "c"
                pend = []
    if NA + NB > 0:
        state_out(cp, o_shift_p, o_wkv_p, o_conv_p, o_gdn_p, 0)

    if NS > 0:
        P.barrier()
        cs = make_ctx(CS, "s")
        pend = []
        for b in range(NS):
            state_in(cs, b)
            xt = load_x(xs[b * CS:(b + 1) * CS, :])
            chunk(cs, xt, True, mixTm[:, :, len(pend) * CS:(len(pend) + 1) * CS])
            state_out(cs, o_shift_s, o_wkv_s, o_conv_s, o_gdn_s, b)
            pend.append((xt, ps[b * CS:(b + 1) * CS, :], ys[b * CS:(b + 1) * CS, :]))
            if len(pend) == 2 or b == NS - 1:
                macro(cs, pend)
                P.cur_pool = "c"
                pend = []
    span = P.schedule()
    if os.environ.get("NOSCHED"):
        P.sched_order = None
    print("arena used", arena.hi)
    info = P.emit()
    return nc, (info, span)


_CACHE = {}


def _get_prog(NA, NB, NS, dbg=None):
    key = (NA, NB, NS, dbg)
    if key not in _CACHE:
        _CACHE[key] = build(NA, NB, NS, dbg=dbg)
    return _CACHE[key]


def run_cores(per_core, NA, NB, NS):
    nc, info = _get_prog(NA, NB, NS)
    res = run_bass_kernel_spmd(nc, per_core, core_ids=list(range(len(per_core))))
    return res.results


def kernel(**inp):
    f = lambda a: np.ascontiguousarray(np.asarray(a, np.float32))
    xp = f(inp["x_prompt"])
    B, SEQ, _ = xp.shape
    xsm = f(inp["x_sample"])
    DB, DS, _ = xsm.shape
    pp = f(inp["p_prompt"])[0]
    psm = f(inp["p_sample"])[0]
    n_cores = 8
    halves = n_cores // B
    assert halves == 2
    HT = SEQ // 2
    NA = NB = HT // 128
    NS = DB // n_cores
    cols, bc = _pack_consts(inp)
    shared = {
        "w_in": f(inp["w_in"][0]), "w_lup": f(inp["w_lora_up"][0]), "a_lup": f(inp["a_lora_up"][0]),
        "g_lup": f(inp["g_lora_up"][0]), "w_out": f(inp["w_out"][0]), "w_gate": f(inp["w_gate"][0]),
        "w_up": f(inp["w_up"][0]), "w_down": f(inp["w_down"][0]), "w_pg": f(inp["w_ple_gate"][0]),
        "w_pp": f(inp["w_ple_proj"][0]), "cols": cols, "bc": bc,
    }
    per_core = []
    for cix in range(n_cores):
        b, half = cix // 2, cix % 2
        m = dict(shared)
        m["xa"] = np.zeros((HT, D), np.float32) if half == 0 else xp[b, 0:HT]
        m["xb"] = xp[b, half * HT:(half + 1) * HT]
        m["pb"] = pp[b, half * HT:(half + 1) * HT]
        sl = slice(cix * NS, (cix + 1) * NS)
        m["xs"] = xsm[sl].reshape(NS * DS, D)
        m["ps"] = psm[sl].reshape(NS * DS, D_PLE)
        m["st_shift"] = f(inp["state_shift"][0][sl])
        m["st_wkv"] = f(inp["state_wkv"][0][sl])
        m["st_conv"] = f(inp["state_conv"][0][sl])
        m["st_gdn"] = f(inp["state_gdn"][0][sl])
        per_core.append({k: np.ascontiguousarray(v) for k, v in m.items()})
    res = run_cores(per_core, NA, NB, NS)
    y_prompt = np.zeros((B, SEQ, D), np.float32)
    y_sample = np.zeros((DB, DS, D), np.float32)
    nsp = np.zeros((1, B, R_PROJ), np.float32)
    nwp = np.zeros((1, B, 8, 64, 64), np.float32)
    ncp = np.zeros((1, B, 3, 1536), np.float32)
    ngp = np.zeros((1, B, 4, 128, 128), np.float32)
    nss = np.zeros((1, DB, R_PROJ), np.float32)
    nws = np.zeros((1, DB, 8, 64, 64), np.float32)
    ncs = np.zeros((1, DB, 3, 1536), np.float32)
    ngs = np.zeros((1, DB, 4, 128, 128), np.float32)
    for cix in range(n_cores):
        b, half = cix // 2, cix % 2
        r = res[cix]
        y_prompt[b, half * HT:(half + 1) * HT] = r["yb"]
        sl = slice(cix * NS, (cix + 1) * NS)
        y_sample[sl] = r["ys"].reshape(NS, DS, D)
        if half == 1:
            nsp[0, b] = r["o_shift_p"][0]
            nwp[0, b] = r["o_wkv_p"][0]
            ncp[0, b] = r["o_conv_p"][0]
            ngp[0, b] = r["o_gdn_p"][0]
        nss[0, sl] = r["o_shift_s"]
        nws[0, sl] = r["o_wkv_s"]
        ncs[0, sl] = r["o_conv_s"]
        ngs[0, sl] = r["o_gdn_s"]
    return (y_prompt, y_sample, nsp, nwp, ncp, ngp, nss, nws, ncs, ngs)
```
